# Optimizing a Trainium2 kernel written in Bass

```python
import math
import jax, jax.numpy as jnp
from jax import lax
import numpy as np

D_MODEL = 2048
BATCH = 1
SEQ = 16384
DEPTH = 2

EPS = 1e-6
D_MIX = D_MODEL
GDN_HEADS = 8
GDN_DK = 128
GDN_DV = 128
GDN_CONV = 4
GDN_CHUNK = 64
MLA_HEADS = 4
MLA_Q_RANK = 448
MLA_KV_RANK = 128
MLA_NOPE = 128
MLA_ROPE = 64
MLA_DV = 128
MLA_BLOCK = 128
ROPE_THETA = 10000.0
SWA_HEADS = 8
SWA_KV_HEADS = 2
SWA_DH = 64
SWA_WINDOW = 128
D_FF = 5632
FFN_CONV = 3

GDN_QK = GDN_HEADS * GDN_DK
GDN_V = GDN_HEADS * GDN_DV
MLA_OUT = MLA_HEADS * MLA_DV
SWA_OUT = SWA_HEADS * SWA_DH
SWA_KV = SWA_KV_HEADS * SWA_DH
IN_SIZES = (GDN_QK, GDN_QK, GDN_V, GDN_V, GDN_HEADS, GDN_HEADS,
            MLA_Q_RANK, MLA_KV_RANK, MLA_ROPE,
            SWA_OUT, SWA_KV, SWA_KV)
D_IN = int(sum(IN_SIZES))
IN_SPLITS = tuple(int(s) for s in np.cumsum(IN_SIZES)[:-1])
N_MOD = 6

kernel_name = 'hybrid_gdn_mla_swa_parallel_heads'


def rmsnorm(x, w):
    xf = x.astype(jnp.float32)
    y = xf * lax.rsqrt(jnp.mean(xf * xf, axis=-1, keepdims=True) + EPS)
    return (y * w.astype(jnp.float32)).astype(x.dtype)


def l2norm(x):
    xf = x.astype(jnp.float32)
    return xf * lax.rsqrt(jnp.sum(xf * xf, axis=-1, keepdims=True) + EPS)


def causal_dwconv(x, w):
    K = w.shape[0]
    S = x.shape[1]
    xp = jnp.pad(x, ((0, 0), (K - 1, 0), (0, 0)))
    return sum(xp[:, j:j + S] * w[j] for j in range(K))


def rope(x, positions):
    half = x.shape[-1] // 2
    inv = ROPE_THETA ** (-jnp.arange(half, dtype=jnp.float32) / half)
    ang = positions.astype(jnp.float32)[..., None, None] * inv
    cos, sin = jnp.cos(ang), jnp.sin(ang)
    xf = x.astype(jnp.float32)
    x1, x2 = xf[..., :half], xf[..., half:]
    return jnp.concatenate([x1 * cos - x2 * sin, x2 * cos + x1 * sin], axis=-1).astype(x.dtype)


def alibi_slopes(n):
    return 2.0 ** (-8.0 * (jnp.arange(n, dtype=jnp.float32) + 1.0) / n)


def gated_delta_rule_chunked(q, k, v, g, beta):
    B, S, H, DK = q.shape
    DV = v.shape[-1]
    C = GDN_CHUNK
    N = S // C

    def chunks(t):
        t = t.astype(jnp.float32).reshape((B, N, C, H) + t.shape[3:])
        return jnp.moveaxis(t, 3, 1)

    q = chunks(q) * (DK ** -0.5)
    k = chunks(k)
    v = chunks(v)
    beta = chunks(beta)
    gc = jnp.cumsum(chunks(g), axis=-1)
    incl = jnp.tril(jnp.ones((C, C), bool))
    strict = jnp.tril(jnp.ones((C, C), bool), -1)
    diff = gc[..., :, None] - gc[..., None, :]
    decay = jnp.where(incl, jnp.exp(jnp.where(incl, diff, 0.0)), 0.0)
    k_beta = k * beta[..., None]
    kk = jnp.einsum('bhnid,bhnjd->bhnij', k_beta, k)
    a_mat = jnp.eye(C, dtype=jnp.float32) + jnp.where(strict, kk * decay, 0.0)
    u = lax.linalg.triangular_solve(a_mat, v * beta[..., None], left_side=True, lower=True,
                                    unit_diagonal=True)
    w = lax.linalg.triangular_solve(a_mat, k_beta * jnp.exp(gc)[..., None], left_side=True,
                                    lower=True, unit_diagonal=True)
    qk = jnp.einsum('bhnid,bhnjd->bhnij', q, k) * decay
    q_dec = q * jnp.exp(gc)[..., None]
    k_tail = k * jnp.exp(gc[..., -1:] - gc)[..., None]
    g_tot = jnp.exp(gc[..., -1])
    xs = tuple(jnp.moveaxis(t, 2, 0) for t in (u, w, qk, q_dec, k_tail, g_tot))

    def step(state, inp):
        u_n, w_n, qk_n, qd_n, kt_n, gt_n = inp
        v_new = u_n - jnp.einsum('bhck,bhkv->bhcv', w_n, state)
        o_n = (jnp.einsum('bhck,bhkv->bhcv', qd_n, state)
               + jnp.einsum('bhij,bhjv->bhiv', qk_n, v_new))
        state = state * gt_n[..., None, None] + jnp.einsum('bhck,bhcv->bhkv', kt_n, v_new)
        return state, o_n

    s0 = jnp.zeros((B, H, DK, DV), jnp.float32)
    _, o = lax.scan(step, s0, xs)
    return o.transpose(1, 0, 3, 2, 4).reshape(B, S, H, DV)


def gdn_mixer(q, k, v, z, a, b, conv_w, a_log, dt_bias, norm_w):
    B, S, _ = q.shape
    qkv = jax.nn.silu(causal_dwconv(jnp.concatenate([q, k, v], axis=-1), conv_w))
    q, k, v = jnp.split(qkv, (GDN_QK, 2 * GDN_QK), axis=-1)
    q = l2norm(q.reshape(B, S, GDN_HEADS, GDN_DK))
    k = l2norm(k.reshape(B, S, GDN_HEADS, GDN_DK))
    v = v.reshape(B, S, GDN_HEADS, GDN_DV)
    g = -jnp.exp(a_log.astype(jnp.float32)) * jax.nn.softplus(
        a.astype(jnp.float32) + dt_bias.astype(jnp.float32))
    beta = jax.nn.sigmoid(b.astype(jnp.float32))
    o = gated_delta_rule_chunked(q, k, v, g, beta)
    o = rmsnorm(o, norm_w) * jax.nn.silu(z.astype(jnp.float32).reshape(B, S, GDN_HEADS, GDN_DV))
    return o.reshape(B, S, GDN_V).astype(z.dtype)


def mla_mixer(c_q, c_kv, k_rope_raw, positions, q_norm_w, w_uq, kv_norm_w, w_ukv):
    B, S, _ = c_q.shape
    H = MLA_HEADS
    dqk = MLA_NOPE + MLA_ROPE
    q = (rmsnorm(c_q, q_norm_w) @ w_uq).reshape(B, S, H, dqk)
    q = jnp.concatenate([q[..., :MLA_NOPE], rope(q[..., MLA_NOPE:], positions)], axis=-1)
    q = q * (dqk ** -0.5)
    kv = (rmsnorm(c_kv, kv_norm_w) @ w_ukv).reshape(B, S, H, MLA_NOPE + MLA_DV)
    k_nope, v = kv[..., :MLA_NOPE], kv[..., MLA_NOPE:]
    k_pe = rope(k_rope_raw[:, :, None, :], positions)
    k = jnp.concatenate([k_nope, jnp.broadcast_to(k_pe, (B, S, H, MLA_ROPE))], axis=-1)
    nb = S // MLA_BLOCK
    q_blocks = q.reshape(B, nb, MLA_BLOCK, H, dqk).transpose(1, 0, 2, 3, 4)
    key_idx = jnp.arange(S)

    def block(args):
        qb, n = args
        s = jnp.einsum('bqhd,bkhd->bhqk', qb, k).astype(jnp.float32)
        q_idx = n * MLA_BLOCK + jnp.arange(MLA_BLOCK)
        s = jnp.where(key_idx[None, :] <= q_idx[:, None], s, -jnp.inf)
        p = jax.nn.softmax(s, axis=-1).astype(v.dtype)
        return jnp.einsum('bhqk,bkhd->bqhd', p, v)

    out = lax.map(block, (q_blocks, jnp.arange(nb)))
    return out.transpose(1, 0, 2, 3, 4).reshape(B, S, MLA_OUT)


def swa_mixer(q, k, v, sinks):
    B, S, _ = q.shape
    W = SWA_WINDOW
    nb = S // W
    G = SWA_HEADS // SWA_KV_HEADS
    qb = q.reshape(B, nb, W, SWA_KV_HEADS, G, SWA_DH)
    kb = k.reshape(B, nb, W, SWA_KV_HEADS, SWA_DH)
    vb = v.reshape(B, nb, W, SWA_KV_HEADS, SWA_DH)

    def with_prev(t):
        prev = jnp.concatenate([jnp.zeros_like(t[:, :1]), t[:, :-1]], axis=1)
        return jnp.concatenate([prev, t], axis=2)

    kk, vv = with_prev(kb), with_prev(vb)
    s = jnp.einsum('bnqhgd,bnkhd->bnhgqk', qb, kk).astype(jnp.float32) * (SWA_DH ** -0.5)
    qi = jnp.arange(W)[:, None]
    kj = jnp.arange(2 * W)[None, :]
    dist = (qi + W - kj).astype(jnp.float32)
    key_pos = jnp.arange(nb)[:, None] * W - W + kj
    valid = ((dist >= 0) & (dist < W))[None] & (key_pos >= 0)[:, None, :]
    slopes = alibi_slopes(SWA_HEADS).reshape(SWA_KV_HEADS, G)
    s = s - slopes[:, :, None, None] * dist
    s = jnp.where(valid[None, :, None, None], s, -jnp.inf)
    sink = jnp.broadcast_to(sinks.astype(jnp.float32).reshape(SWA_KV_HEADS, G)[:, :, None, None],
                            s.shape[:-1] + (1,))
    p = jax.nn.softmax(jnp.concatenate([s, sink], axis=-1), axis=-1)[..., :-1]
    o = jnp.einsum('bnhgqk,bnkhd->bnqhgd', p.astype(v.dtype), vv)
    return o.reshape(B, S, SWA_OUT)


def setup_inputs(seed: int = 0) -> dict:
    key = jax.random.key(seed)
    ks = jax.random.split(key, 24)
    f32 = jnp.float32
    L = DEPTH

    def nrm(k, shape, scale):
        return jax.random.normal(k, shape, f32) * scale

    def gain(k, shape):
        return 1.0 + 0.05 * jax.random.normal(k, shape, f32)

    x = nrm(ks[0], (BATCH, SEQ, D_MODEL), 1.0)
    c = nrm(ks[1], (BATCH, D_MODEL), 1.0)
    start = jax.random.randint(ks[2], (BATCH, 1), 0, 1024, jnp.int32)
    positions = start + jnp.arange(SEQ, dtype=jnp.int32)[None, :]
    dt = jnp.exp(jax.random.uniform(ks[11], (L, GDN_HEADS), f32, math.log(1e-3), math.log(1e-1)))
    return {
        'x': x,
        'c': c,
        'positions': positions,
        'ada_w': nrm(ks[3], (L, D_MODEL, N_MOD * D_MODEL), 0.5 * D_MODEL ** -0.5),
        'ada_b': nrm(ks[4], (L, N_MOD * D_MODEL), 0.02),
        'mix_pre_norm': gain(ks[5], (L, D_MODEL)),
        'mix_post_norm': gain(ks[6], (L, D_MODEL)),
        'w_in': nrm(ks[7], (L, D_MODEL, D_IN), D_MODEL ** -0.5),
        'w_out': nrm(ks[8], (L, D_MIX, D_MODEL), D_MIX ** -0.5),
        'gdn_conv': nrm(ks[9], (L, GDN_CONV, 2 * GDN_QK + GDN_V), GDN_CONV ** -0.5),
        'gdn_a_log': jnp.log(jax.random.uniform(ks[10], (L, GDN_HEADS), f32, 1.0, 16.0)),
        'gdn_dt_bias': dt + jnp.log(-jnp.expm1(-dt)),
        'gdn_norm': gain(ks[12], (L, GDN_DV)),
        'mla_q_norm': gain(ks[13], (L, MLA_Q_RANK)),
        'mla_w_uq': nrm(ks[14], (L, MLA_Q_RANK, MLA_HEADS * (MLA_NOPE + MLA_ROPE)), MLA_Q_RANK ** -0.5),
        'mla_kv_norm': gain(ks[15], (L, MLA_KV_RANK)),
        'mla_w_ukv': nrm(ks[16], (L, MLA_KV_RANK, MLA_HEADS * (MLA_NOPE + MLA_DV)), MLA_KV_RANK ** -0.5),
        'swa_sinks': nrm(ks[17], (L, SWA_HEADS), 1.0),
        'ffn_pre_norm': gain(ks[18], (L, D_MODEL)),
        'ffn_post_norm': gain(ks[19], (L, D_MODEL)),
        'ffn_w_up': nrm(ks[20], (L, D_MODEL, 2 * D_FF), D_MODEL ** -0.5),
        'ffn_conv': nrm(ks[21], (L, FFN_CONV, 2 * D_FF), FFN_CONV ** -0.5),
        'ffn_conv_b': nrm(ks[22], (L, 2 * D_FF), 0.02),
        'ffn_w_down': nrm(ks[23], (L, D_FF, D_MODEL), D_FF ** -0.5),
    }


def reference(x, c, positions, ada_w, ada_b, mix_pre_norm, mix_post_norm, w_in, w_out,
              gdn_conv, gdn_a_log, gdn_dt_bias, gdn_norm, mla_q_norm, mla_w_uq, mla_kv_norm,
              mla_w_ukv, swa_sinks, ffn_pre_norm, ffn_post_norm, ffn_w_up, ffn_conv, ffn_conv_b,
              ffn_w_down):
    B, S, D = x.shape
    c_act = jax.nn.silu(c)
    for l in range(DEPTH):
        mod = (c_act @ ada_w[l] + ada_b[l]).reshape(B, N_MOD, D)
        shift1, scale1, gate1, shift2, scale2, gate2 = (mod[:, i][:, None, :] for i in range(N_MOD))

        h = rmsnorm(x, mix_pre_norm[l]) * (1.0 + scale1) + shift1
        (a_q, a_k, a_v, a_z, a_a, a_b, b_cq, b_ckv, b_krope,
         c_q, c_k, c_v) = jnp.split(h @ w_in[l], IN_SPLITS, axis=-1)
        o_a = gdn_mixer(a_q, a_k, a_v, a_z, a_a, a_b, gdn_conv[l], gdn_a_log[l], gdn_dt_bias[l],
                        gdn_norm[l])
        o_b = mla_mixer(b_cq, b_ckv, b_krope, positions, mla_q_norm[l], mla_w_uq[l],
                        mla_kv_norm[l], mla_w_ukv[l])
        o_c = swa_mixer(c_q, c_k, c_v, swa_sinks[l])
        mix = jnp.concatenate([o_a, o_b, o_c], axis=-1) @ w_out[l]
        x = x + gate1 * rmsnorm(mix, mix_post_norm[l])

        h = rmsnorm(x, ffn_pre_norm[l]) * (1.0 + scale2) + shift2
        u = causal_dwconv(h @ ffn_w_up[l], ffn_conv[l]) + ffn_conv_b[l]
        gate_br, up = u[..., :D_FF], u[..., D_FF:]
        y = (jax.nn.gelu(gate_br, approximate=True) * up) @ ffn_w_down[l]
        x = x + gate2 * rmsnorm(y, ffn_post_norm[l])
    return x
```

```python
import contextlib
import numpy as np
import concourse.bass as bass
import concourse.mybir as mybir
from concourse.bass_utils import run_bass_kernel_spmd

F32 = mybir.dt.float32
BF16 = mybir.dt.bfloat16
I32 = mybir.dt.int32
ALU = mybir.AluOpType
AF = mybir.ActivationFunctionType
AX = mybir.AxisListType

NCORES = 8
D = 2048
KT = 16
EPS = 1e-6
D_IN = 5520
D_FF = 5632
SEM_ROT = 2000


class T:
    __slots__ = ("ap", "w", "r", "dsem", "dcnt", "base")

    def __init__(self, ap, base=None):
        self.base = base
        if not isinstance(ap, bass.AP):
            ap = ap.ap()
        self.ap = ap
        self.w = None
        self.r = {}
        self.dsem = None
        self.dcnt = 0

    def __getitem__(self, idx):
        return self.ap[idx]


class _Rec:
    def __init__(self):
        self.calls = []

    def __getattr__(self, name):
        def f(*a, **k):
            self.calls.append((name, a, k))
            return None
        return f


class Sched:
    ENG = ("pe", "act", "dve", "pool", "sp")

    def __init__(self, nc):
        self.nc = nc
        self.es = contextlib.ExitStack()
        self.ops = {e: [] for e in self.ENG}
        self.cur = {}
        self.cnt = {e: 0 for e in self.ENG}
        self.waited = {e: {} for e in self.ENG}
        self.nsem = 0
        self.out_tokens = []
        self.nname = 0
        for e in self.ENG:
            self.cur[e] = self.new_sem("p_" + e)

    def new_sem(self, name):
        self.nsem += 1
        return self.es.enter_context(self.nc.semaphore(f"{name}_{self.nsem}"))

    def sb(self, shape, dtype=F32, name=None):
        self.nname += 1
        return T(self.es.enter_context(self.nc.sbuf_tensor(f"{name or 'sb'}_{self.nname}", list(shape), dtype)))

    def ps(self, shape, dtype=F32, name=None):
        self.nname += 1
        return T(self.es.enter_context(self.nc.psum_tensor(f"{name or 'ps'}_{self.nname}", list(shape), dtype)))

    def din(self, name, shape, dtype=F32):
        return T(self.nc.dram_tensor(name, list(shape), dtype, kind="ExternalInput").ap())

    def dout(self, name, shape, dtype=F32):
        return T(self.nc.dram_tensor(name, list(shape), dtype, kind="ExternalOutput").ap())

    def _deps(self, e, reads, writes):
        need = {}

        def add(tok):
            if tok is None:
                return
            s, v = tok
            if need.get(id(s), (None, 0))[1] < v:
                need[id(s)] = (s, v)

        reads = [t.base or t for t in reads]
        writes = [t.base or t for t in writes]
        for t in reads:
            add(t.w)
        for t in writes:
            add(t.w)
            for s_v in t.r.values():
                add(s_v)
        waits = []
        for k, (s, v) in need.items():
            if s is self.cur[e] and e == "pe":
                continue
            if self.waited[e].get(k, 0) >= v:
                continue
            self.waited[e][k] = v
            waits.append((s, v))
        return waits

    def _commit(self, tok, reads, writes):
        s, v = tok
        reads = [t.base or t for t in reads]
        writes = [t.base or t for t in writes]
        for t in reads:
            old = t.r.get(id(s))
            if old is None or old[1] < v:
                t.r[id(s)] = (s, v)
        for t in writes:
            t.w = tok
            t.r = {}

    def op(self, e, fn, reads=(), writes=()):
        rec = _Rec()
        fn(rec)
        calls = rec.calls

        def fn(eng, calls=calls):
            ins = None
            for name, a, k in calls:
                ins = getattr(eng, name)(*a, **k)
            return ins
        waits = self._deps(e, reads, writes)
        if self.cnt[e] >= SEM_ROT:
            self.cur[e] = self.new_sem("p_" + e)
            self.cnt[e] = 0
        self.cnt[e] += 1
        tok = (self.cur[e], self.cnt[e])
        self.ops[e].append((waits, fn, tok[0], 1))
        self._commit(tok, reads, writes)
        return tok

    def pe(self, fn, reads=(), writes=()):
        return self.op("pe", fn, reads, writes)

    def act(self, fn, reads=(), writes=()):
        return self.op("act", fn, reads, writes)

    def dve(self, fn, reads=(), writes=()):
        return self.op("dve", fn, reads, writes)

    def pool(self, fn, reads=(), writes=()):
        return self.op("pool", fn, reads, writes)

    def mm(self, out_ap, lhsT_ap, rhs_ap, reads, writes, start=True, stop=True):
        return self.op("pe", lambda e: e.matmul(out_ap, lhsT_ap, rhs_ap, start=start, stop=stop), reads, writes)

    def mmg(self, out_ap, pairs, reads, writes):
        def fn(e):
            n = len(pairs)
            ins = None
            for i, (l, r) in enumerate(pairs):
                ins = e.matmul(out_ap, l, r, start=(i == 0), stop=(i == n - 1))
            return ins
        return self.op("pe", fn, reads, writes)

    def dma(self, e, out_t, in_t, out_ap=None, in_ap=None, sem_of=None, is_output=False):
        rl = [in_t] if in_t is not None else []
        wl = [out_t] if out_t is not None else []
        waits = self._deps(e, rl, wl)
        so = sem_of or out_t
        if so.dsem is None:
            so.dsem = self.new_sem("d")
        so.dcnt += 16
        tok = (so.dsem, so.dcnt)
        oa = out_ap if out_ap is not None else out_t.ap
        ia = in_ap if in_ap is not None else in_t.ap
        self.ops[e].append((waits, lambda eng: eng.dma_start(out=oa, in_=ia), tok[0], 16))
        self._commit(tok, rl, wl)
        if is_output:
            self.out_tokens.append(tok)
        return tok

    def finish(self):
        nc = self.nc
        fin = {}
        for s, v in self.out_tokens:
            if fin.get(id(s), (None, 0))[1] < v:
                fin[id(s)] = (s, v)
        final_waits = list(fin.values())
        ops = self.ops
        emap = {"pe": "tensor", "act": "scalar", "dve": "vector", "pool": "gpsimd", "sp": "sync"}

        def body(e):
            def f(eng):
                for waits, fn, sem, inc in ops[e]:
                    for s, v in waits:
                        eng.wait_ge(s, v)
                    fn(eng).then_inc(sem, inc)
                if e == "sp":
                    for s, v in final_waits:
                        eng.wait_ge(s, v)
            return f

        with nc.Block() as block:
            for e in self.ENG:
                getattr(block, emap[e])(body(e))
        self.es.close()


def new_nc():
    return bass.Bass("TRN2", target_bir_lowering=False)


class Rot:
    def __init__(self, items):
        self.items = items
        self.i = 0

    def get(self):
        t = self.items[self.i % len(self.items)]
        self.i += 1
        return t


def rms_rstd(S, xs_list, n, ones, ps_ss, sqrot, rstd, dim):
    m = len(xs_list)
    for i, (t, ap) in enumerate(xs_list):
        p = ap.shape[0]
        sq = sqrot.get()
        S.act(lambda e, sq=sq, ap=ap, p=p: e.activation(sq[0:p, 0:n], ap, AF.Square), [t], [sq])
        S.mm(ps_ss[:, 0:n], ones[0:p, :], sq[0:p, 0:n], [ones, sq], [ps_ss], start=(i == 0), stop=(i == m - 1))
    S.dve(lambda e: e.tensor_scalar(rstd[:, 0:n], ps_ss[:, 0:n], 1.0 / dim, EPS, ALU.mult, ALU.add), [ps_ss], [rstd])
    S.act(lambda e: e.activation(rstd[:, 0:n], rstd[:, 0:n], AF.Sqrt), [rstd], [rstd])
    S.dve(lambda e: e.reciprocal(rstd[:, 0:n], rstd[:, 0:n]), [rstd], [rstd])


def build_mod():
    nc = new_nc()
    S = Sched(nc)
    NC = 1536
    c_d = S.din("c_col", [128, KT])
    w_d = [S.din(f"ada_w{l}", [D, NC]) for l in range(2)]
    b_d = [S.din(f"ada_b{l}", [1, NC]) for l in range(2)]
    o_d = [S.dout(f"mod{l}", [1, NC]) for l in range(2)]
    cc = S.sb([128, KT])
    S.dma("sp", cc, c_d)
    S.act(lambda e: e.activation(cc[:], cc[:], AF.Silu), [cc], [cc])
    wsb = [S.sb([128, KT, NC], name="adaw") for _ in range(1)]
    for l in range(2):
        w = wsb[0]
        S.dma("sp", w, w_d[l], in_ap=w_d[l].ap.rearrange("(kt p) c -> p kt c", p=128))
        bsb = S.sb([1, NC])
        S.dma("sp", bsb, b_d[l])
        res = S.sb([1, NC])
        for j in range(NC // 512):
            pp = S.ps([1, 512])
            S.mmg(pp[:], [(cc[:, kt:kt + 1], w[:, kt, j * 512:(j + 1) * 512]) for kt in range(KT)], [cc, w], [pp])
            S.dve(lambda e, pp=pp, j=j, res=res, bsb=bsb: e.tensor_tensor(
                res[:, j * 512:(j + 1) * 512], pp[:], bsb[:, j * 512:(j + 1) * 512], ALU.add), [pp, bsb], [res])
        S.dma("sp", o_d[l], res, is_output=True)
    S.finish()
    return nc


def modulated_norm(S, xs, n, gp, sh, ones, ps_ss, sqrot, rstd, tmprot, out_aps, out_ts):
    rms_rstd(S, [(xs, xs[:, kt, 0:n]) for kt in range(KT)], n, ones, ps_ss, sqrot, rstd, D)
    for kt in range(KT):
        tmp = tmprot.get()
        S.dve(lambda e, tmp=tmp, kt=kt: e.tensor_tensor(tmp[:, 0:n], xs[:, kt, 0:n], rstd[:, 0:n], ALU.mult),
              [xs, rstd], [tmp])
        S.act(lambda e, tmp=tmp, kt=kt: e.activation(out_aps[kt], tmp[:, 0:n], AF.Identity,
                                                     bias=sh[:, kt:kt + 1], scale=gp[:, kt:kt + 1]),
              [tmp, gp, sh], [out_ts[kt]])


def build_inproj(NT=2048, NO=D_IN):
    nc = new_nc()
    S = Sched(nc)
    NTT = NT // 512
    x_d = S.din("xT", [D, NT])
    w_d = S.din("w", [D, NO])
    vec_d = S.din("vecs", [128, 3, KT])
    ones_d = S.din("ones", [128, 128])
    o_d = S.dout("projT", [NO, NT])
    ones = S.sb([128, 128]); S.dma("sp", ones, ones_d)
    vec = S.sb([128, 3, KT]); S.dma("sp", vec, vec_d)
    gp = S.sb([128, KT]); sh = S.sb([128, KT])
    S.dve(lambda e: e.tensor_scalar(gp[:], vec[:, 1, :], 1.0, None, ALU.add), [vec], [gp])
    S.dve(lambda e: e.tensor_tensor(gp[:], gp[:], vec[:, 0, :], ALU.mult), [gp, vec], [gp])
    S.dve(lambda e: e.tensor_copy(sh[:], vec[:, 2, :]), [vec], [sh])
    xrot = Rot([S.sb([128, KT, 512], name="xs") for _ in range(2)])
    sqrot = Rot([S.sb([128, 512], name="sq") for _ in range(3)])
    tmprot = Rot([S.sb([128, 512], name="tmp") for _ in range(3)])
    rstd = S.sb([128, 512])
    ps_ss = S.ps([128, 512])
    hT = [S.sb([128, KT, 512], BF16, name="hT") for _ in range(NTT)]
    for tt in range(NTT):
        xs = xrot.get()
        S.dma("sp", xs, x_d, in_ap=x_d.ap[:, tt * 512:(tt + 1) * 512].rearrange("(kt p) n -> p kt n", p=128))
        modulated_norm(S, xs, 512, gp, sh, ones, ps_ss, sqrot, rstd, tmprot,
                       [hT[tt][:, kt, :] for kt in range(KT)], [hT[tt]] * KT)
    wrot = Rot([S.sb([128, KT, 512], BF16, name="wsl") for _ in range(2)])
    psrot = Rot([S.ps([128, 512], name="pso") for _ in range(4)])
    evrot = Rot([S.sb([128, 512], name="ev") for _ in range(4)])
    c0 = 0
    while c0 < NO:
        cw = min(512, NO - c0)
        ws = wrot.get()
        S.dma("pool", ws, w_d, out_ap=ws[:, :, 0:cw],
              in_ap=w_d.ap[:, c0:c0 + cw].rearrange("(kt p) c -> p kt c", p=128))
        m0 = 0
        while m0 < cw:
            M = min(128, cw - m0)
            for tt in range(NTT):
                pp = psrot.get()
                S.mmg(pp[0:M, :], [(ws[:, kt, m0:m0 + M], hT[tt][:, kt, :]) for kt in range(KT)], [ws, hT[tt]], [pp])
                ev = evrot.get()
                S.act(lambda e, ev=ev, pp=pp, M=M: e.copy(ev[0:M, :], pp[0:M, :]), [pp], [ev])
                S.dma("sp", None, ev, out_ap=o_d.ap[c0 + m0:c0 + m0 + M, tt * 512:(tt + 1) * 512], in_ap=ev[0:M, :],
                      sem_of=ev, is_output=True)
            m0 += M
        c0 += cw
    S.finish()
    return nc


def gdn_consts():
    idx = np.arange(128)
    same = (idx[:, None] // 64) == (idx[None, :] // 64)
    tri = (same & (idx[:, None] <= idx[None, :])).astype(np.float32)
    mus = (same & (idx[:, None] < idx[None, :])).astype(np.float32)
    mls = np.ascontiguousarray(mus.T)
    bd2 = np.zeros((128, 2), np.float32)
    bd2[:64, 0] = 1
    bd2[64:, 1] = 1
    return {"ident": np.eye(128, dtype=np.float32), "tri": tri, "mus": mus, "mls": mls, "bd2": bd2,
            "ones": np.ones((128, 128), np.float32)}


def build_gdn(NT=16384, STAGE=9):
    nc = new_nc()
    S = Sched(nc)
    NS = NT // 512
    qkv_d = [S.din(n, [128, NT]) for n in ("qT", "kT", "vT")]
    z_d = S.din("zT", [128, NT])
    a_d = S.din("a_row", [1, NT]); b_d = S.din("b_row", [1, NT])
    cw_d = S.din("cw", [128, 12])
    par_d = S.din("par", [128, 3])
    cst = {}
    for n, shp in (("ident", [128, 128]), ("tri", [128, 128]), ("mus", [128, 128]), ("mls", [128, 128]),
                   ("bd2", [128, 2]), ("ones", [128, 128])):
        d = S.din(n, shp)
        cst[n] = S.sb(shp, name=n)
        S.dma("sp", cst[n], d)
    ident, tri, mus, mls, bd2, ones = (cst[n] for n in ("ident", "tri", "mus", "mls", "bd2", "ones"))
    o_d = S.dout("oT", [128, NT])
    cw = S.sb([128, 12]); S.dma("sp", cw, cw_d)
    par = S.sb([128, 3]); S.dma("sp", par, par_d)
    negA = S.sb([128, 1])
    S.act(lambda e: e.activation(negA[:], par[:, 0:1], AF.Exp), [par], [negA])
    S.dve(lambda e: e.tensor_scalar(negA[:], negA[:], -1.0, None, ALU.mult), [negA], [negA])
    Sst = S.sb([128, 128], name="state")
    S.dve(lambda e: e.memset(Sst[:], 0.0), [], [Sst])

    rawrot = [Rot([S.sb([128, 515], name="raw") for _ in range(2)]) for _ in range(3)]
    zrot = Rot([S.sb([128, 512], name="z") for _ in range(2)])
    arot = Rot([S.sb([1, 512], name="ar") for _ in range(2)])
    brot = Rot([S.sb([1, 512], name="br") for _ in range(2)])
    crot = [Rot([S.sb([128, 512], name="c") for _ in range(2)]) for _ in range(3)]
    yrot = Rot([S.sb([128, 512], name="y") for _ in range(2)])
    sqrot = Rot([S.sb([128, 512], name="sq") for _ in range(2)])
    rnrot = Rot([S.sb([128, 512], name="rn") for _ in range(2)])
    orot = Rot([S.sb([128, 512], name="oslab") for _ in range(2)])
    rowt = Rot([S.sb([1, 512], name="rowt") for _ in range(2)])
    ps_big = Rot([S.ps([128, 512], name="psb") for _ in range(2)])
    banks = [S.ps([128, 512], name="bank") for _ in range(6)]
    pp = Rot([T(bk.ap[:, 0:128], base=bk) for bk in banks[:5]])
    psm = Rot([T(banks[5].ap[:, j * 128:j * 128 + 4], base=banks[5]) for j in range(4)])
    mrot = Rot([S.sb([128, 128], name="m") for _ in range(24)])
    crot_s = Rot([S.sb([128, 8], name="sc") for _ in range(4)])

    def M():
        return mrot.get()

    for s in range(NS):
        t0 = s * 512
        cs = []
        for qi in range(3):
            raw = rawrot[qi].get()
            if s == 0:
                S.dve(lambda e, raw=raw: e.memset(raw[:, 0:3], 0.0), [], [raw])
                S.dma("sp", raw, qkv_d[qi], out_ap=raw[:, 3:515], in_ap=qkv_d[qi].ap[:, 0:512])
            else:
                S.dma("sp", raw, qkv_d[qi], in_ap=qkv_d[qi].ap[:, t0 - 3:t0 + 512])
            y = yrot.get()
            S.dve(lambda e, y=y, raw=raw, qi=qi: e.tensor_scalar(y[:], raw[:, 3:515], cw[:, 4 * qi + 3:4 * qi + 4], None,
                                                                 ALU.mult), [raw, cw], [y])
            for j in (2, 1, 0):
                S.dve(lambda e, y=y, raw=raw, qi=qi, j=j: e.scalar_tensor_tensor(
                    y[:], raw[:, j:j + 512], cw[:, 4 * qi + j:4 * qi + j + 1], y[:], ALU.mult, ALU.add), [raw, cw, y], [y])
            c = crot[qi].get()
            S.act(lambda e, c=c, y=y: e.activation(c[:], y[:], AF.Silu), [y], [c])
            cs.append(c)
        cq, ck, cv = cs
        sz = zrot.get()
        S.dma("sp", sz, z_d, in_ap=z_d.ap[:, t0:t0 + 512])
        S.act(lambda e, sz=sz: e.activation(sz[:], sz[:], AF.Silu), [sz], [sz])
        for c, extra in ((cq, 128.0 ** -0.5), (ck, 1.0)):
            sq = sqrot.get(); pb = ps_big.get(); rn = rnrot.get()
            S.act(lambda e, sq=sq, c=c: e.activation(sq[:], c[:], AF.Square), [c], [sq])
            S.mm(pb[:], ones[:], sq[:], [ones, sq], [pb])
            S.dve(lambda e, rn=rn, pb=pb: e.tensor_scalar(rn[:], pb[:], EPS, None, ALU.add), [pb], [rn])
            S.act(lambda e, rn=rn: e.activation(rn[:], rn[:], AF.Sqrt), [rn], [rn])
            S.dve(lambda e, rn=rn: e.reciprocal(rn[:], rn[:]), [rn], [rn])
            S.dve(lambda e, c=c, rn=rn, extra=extra: e.scalar_tensor_tensor(c[:], c[:], extra, rn[:], ALU.mult, ALU.mult),
                  [c, rn], [c])
        gr = arot.get(); br = brot.get(); rt = rowt.get()
        S.dma("sp", gr, a_d, in_ap=a_d.ap[:, t0:t0 + 512])
        S.dma("sp", br, b_d, in_ap=b_d.ap[:, t0:t0 + 512])
        S.dve(lambda e, gr=gr: e.tensor_scalar(gr[:], gr[:], par[0:1, 1:2], None, ALU.add), [gr, par], [gr])
        S.dve(lambda e, gr=gr, rt=rt: e.tensor_scalar(rt[:], gr[:], 0.0, None, ALU.abs_max), [gr], [rt]) if False else None
        S.act(lambda e, gr=gr, rt=rt: e.activation(rt[:], gr[:], AF.Abs), [gr], [rt])
        S.act(lambda e, rt=rt: e.activation(rt[:], rt[:], AF.Exp, scale=-1.0), [rt], [rt])
        S.dve(lambda e, rt=rt: e.tensor_scalar(rt[:], rt[:], 1.0, None, ALU.add), [rt], [rt])
        S.act(lambda e, rt=rt: e.activation(rt[:], rt[:], AF.Ln), [rt], [rt])
        S.dve(lambda e, gr=gr: e.tensor_scalar(gr[:], gr[:], 0.0, None, ALU.max), [gr], [gr])
        S.dve(lambda e, gr=gr, rt=rt: e.tensor_tensor(gr[:], gr[:], rt[:], ALU.add), [gr, rt], [gr])
        S.dve(lambda e, gr=gr: e.tensor_scalar(gr[:], gr[:], negA[0:1, 0:1], None, ALU.mult), [gr, negA], [gr])
        S.act(lambda e, br=br: e.activation(br[:], br[:], AF.Sigmoid), [br], [br])
        oslab = orot.get()
        if STAGE < 2:
            S.dve(lambda e, oslab=oslab, cq=cq, ck=ck, cv=cv, sz=sz: e.tensor_tensor(oslab[:], cq[:], ck[:], ALU.add), [cq, ck, cv, sz, gr, br], [oslab])
        for b in range(4 if STAGE >= 2 else 0):
            c0 = b * 128
            bl = slice(c0, c0 + 128)
            pc = psm.get()
            S.mm(pc[:, 0:1], gr[0:1, bl], ones[0:1, 0:1], [gr, ones], [pc])
            S.mm(pc[:, 1:2], br[0:1, bl], ones[0:1, 0:1], [br, ones], [pc])
            sc = crot_s.get()
            S.dve(lambda e, sc=sc, pc=pc: e.tensor_copy(sc[:, 6:8], pc[:, 0:2]), [pc], [sc])
            Gb = M()
            S.dve(lambda e, Gb=Gb, sc=sc: e.tensor_scalar(Gb[:], ones[:], sc[:, 6:7], None, ALU.mult), [ones, sc], [Gb])
            pg = psm.get()
            S.mm(pg[:, 0:1], tri[:], sc[:, 6:7], [tri, sc], [pg])
            S.mm(pg[:, 1:3], Gb[:], bd2[:], [Gb, bd2], [pg])
            pgrow = pp.get()
            S.mm(pgrow[:], Gb[:], tri[:], [Gb, tri], [pgrow])
            pbrow = pp.get()
            S.mm(pbrow[:], ones[0:1, :], br[0:1, bl], [ones, br], [pbrow])
            S.dve(lambda e, sc=sc, pg=pg: e.tensor_copy(sc[:, 0:1], pg[:, 0:1]), [pg], [sc])
            S.act(lambda e, sc=sc, pg=pg: e.activation(sc[:, 1:3], pg[:, 1:3], AF.Exp), [pg], [sc])
            S.dve(lambda e, sc=sc, pg=pg: e.tensor_copy(sc[0:64, 3:4], pg[0:64, 1:2]), [pg], [sc])
            S.dve(lambda e, sc=sc, pg=pg: e.tensor_copy(sc[64:128, 3:4], pg[64:128, 2:3]), [pg], [sc])
            S.act(lambda e, sc=sc: e.activation(sc[:, 4:5], sc[:, 0:1], AF.Exp), [sc], [sc])
            S.dve(lambda e, sc=sc: e.tensor_tensor(sc[:, 5:6], sc[:, 3:4], sc[:, 0:1], ALU.subtract), [sc], [sc])
            S.act(lambda e, sc=sc: e.activation(sc[:, 5:6], sc[:, 5:6], AF.Exp), [sc], [sc])
            S.dve(lambda e, sc=sc: e.tensor_tensor(sc[:, 6:7], sc[:, 7:8], sc[:, 4:5], ALU.mult), [sc], [sc])
            tdm = M(); dU = M(); dL = M()
            S.dve(lambda e, tdm=tdm, pgrow=pgrow, sc=sc: e.tensor_scalar(tdm[:], pgrow[:], sc[:, 0:1], None, ALU.subtract),
                  [pgrow, sc], [tdm])
            S.dve(lambda e, tdm=tdm, dU=dU: e.tensor_scalar(dU[:], tdm[:], 0.0, None, ALU.min), [tdm], [dU])
            S.dve(lambda e, tdm=tdm, dL=dL: e.tensor_scalar(dL[:], tdm[:], 0.0, -1.0, ALU.max, ALU.mult), [tdm], [dL])
            S.act(lambda e, dU=dU: e.activation(dU[:], dU[:], AF.Exp), [dU], [dU])
            S.act(lambda e, dL=dL: e.activation(dL[:], dL[:], AF.Exp), [dL], [dL])
            if STAGE < 3:
                S.dve(lambda e, oslab=oslab, dU=dU, bl=bl: e.tensor_copy(oslab[:, bl], dU[:]), [dU, dL, sc], [oslab])
                continue
            pG = pp.get(); pQK = pp.get()
            S.mm(pG[:], ck[:, bl], ck[:, bl], [ck], [pG])
            S.mm(pQK[:], ck[:, bl], cq[:, bl], [ck, cq], [pQK])
            U = M(); L = M(); qkT = M(); R = M()
            import os
            SUB = int(os.environ.get("GDN_SUB", "9"))
            if SUB == 1:
                S.dve(lambda e, oslab=oslab, pG=pG, bl=bl: e.tensor_copy(oslab[:, bl], pG[:]), [pG, pQK], [oslab])
                continue
            if SUB == 2:
                S.dve(lambda e, U=U, dU=dU: e.tensor_tensor(U[:], dU[:], mus[:], ALU.mult), [dU, mus], [U])
                S.dve(lambda e, U=U, pbrow=pbrow: e.tensor_tensor(U[:], pbrow[:], U[:], ALU.mult), [U, pbrow], [U])
                S.dve(lambda e, U=U, pG=pG: e.tensor_tensor(U[:], pG[:], U[:], ALU.mult), [U, pG], [U])
                S.dve(lambda e, oslab=oslab, U=U, bl=bl: e.tensor_copy(oslab[:, bl], U[:]), [U, pQK], [oslab])
                continue
            if SUB == 3:
                S.dve(lambda e, dL=dL: e.tensor_tensor(dL[:], dL[:], mls[:], ALU.mult), [dL, mls], [dL])
                S.dve(lambda e, L=L, pG=pG, sc=sc, dL=dL: e.scalar_tensor_tensor(L[:], pG[:], sc[:, 7:8], dL[:], ALU.mult, ALU.mult),
                      [pG, sc, dL], [L])
                S.dve(lambda e, oslab=oslab, L=L, bl=bl: e.tensor_copy(oslab[:, bl], L[:]), [L, pQK], [oslab])
                continue
            S.dve(lambda e, U=U, dU=dU: e.tensor_tensor(U[:], dU[:], mus[:], ALU.mult), [dU, mus], [U])
            S.dve(lambda e, U=U, pbrow=pbrow: e.tensor_tensor(U[:], pbrow[:], U[:], ALU.mult), [U, pbrow], [U])
            S.dve(lambda e, U=U, pG=pG: e.tensor_tensor(U[:], pG[:], U[:], ALU.mult), [U, pG], [U])
            S.dve(lambda e, dL=dL: e.tensor_tensor(dL[:], dL[:], mls[:], ALU.mult), [dL, mls], [dL])
            S.dve(lambda e, L=L, pG=pG, sc=sc, dL=dL: e.scalar_tensor_tensor(L[:], pG[:], sc[:, 7:8], dL[:], ALU.mult, ALU.mult),
                  [pG, sc, dL], [L])
            S.dve(lambda e, dU=dU: e.tensor_tensor(dU[:], dU[:], tri[:], ALU.mult), [dU, tri], [dU])
            S.dve(lambda e, qkT=qkT, pQK=pQK, dU=dU: e.tensor_tensor(qkT[:], pQK[:], dU[:], ALU.mult), [dU, pQK], [qkT])
            if SUB == 4:
                S.dve(lambda e, oslab=oslab, qkT=qkT, bl=bl: e.tensor_copy(oslab[:, bl], qkT[:]), [U, L, qkT], [oslab])
                continue
            if SUB == 7:
                S.dve(lambda e, R=R, U=U: e.tensor_copy(R[:], U[:]), [U], [R])
                S.dve(lambda e, oslab=oslab, R=R, bl=bl: e.tensor_copy(oslab[:, bl], R[:]), [U, L, qkT, R], [oslab])
                continue
            if SUB == 6:
                S.dve(lambda e, R=R: e.memset(R[:], 1.0), [], [R])
                S.dve(lambda e, oslab=oslab, R=R, bl=bl: e.tensor_copy(oslab[:, bl], R[:]), [U, L, qkT, R], [oslab])
                continue
            S.act(lambda e, R=R, U=U: e.activation(R[:], U[:], AF.Copy, scale=-1.0), [U], [R])
            S.dve(lambda e, R=R: e.tensor_tensor(R[:], R[:], ident[:], ALU.add), [ident, R], [R])
            if SUB == 5:
                S.dve(lambda e, oslab=oslab, qkT=qkT, bl=bl: e.tensor_copy(oslab[:, bl], qkT[:]), [U, L, qkT, R], [oslab])
                continue
            if STAGE < 4:
                S.dve(lambda e, oslab=oslab, R=R, bl=bl: e.tensor_copy(oslab[:, bl], R[:]), [R, L, qkT], [oslab])
                continue
            P, Q = U, L
            for k in range(1, 6):
                pQn = pp.get()
                S.mm(pQn[:], P[:], Q[:], [P, Q], [pQn])
                Qn = M()
                S.act(lambda e, Qn=Qn, pQn=pQn: e.copy(Qn[:], pQn[:]), [pQn], [Qn])
                if k < 5:
                    pPn = pp.get()
                    S.mm(pPn[:], Q[:], P[:], [P, Q], [pPn])
                    Pn = M()
                    S.dve(lambda e, Pn=Pn, pPn=pPn: e.tensor_copy(Pn[:], pPn[:]), [pPn], [Pn])
                pR = pp.get()
                S.mm(pR[:], Qn[:], R[:], [Qn, R], [pR])
                S.dve(lambda e, R=R, pR=pR: e.tensor_tensor(R[:], pR[:], R[:], ALU.add), [R, pR], [R])
                Q = Qn
                if k < 5:
                    P = Pn
            if STAGE < 5:
                S.dve(lambda e, oslab=oslab, R=R, bl=bl: e.tensor_copy(oslab[:, bl], R[:]), [R, L, qkT], [oslab])
                continue
            if SUB == 10:
                S.dve(lambda e, oslab=oslab, R=R, bl=bl: e.tensor_copy(oslab[:, bl], R[:]), [R, L, qkT], [oslab])
                continue
            pKT = pp.get(); pVT = pp.get()
            S.pe(lambda e, pKT=pKT: e.transpose(pKT[:], ck[:, bl], ident[:]), [ck, ident], [pKT])
            S.pe(lambda e, pVT=pVT: e.transpose(pVT[:], cv[:, bl], ident[:]), [cv, ident], [pVT])
            Kbg = M(); Ktail = M(); Vb = M()
            S.dve(lambda e, Kbg=Kbg, pKT=pKT, sc=sc: e.tensor_scalar(Kbg[:], pKT[:], sc[:, 6:7], None, ALU.mult), [pKT, sc], [Kbg])
            S.dve(lambda e, Ktail=Ktail, pKT=pKT, sc=sc: e.tensor_scalar(Ktail[:], pKT[:], sc[:, 5:6], None, ALU.mult),
                  [pKT, sc], [Ktail])
            S.dve(lambda e, Vb=Vb, pVT=pVT, sc=sc: e.tensor_scalar(Vb[:], pVT[:], sc[:, 7:8], None, ALU.mult), [pVT, sc], [Vb])
            if SUB == 11:
                S.dve(lambda e, oslab=oslab, Kbg=Kbg, bl=bl: e.tensor_copy(oslab[:, bl], Kbg[:]), [R, L, qkT, Kbg, Ktail, Vb], [oslab])
                continue
            pu = pp.get(); pw = pp.get()
            S.mm(pu[:], R[:], Vb[:], [R, Vb], [pu])
            S.mm(pw[:], Kbg[:], R[:], [Kbg, R], [pw])
            u = M(); wT = M()
            S.act(lambda e, u=u, pu=pu: e.copy(u[:], pu[:]), [pu], [u])
            S.dve(lambda e, wT=wT, pw=pw: e.tensor_copy(wT[:], pw[:]), [pw], [wT])
            if STAGE < 6:
                S.dve(lambda e, oslab=oslab, u=u, bl=bl: e.tensor_copy(oslab[:, bl], u[:]), [u, wT, Ktail], [oslab])
                continue
            vnew = M(); o = M(); ot = M()
            for c in range(2):
                rc = slice(c * 64, (c + 1) * 64)
                pv = pp.get()
                S.mm(pv[:], wT[:], Sst[:], [wT, Sst], [pv])
                S.dve(lambda e, vnew=vnew, u=u, pv=pv, rc=rc: e.scalar_tensor_tensor(vnew[rc, :], pv[rc, :], -1.0, u[rc, :], ALU.mult, ALU.add),
                      [u, pv], [vnew])
                po1 = pp.get(); po2 = pp.get()
                S.mm(po1[:], cq[:, bl], Sst[:], [cq, Sst], [po1])
                S.mm(po2[:], qkT[rc, :], vnew[rc, :], [qkT, vnew], [po2])
                S.dve(lambda e, ot=ot, po1=po1, sc=sc, rc=rc: e.tensor_scalar(ot[rc, :], po1[rc, :], sc[rc, 4:5], None, ALU.mult),
                      [po1, sc], [ot])
                S.dve(lambda e, o=o, ot=ot, po2=po2, rc=rc: e.tensor_tensor(o[rc, :], po2[rc, :], ot[rc, :], ALU.add),
                      [ot, po2], [o])
                pS = pp.get()
                S.mm(pS[:], Ktail[rc, :], vnew[rc, :], [Ktail, vnew], [pS])
                S.dve(lambda e, sc=sc, c=c: e.tensor_scalar(Sst[:], Sst[:], sc[:, 1 + c:2 + c], None, ALU.mult), [Sst, sc], [Sst])
                S.dve(lambda e, pS=pS: e.tensor_tensor(Sst[:], pS[:], Sst[:], ALU.add), [Sst, pS], [Sst])
            if STAGE < 7:
                S.dve(lambda e, oslab=oslab, o=o, bl=bl: e.tensor_copy(oslab[:, bl], o[:]), [o, Sst], [oslab])
                continue
            sso = crot_s.get(); osq = M()
            S.dve(lambda e, sso=sso: e.memset(sso[:, 0:1], 0.0), [], [sso])
            S.act(lambda e, osq=osq, o=o, sso=sso: e.activation(osq[:], o[:], AF.Square, accum_out=sso[:, 0:1]), [o, sso], [osq, sso])
            S.dve(lambda e, sso=sso: e.tensor_scalar(sso[:, 0:1], sso[:, 0:1], 1.0 / 128, EPS, ALU.mult, ALU.add), [sso], [sso])
            S.act(lambda e, sso=sso: e.activation(sso[:, 0:1], sso[:, 0:1], AF.Sqrt), [sso], [sso])
            S.dve(lambda e, sso=sso: e.reciprocal(sso[:, 0:1], sso[:, 0:1]), [sso], [sso])
            S.dve(lambda e, osq=osq, o=o, sso=sso: e.tensor_scalar(osq[:], o[:], sso[:, 0:1], None, ALU.mult), [o, sso], [osq])
            pOT = pp.get()
            S.pe(lambda e, pOT=pOT, osq=osq: e.transpose(pOT[:], osq[:], ident[:]), [osq, ident], [pOT])
            S.dve(lambda e, oslab=oslab, pOT=pOT, sz=sz, bl=bl: e.scalar_tensor_tensor(
                oslab[:, bl], pOT[:], par[:, 2:3], sz[:, bl], ALU.mult, ALU.mult), [pOT, par, sz], [oslab])
        S.dma("sp", None, oslab, out_ap=o_d.ap[:, t0:t0 + 512], sem_of=oslab, is_output=True)
    S.finish()
    return nc


TWO_PI = 6.283185307179586
CW1 = 6.28125
CW2 = TWO_PI - CW1


def build_mlaprep(NT=2048):
    nc = new_nc()
    S = Sched(nc)
    NTT = NT // 512
    cq_d = S.din("cqT", [448, NT]); ckv_d = S.din("ckvT", [128, NT])
    kr_d = S.din("krT", [64, NT]); krs_d = S.din("krsT", [64, NT])
    pos_d = S.din("pos", [1, NT], I32)
    inv_d = S.din("inv2", [1, 64])
    qnw_d = S.din("qnw", [128, 4]); kvnw_d = S.din("kvnw", [128, 1])
    wq_d = S.din("w_uq", [448, 768]); wqs_d = S.din("w_uq_sw", [448, 256])
    wkv_d = S.din("w_ukv", [128, 1024])
    ones_d = S.din("ones", [128, 128])
    sgn_d = S.din("sgn", [64, 1])
    q_o = S.dout("qT", [4, 192, NT]); kn_o = S.dout("knT", [4, 128, NT]); kpe_o = S.dout("kpeT", [64, NT])
    v_o = S.dout("V", [NT, 512])
    ones = S.sb([128, 128]); S.dma("sp", ones, ones_d)
    inv2 = S.sb([1, 64]); S.dma("sp", inv2, inv_d)
    qnw = S.sb([128, 4]); S.dma("sp", qnw, qnw_d)
    kvnw = S.sb([128, 1]); S.dma("sp", kvnw, kvnw_d)
    sgn = S.sb([64, 1]); S.dma("sp", sgn, sgn_d)
    KS = [128, 128, 128, 64]
    wq = S.sb([128, 4, 768], BF16); wqs = S.sb([128, 4, 256], BF16)
    for kt in range(4):
        S.dma("pool", wq, wq_d, out_ap=wq[0:KS[kt], kt, :], in_ap=wq_d.ap[kt * 128:kt * 128 + KS[kt], :])
        S.dma("pool", wqs, wqs_d, out_ap=wqs[0:KS[kt], kt, :], in_ap=wqs_d.ap[kt * 128:kt * 128 + KS[kt], :])
    wkv = S.sb([128, 1024], BF16); S.dma("pool", wkv, wkv_d)
    wv = S.sb([128, 512], BF16)
    for h in range(4):
        S.dma("pool", wv, wkv_d, out_ap=wv[:, h * 128:(h + 1) * 128], in_ap=wkv_d.ap[:, h * 256 + 128:h * 256 + 256])
    sqrot = Rot([S.sb([128, 512], name="sq") for _ in range(2)])
    rstd = S.sb([128, 512])
    banks = [S.ps([128, 512], name="bank") for _ in range(8)]
    ps_ss = banks[0]
    prot = Rot(banks[1:8])
    xq = Rot([S.sb([128, 4, 512], name="xq") for _ in range(2)])
    xkv = Rot([S.sb([128, 512], name="xkv") for _ in range(2)])
    xr = Rot([S.sb([64, 512], name="xr") for _ in range(2)])
    xrs = Rot([S.sb([64, 512], name="xrs") for _ in range(2)])
    posr = Rot([S.sb([1, 512], name="posr") for _ in range(2)])
    posi = Rot([S.sb([1, 512], I32, name="posi") for _ in range(2)])
    cqn = Rot([S.sb([128, 4, 512], BF16, name="cqn") for _ in range(2)])
    kvn = Rot([S.sb([128, 512], BF16, name="kvn") for _ in range(2)])
    tmpf = Rot([S.sb([128, 512], name="tmpf") for _ in range(3)])
    tmpi = S.sb([64, 512], I32)
    cs = [S.sb([64, 512], name="cos2"), S.sb([64, 512], name="sin2")]
    ev = Rot([S.sb([128, 512], name="ev") for _ in range(4)])

    def evac_out(pp_, M, dst_ap, scale=None):
        e_ = ev.get()
        if scale is None:
            S.act(lambda e: e.copy(e_[0:M, :], pp_[0:M, :]), [pp_], [e_])
        else:
            S.dve(lambda e: e.tensor_scalar(e_[0:M, :], pp_[0:M, :], scale, None, ALU.mult), [pp_], [e_])
        S.dma("sp", None, e_, out_ap=dst_ap, in_ap=e_[0:M, :], sem_of=e_, is_output=True)

    def rope_apply(x_ap, xs_ap, x_t, xs_t, dst):
        t1 = tmpf.get()
        S.dve(lambda e: e.tensor_tensor(t1[0:64, :], x_ap, cs[0][:], ALU.mult), [x_t, cs[0]], [t1])
        S.dve(lambda e: e.tensor_tensor(dst[0:64, :], xs_ap, cs[1][:], ALU.mult), [xs_t, cs[1]], [dst])
        S.dve(lambda e: e.tensor_tensor(dst[0:64, :], dst[0:64, :], t1[0:64, :], ALU.add), [dst, t1], [dst])

    for tt in range(NTT):
        ts = slice(tt * 512, (tt + 1) * 512)
        x = xq.get()
        for kt in range(4):
            S.dma("sp", x, cq_d, out_ap=x[0:KS[kt], kt, :], in_ap=cq_d.ap[kt * 128:kt * 128 + KS[kt], ts])
        xk = xkv.get(); S.dma("sp", xk, ckv_d, in_ap=ckv_d.ap[:, ts])
        r_ = xr.get(); S.dma("sp", r_, kr_d, in_ap=kr_d.ap[:, ts])
        rs_ = xrs.get(); S.dma("sp", rs_, krs_d, in_ap=krs_d.ap[:, ts])
        pi_ = posi.get(); S.dma("sp", pi_, pos_d, in_ap=pos_d.ap[:, ts])
        pr = posr.get()
        S.dve(lambda e: e.tensor_copy(pr[:], pi_[:]), [pi_], [pr])
        pang = prot.get()
        S.mm(pang[0:64, :], inv2[0:1, :], pr[0:1, :], [inv2, pr], [pang])
        for ci, shift in ((0, np.pi / 2), (1, 0.0)):
            a_ = tmpf.get(); kf = tmpf.get()
            S.dve(lambda e: e.tensor_scalar(a_[0:64, :], pang[0:64, :], shift, None, ALU.add), [pang], [a_])
            S.dve(lambda e: e.tensor_scalar(tmpi[:], a_[0:64, :], 1.0 / TWO_PI, None, ALU.mult), [a_], [tmpi])
            S.dve(lambda e: e.tensor_copy(kf[0:64, :], tmpi[:]), [tmpi], [kf])
            S.dve(lambda e: e.scalar_tensor_tensor(a_[0:64, :], kf[0:64, :], -CW1, a_[0:64, :], ALU.mult, ALU.add), [kf, a_], [a_])
            S.dve(lambda e: e.scalar_tensor_tensor(a_[0:64, :], kf[0:64, :], -CW2, a_[0:64, :], ALU.mult, ALU.add), [kf, a_], [a_])
            S.dve(lambda e: e.tensor_scalar(a_[0:64, :], a_[0:64, :], 3.1415925, -3.1415925, ALU.min, ALU.max), [a_], [a_])
            S.act(lambda e: e.activation(cs[ci][:], a_[0:64, :], AF.Sin), [a_], [cs[ci]])
        S.dve(lambda e: e.tensor_scalar(cs[1][:], cs[1][:], sgn[:, 0:1], None, ALU.mult), [cs[1], sgn], [cs[1]])
        rms_rstd(S, [(x, x[0:KS[kt], kt, :]) for kt in range(4)], 512, ones, ps_ss, sqrot, rstd, 448)
        cn = cqn.get()
        for kt in range(4):
            t1 = tmpf.get()
            S.dve(lambda e: e.tensor_tensor(t1[0:KS[kt], :], x[0:KS[kt], kt, :], rstd[0:KS[kt], :], ALU.mult), [x, rstd], [t1])
            S.dve(lambda e: e.tensor_scalar(cn[0:KS[kt], kt, :], t1[0:KS[kt], :], qnw[0:KS[kt], kt:kt + 1], None, ALU.mult),
                  [t1, qnw], [cn])
        qs = 192.0 ** -0.5
        for h in range(4):
            pn = prot.get()
            S.mmg(pn[:], [(wq[0:KS[kt], kt, h * 192:h * 192 + 128], cn[0:KS[kt], kt, :]) for kt in range(4)], [wq, cn], [pn])
            evac_out(pn, 128, q_o.ap[h, 0:128, ts], scale=qs)
            px = prot.get(); pxs = prot.get()
            S.mmg(px[0:64, :], [(wq[0:KS[kt], kt, h * 192 + 128:h * 192 + 192], cn[0:KS[kt], kt, :]) for kt in range(4)], [wq, cn], [px])
            S.mmg(pxs[0:64, :], [(wqs[0:KS[kt], kt, h * 64:(h + 1) * 64], cn[0:KS[kt], kt, :]) for kt in range(4)], [wqs, cn], [pxs])
            xsb = tmpf.get()
            S.act(lambda e: e.copy(xsb[0:64, :], pxs[0:64, :]), [pxs], [xsb])
            d_ = ev.get()
            rope_apply(px[0:64, :], xsb[0:64, :], px, xsb, d_)
            S.dve(lambda e: e.tensor_scalar(d_[0:64, :], d_[0:64, :], qs, None, ALU.mult), [d_], [d_])
            S.dma("sp", None, d_, out_ap=q_o.ap[h, 128:192, ts], in_ap=d_[0:64, :], sem_of=d_, is_output=True)
        rms_rstd(S, [(xk, xk[:, :])], 512, ones, ps_ss, sqrot, rstd, 128)
        kn_ = kvn.get()
        t1 = tmpf.get()
        S.dve(lambda e: e.tensor_tensor(t1[:], xk[:], rstd[:], ALU.mult), [xk, rstd], [t1])
        S.dve(lambda e: e.tensor_scalar(kn_[:], t1[:], kvnw[:, 0:1], None, ALU.mult), [t1, kvnw], [kn_])
        for h in range(4):
            pk = prot.get()
            S.mm(pk[:], wkv[:, h * 256:h * 256 + 128], kn_[:], [wkv, kn_], [pk])
            evac_out(pk, 128, kn_o.ap[h, :, ts])
        for j in range(4):
            pv = prot.get()
            S.mm(pv[:], kn_[:, j * 128:(j + 1) * 128], wv[:], [kn_, wv], [pv])
            evac_out(pv, 128, v_o.ap[tt * 512 + j * 128:tt * 512 + (j + 1) * 128, :])
        d_ = ev.get()
        rope_apply(r_[:], rs_[:], r_, rs_, d_)
        S.dma("sp", None, d_, out_ap=kpe_o.ap[:, ts], in_ap=d_[0:64, :], sem_of=d_, is_output=True)
    S.finish()
    return nc


def build_mla(NK=16384, nkt=(32, 64, 96, 128), mask_from=(0, 32, 64, 96)):
    nc = new_nc()
    S = Sched(nc)
    NSL = len(nkt)
    NQ = NSL * 512
    NKT = NK // 128
    q_d = S.din("qT", [4, 192, NQ]); qi_d = S.din("qidx", [1, NQ])
    kn_d = S.din("knT", [4, 128, NK]); kpe_d = S.din("kpeT", [64, NK]); v_d = S.din("V", [NK, 512])
    ki_d = S.din("kidx", [128, NKT])
    ones_d = S.din("ones", [128, 128]); id_d = S.din("ident", [128, 128])
    o_d = S.dout("obT", [512, NQ])
    ones = S.sb([128, 128]); S.dma("sp", ones, ones_d)
    identb = S.sb([128, 128], BF16); S.dma("pool", identb, id_d)
    kidx = S.sb([128, NKT]); S.dma("sp", kidx, ki_d)
    qir = S.sb([1, NQ]); S.dma("sp", qir, qi_d)
    kpe = S.sb([64, NK], BF16); S.dma("pool", kpe, kpe_d)
    banks = [S.ps([128, 512], name="bank") for _ in range(8)]
    st_rot = Rot(banks[0:3]); acc_rot = Rot(banks[3:5]); misc = Rot(banks[5:8])
    qib = S.sb([128, NSL, 512])
    for i in range(NSL):
        pq = misc.get()
        S.mm(pq[:], ones[0:1, :], qir[0:1, i * 512:(i + 1) * 512], [ones, qir], [pq])
        S.act(lambda e: e.copy(qib[:, i, :], pq[:]), [pq], [qib])
    knh = S.sb([128, NK], BF16, name="knh")
    vh = S.sb([128, NKT, 128], BF16, name="vh")
    qn = S.sb([128, NQ], BF16); qr = S.sb([64, NQ], BF16)
    prot = Rot([S.sb([128, 512], BF16, name="pT") for _ in range(3)])
    mrot = Rot([S.sb([128, 512], BF16, name="mask") for _ in range(2)])
    accs = Rot([S.sb([128, 512], name="accs") for _ in range(2)])
    orot = Rot([S.sb([128, 512], name="o") for _ in range(2)])
    rinv = S.sb([128, 512])
    for h in range(4):
        S.dma("pool", knh, kn_d, in_ap=kn_d.ap[h])
        for k0 in range(0, NKT, 32):
            k1 = min(NKT, k0 + 32)
            S.dma("pool", vh, v_d, out_ap=vh[:, k0:k1, :],
                  in_ap=v_d.ap[k0 * 128:k1 * 128, h * 128:(h + 1) * 128].rearrange("(kt p) d -> p kt d", p=128))
        S.dma("pool", qn, q_d, in_ap=q_d.ap[h, 0:128, :])
        S.dma("pool", qr, q_d, in_ap=q_d.ap[h, 128:192, :])
        for i in range(NSL):
            qs_ = slice(i * 512, (i + 1) * 512)
            acc = acc_rot.get(); asum = accs.get()
            for kt in range(nkt[i]):
                ks = slice(kt * 128, (kt + 1) * 128)
                masked = kt >= mask_from[i]
                st = st_rot.get()
                pairs = [(knh[:, ks], qn[:, qs_]), (kpe[:, ks], qr[:, qs_])]
                rd = [knh, kpe, qn, qr]
                if masked:
                    mk = mrot.get()
                    S.dve(lambda e: e.tensor_scalar(mk[:], qib[:, i, :], kidx[:, kt:kt + 1], -30000.0, ALU.is_lt, ALU.mult),
                          [qib, kidx], [mk])
                    pairs.append((identb[:], mk[:]))
                    rd = rd + [identb, mk]
                S.mmg(st[:], pairs, rd, [st])
                pT = prot.get()
                S.act(lambda e: e.activation(pT[:], st[:], AF.Exp), [st], [pT])
                if kt == 0:
                    S.pool(lambda e: e.tensor_copy(asum[:], pT[:]), [pT], [asum])
                else:
                    S.pool(lambda e: e.tensor_tensor(asum[:], asum[:], pT[:], ALU.add), [asum, pT], [asum])
                S.mm(acc[:], vh[:, kt, :], pT[:], [vh, pT], [acc], start=(kt == 0), stop=(kt == nkt[i] - 1))
            pr = misc.get()
            S.mm(pr[:], ones[:], asum[:], [ones, asum], [pr])
            S.dve(lambda e: e.reciprocal(rinv[:], pr[:]), [pr], [rinv])
            o_ = orot.get()
            S.dve(lambda e: e.tensor_tensor(o_[:], acc[:], rinv[:], ALU.mult), [acc, rinv], [o_])
            S.dma("sp", None, o_, out_ap=o_d.ap[h * 128:(h + 1) * 128, qs_], sem_of=o_, is_output=True)
    S.finish()
    return nc


def swa_bias_tables():
    W = 128
    qi = np.arange(W)[:, None]; kj = np.arange(2 * W)[None, :]
    dist = (qi + W - kj).astype(np.float32)
    valid = (dist >= 0) & (dist < W)
    slopes = (2.0 ** (-8.0 * (np.arange(8, dtype=np.float32) + 1.0) / 8)).astype(np.float32)
    b = np.where(valid[:, None, :], -slopes[None, :, None] * dist[:, None, :], -30000.0).astype(np.float32)
    bf = b.copy(); bf[:, :, :W] = -30000.0
    return np.ascontiguousarray(b), np.ascontiguousarray(bf)


def build_swa(NT=2048):
    nc = new_nc()
    S = Sched(nc)
    NB = NT // 128
    q_d = S.din("qT", [512, NT]); k_d = S.din("kT", [128, 128 + NT]); v_d = S.din("V", [128 + NT, 128])
    b_d = S.din("bias", [128, 8, 256]); bf_d = S.din("bias_first", [128, 8, 256])
    sk_d = S.din("sinkc", [128, 8]); one_d = S.din("onec", [128, 1]); id_d = S.din("ident", [128, 128])
    o_d = S.dout("ocT", [512, NT])
    bias = S.sb([128, 8, 256]); S.dma("sp", bias, b_d)
    biasf = S.sb([128, 8, 256]); S.dma("sp", biasf, bf_d)
    sinkc = S.sb([128, 8]); S.dma("sp", sinkc, sk_d)
    onec = S.sb([128, 1]); S.dma("sp", onec, one_d)
    identb = S.sb([128, 128], BF16); S.dma("pool", identb, id_d)
    q64 = S.sb([64, 8, NT], BF16); S.dma("pool", q64, q_d, in_ap=q_d.ap.rearrange("(h d) n -> d h n", d=64))
    k64 = S.sb([64, 2, 128 + NT], BF16); S.dma("pool", k64, k_d, in_ap=k_d.ap.rearrange("(h d) n -> d h n", d=64))
    vsb = S.sb([128, NB + 1, 128], BF16); S.dma("pool", vsb, v_d, in_ap=v_d.ap.rearrange("(b p) d -> p b d", p=128))
    sp_rot = Rot([S.ps([128, 512], name="sps") for _ in range(2)])
    pt_rot = Rot([S.ps([128, 256], BF16, name="ptp") for _ in range(2)])
    op_rot = Rot([S.ps([128, 512], name="ops") for _ in range(2)])
    s_rot = Rot([S.sb([128, 256], name="s") for _ in range(2)])
    p_rot = Rot([S.sb([128, 256], name="p") for _ in range(2)])
    pn_rot = Rot([S.sb([128, 256], BF16, name="pn") for _ in range(2)])
    pnt_rot = Rot([S.sb([128, 256], BF16, name="pnt") for _ in range(2)])
    c_rot = Rot([S.sb([128, 8], name="col") for _ in range(4)])
    ost = Rot([S.sb([64, 8, 128], name="ost") for _ in range(2)])
    for n in range(NB):
        bt = biasf if n == 0 else bias
        og = ost.get()
        for h in range(8):
            kv = h // 4
            sp_ = sp_rot.get()
            S.mm(sp_[:, 0:256], q64[:, h, n * 128:(n + 1) * 128], k64[:, kv, n * 128:n * 128 + 256], [q64, k64], [sp_])
            s_ = s_rot.get(); c_ = c_rot.get()
            S.dve(lambda e: e.scalar_tensor_tensor(s_[:], sp_[:, 0:256], 0.125, bt[:, h, :], ALU.mult, ALU.add), [sp_, bt], [s_])
            S.dve(lambda e: e.tensor_reduce(c_[:, 0:1], s_[:], AX.X, ALU.max), [s_], [c_])
            S.dve(lambda e: e.tensor_tensor(c_[:, 0:1], c_[:, 0:1], sinkc[:, h:h + 1], ALU.max), [c_, sinkc], [c_])
            S.dve(lambda e: e.tensor_scalar(c_[:, 1:2], c_[:, 0:1], -1.0, None, ALU.mult), [c_], [c_])
            S.dve(lambda e: e.memset(c_[:, 2:3], 0.0), [], [c_])
            p_ = p_rot.get()
            S.act(lambda e: e.activation(p_[:], s_[:], AF.Exp, bias=c_[:, 1:2], scale=onec[:, 0:1], accum_out=c_[:, 2:3]),
                  [s_, c_, onec], [p_, c_])
            S.act(lambda e: e.activation(c_[:, 3:4], sinkc[:, h:h + 1], AF.Exp, bias=c_[:, 1:2], scale=onec[:, 0:1]),
                  [sinkc, c_, onec], [c_])
            S.dve(lambda e: e.tensor_tensor(c_[:, 4:5], c_[:, 2:3], c_[:, 3:4], ALU.add), [c_], [c_])
            S.dve(lambda e: e.reciprocal(c_[:, 4:5], c_[:, 4:5]), [c_], [c_])
            pn = pn_rot.get()
            S.dve(lambda e: e.tensor_scalar(pn[:], p_[:], c_[:, 4:5], None, ALU.mult), [p_, c_], [pn])
            ptp = pt_rot.get()
            S.pe(lambda e: e.transpose(ptp[:, 0:128], pn[:, 0:128], identb[:]), [pn, identb], [ptp])
            S.pe(lambda e: e.transpose(ptp[:, 128:256], pn[:, 128:256], identb[:]), [pn, identb], [ptp])
            pnt = pnt_rot.get()
            S.act(lambda e: e.copy(pnt[:], ptp[:]), [ptp], [pnt])
            ops = op_rot.get()
            S.mmg(ops[0:64, 0:128], [(vsb[:, n, kv * 64:(kv + 1) * 64], pnt[:, 0:128]),
                                     (vsb[:, n + 1, kv * 64:(kv + 1) * 64], pnt[:, 128:256])], [vsb, pnt], [ops])
            S.act(lambda e: e.copy(og[:, h, :], ops[0:64, 0:128]), [ops], [og])
        S.dma("sp", None, og, out_ap=o_d.ap[:, n * 128:(n + 1) * 128].rearrange("(h d) n -> d h n", d=64), sem_of=og, is_output=True)
    S.finish()
    return nc


def build_outproj(NT=2048):
    nc = new_nc()
    S = Sched(nc)
    NQ = NT // 512
    cat_d = S.din("catT", [D, NT]); x_d = S.din("xT", [D, NT]); w_d = S.din("w_out", [D, D])
    vec_d = S.din("vecs", [128, 5, KT])
    ones_d = S.din("ones", [128, 128])
    x1_o = S.dout("x1T", [D, NT]); h2_o = S.dout("h2T", [D, NT])
    ones = S.sb([128, 128]); S.dma("sp", ones, ones_d)
    vec = S.sb([128, 5, KT]); S.dma("sp", vec, vec_d)
    pg = S.sb([128, KT]); gp2 = S.sb([128, KT]); sh2 = S.sb([128, KT])
    S.dve(lambda e: e.tensor_tensor(pg[:], vec[:, 0, :], vec[:, 1, :], ALU.mult), [vec], [pg])
    S.dve(lambda e: e.tensor_scalar(gp2[:], vec[:, 3, :], 1.0, None, ALU.add), [vec], [gp2])
    S.dve(lambda e: e.tensor_tensor(gp2[:], gp2[:], vec[:, 2, :], ALU.mult), [gp2, vec], [gp2])
    S.dve(lambda e: e.tensor_copy(sh2[:], vec[:, 4, :]), [vec], [sh2])
    cc = S.sb([128, KT, 512], BF16, name="cc")
    wrot = Rot([S.sb([128, KT, 512], BF16, name="wsl") for _ in range(2)])
    mix = S.sb([128, KT, 512], name="mix")
    xs = S.sb([128, KT, 512], name="xs")
    sqrot = Rot([S.sb([128, 512], name="sq") for _ in range(3)])
    tmprot = Rot([S.sb([128, 512], name="tmp") for _ in range(3)])
    rstd = S.sb([128, 512])
    ps_ss = S.ps([128, 512])
    psrot = Rot([S.ps([128, 512], name="pso") for _ in range(4)])
    for tq in range(NQ):
        ts = slice(tq * 512, (tq + 1) * 512)
        S.dma("pool", cc, cat_d, in_ap=cat_d.ap[:, ts].rearrange("(kt p) n -> p kt n", p=128))
        S.dma("sp", xs, x_d, in_ap=x_d.ap[:, ts].rearrange("(kt p) n -> p kt n", p=128))
        for sl in range(4):
            ws = wrot.get()
            S.dma("pool", ws, w_d, in_ap=w_d.ap[:, sl * 512:(sl + 1) * 512].rearrange("(kt p) c -> p kt c", p=128))
            for j in range(4):
                ot = sl * 4 + j
                pp_ = psrot.get()
                S.mmg(pp_[:], [(ws[:, kt, j * 128:(j + 1) * 128], cc[:, kt, :]) for kt in range(KT)], [ws, cc], [pp_])
                S.act(lambda e: e.copy(mix[:, ot, :], pp_[:]), [pp_], [mix])
        rms_rstd(S, [(mix, mix[:, kt, :]) for kt in range(KT)], 512, ones, ps_ss, sqrot, rstd, D)
        for kt in range(KT):
            t1 = tmprot.get()
            S.dve(lambda e: e.tensor_tensor(t1[:], mix[:, kt, :], rstd[:], ALU.mult), [mix, rstd], [t1])
            S.dve(lambda e: e.scalar_tensor_tensor(xs[:, kt, :], t1[:], pg[:, kt:kt + 1], xs[:, kt, :], ALU.mult, ALU.add),
                  [t1, pg, xs], [xs])
        S.dma("sp", None, xs, out_ap=x1_o.ap[:, ts].rearrange("(kt p) n -> p kt n", p=128), sem_of=xs, is_output=True)
        modulated_norm(S, xs, 512, gp2, sh2, ones, ps_ss, sqrot, rstd, tmprot, [mix[:, kt, :] for kt in range(KT)], [mix] * KT)
        S.dma("sp", None, mix, out_ap=h2_o.ap[:, ts].rearrange("(kt p) n -> p kt n", p=128), sem_of=mix, is_output=True)
    S.finish()
    return nc


NFT = D_FF // 128


def build_ffn(NT=2048):
    nc = new_nc()
    S = Sched(nc)
    NQ = NT // 512
    h2_d = S.din("h2T", [D, 2 + NT]); x1_d = S.din("x1T", [D, NT])
    wu_d = S.din("w_up", [D, 2 * D_FF]); wd_d = S.din("w_down", [D_FF, D])
    cw_d = S.din("cw", [128, 2 * NFT, 3]); cb_d = S.din("cb", [128, 2 * NFT])
    vec_d = S.din("vecs", [128, 2, KT])
    ones_d = S.din("ones", [128, 128])
    o_d = S.dout("x2T", [D, NT])
    ones = S.sb([128, 128]); S.dma("sp", ones, ones_d)
    vec = S.sb([128, 2, KT]); S.dma("sp", vec, vec_d)
    cw = S.sb([128, 2 * NFT, 3]); S.dma("sp", cw, cw_d)
    cb = S.sb([128, 2 * NFT]); S.dma("sp", cb, cb_d)
    pg = S.sb([128, KT])
    S.dve(lambda e: e.tensor_tensor(pg[:], vec[:, 0, :], vec[:, 1, :], ALU.mult), [vec], [pg])
    h2q = S.sb([128, KT, 514], BF16, name="h2q")
    actT = S.sb([128, NFT, 512], BF16, name="actT")
    wgu = [Rot([S.sb([128, KT, 128], BF16, name="wgu") for _ in range(2)]) for _ in range(2)]
    wdr = Rot([S.sb([128, NFT, 128], BF16, name="wd") for _ in range(2)])
    y = S.sb([128, KT, 512], name="y")
    urot = Rot([S.sb([128, 514], name="u") for _ in range(4)])
    crot = Rot([S.sb([128, 512], name="c") for _ in range(4)])
    x1r = Rot([S.sb([128, 512], name="x1") for _ in range(3)])
    sqrot = Rot([S.sb([128, 512], name="sq") for _ in range(3)])
    tmprot = Rot([S.sb([128, 512], name="tmp") for _ in range(3)])
    rstd = S.sb([128, 512])
    ps_ss = S.ps([128, 512])
    psrot = Rot([S.ps([128, 512], name="pso") for _ in range(4)])
    phrot = Rot([S.ps([128, 512], name="psh") for _ in range(2)])
    for tq in range(NQ):
        ts = slice(tq * 512, (tq + 1) * 512)
        S.dma("pool", h2q, h2_d, in_ap=h2_d.ap[:, tq * 512:tq * 512 + 514].rearrange("(kt p) n -> p kt n", p=128))
        for ft in range(NFT):
            cs_ = []
            for gi in range(2):
                f = gi * NFT + ft
                w = wgu[gi].get()
                S.dma("pool", w, wu_d, in_ap=wu_d.ap[:, f * 128:(f + 1) * 128].rearrange("(kt p) c -> p kt c", p=128))
                pm = psrot.get(); ph = phrot.get()
                S.mmg(pm[:], [(w[:, kt, :], h2q[:, kt, 2:514]) for kt in range(KT)], [w, h2q], [pm])
                S.mmg(ph[:, 0:2], [(w[:, kt, :], h2q[:, kt, 0:2]) for kt in range(KT)], [w, h2q], [ph])
                u = urot.get()
                S.act(lambda e: e.copy(u[:, 2:514], pm[:]), [pm], [u])
                S.dve(lambda e: e.tensor_copy(u[:, 0:2], ph[:, 0:2]), [ph], [u])
                c = crot.get()
                S.dve(lambda e: e.tensor_scalar(c[:], u[:, 2:514], cw[:, f, 2:3], cb[:, f:f + 1], ALU.mult, ALU.add), [u, cw, cb], [c])
                S.dve(lambda e: e.scalar_tensor_tensor(c[:], u[:, 1:513], cw[:, f, 1:2], c[:], ALU.mult, ALU.add), [u, cw, c], [c])
                S.dve(lambda e: e.scalar_tensor_tensor(c[:], u[:, 0:512], cw[:, f, 0:1], c[:], ALU.mult, ALU.add), [u, cw, c], [c])
                cs_.append(c)
            cg, cu = cs_
            S.act(lambda e: e.activation(cg[:], cg[:], AF.Gelu_apprx_tanh), [cg], [cg])
            S.dve(lambda e: e.tensor_tensor(actT[:, ft, :], cg[:], cu[:], ALU.mult), [cg, cu], [actT])
        for ot in range(KT):
            wd = wdr.get()
            S.dma("pool", wd, wd_d, in_ap=wd_d.ap[:, ot * 128:(ot + 1) * 128].rearrange("(ft p) c -> p ft c", p=128))
            pp_ = psrot.get()
            S.mmg(pp_[:], [(wd[:, ft, :], actT[:, ft, :]) for ft in range(NFT)], [wd, actT], [pp_])
            S.act(lambda e: e.copy(y[:, ot, :], pp_[:]), [pp_], [y])
        rms_rstd(S, [(y, y[:, kt, :]) for kt in range(KT)], 512, ones, ps_ss, sqrot, rstd, D)
        for kt in range(KT):
            t1 = tmprot.get(); x1 = x1r.get()
            S.dma("sp", x1, x1_d, in_ap=x1_d.ap[kt * 128:(kt + 1) * 128, ts])
            S.dve(lambda e: e.tensor_tensor(t1[:], y[:, kt, :], rstd[:], ALU.mult), [y, rstd], [t1])
            S.dve(lambda e: e.scalar_tensor_tensor(x1[:], t1[:], pg[:, kt:kt + 1], x1[:], ALU.mult, ALU.add), [t1, pg, x1], [x1])
            S.dma("sp", None, x1, out_ap=o_d.ap[kt * 128:(kt + 1) * 128, ts], sem_of=x1, is_output=True)
    S.finish()
    return nc


_NC_CACHE = {}


def _prog(name, builder, *args):
    return builder(*args)


def _col(v):
    return np.ascontiguousarray(np.asarray(v, np.float32).reshape(KT, 128).T)


def _cols(vs):
    return np.ascontiguousarray(np.stack([_col(v) for v in vs], 1))


def _run(nc, in_maps):
    res = run_bass_kernel_spmd(nc, in_maps, core_ids=list(range(NCORES)))
    return res.results


def _c(a):
    return np.ascontiguousarray(a)


def kernel(x, c, positions, ada_w, ada_b, mix_pre_norm, mix_post_norm, w_in, w_out,
           gdn_conv, gdn_a_log, gdn_dt_bias, gdn_norm, mla_q_norm, mla_w_uq, mla_kv_norm,
           mla_w_ukv, swa_sinks, ffn_pre_norm, ffn_post_norm, ffn_w_up, ffn_conv, ffn_conv_b,
           ffn_w_down):
    f32 = np.float32
    x = np.asarray(x, f32)
    NS = x.shape[1]
    TPC = NS // NCORES
    ones = np.ones((128, 128), f32)
    ident = np.eye(128, dtype=f32)
    xT = _c(x[0].T)
    positions = np.asarray(positions).astype(np.int32)
    tok = [slice(cc * TPC, (cc + 1) * TPC) for cc in range(NCORES)]

    ims = []
    for cc in range(NCORES):
        sl = slice(cc * 1536, (cc + 1) * 1536)
        ims.append({"c_col": _col(np.asarray(c, f32)[0]),
                    "ada_w0": _c(np.asarray(ada_w[0], f32)[:, sl]), "ada_w1": _c(np.asarray(ada_w[1], f32)[:, sl]),
                    "ada_b0": _c(np.asarray(ada_b[0], f32)[None, sl]), "ada_b1": _c(np.asarray(ada_b[1], f32)[None, sl])})
    r = _run(build_mod(), ims)
    mods = [np.concatenate([r[cc][f"mod{l}"][0] for cc in range(NCORES)]).reshape(6, D) for l in range(2)]

    inv = (10000.0 ** (-np.arange(32, dtype=f32) / 32)).astype(f32)
    inv2 = np.concatenate([inv, inv])[None, :].astype(f32)
    sgn = np.ones((64, 1), f32); sgn[:32] = -1
    swa_b, swa_bf = swa_bias_tables()
    gcst = gdn_consts()
    kidx = _c(np.arange(NS, dtype=f32).reshape(-1, 128).T)
    for l in range(2):
        shift1, scale1, gate1, shift2, scale2, gate2 = mods[l]
        w = np.asarray(w_in[l], f32)
        vec = _cols([mix_pre_norm[l], scale1, shift1])
        r = _run(build_inproj(TPC, D_IN), [{"xT": _c(xT[:, tok[cc]]), "w": w, "vecs": vec, "ones": ones} for cc in range(NCORES)])
        projT = np.concatenate([r[cc]["projT"] for cc in range(NCORES)], axis=1)
        del r
        conv = np.asarray(gdn_conv[l], f32)
        ims = []
        for h in range(8):
            im = dict(gcst)
            im["qT"] = _c(projT[h * 128:(h + 1) * 128]); im["kT"] = _c(projT[1024 + h * 128:1024 + (h + 1) * 128])
            im["vT"] = _c(projT[2048 + h * 128:2048 + (h + 1) * 128]); im["zT"] = _c(projT[3072 + h * 128:3072 + (h + 1) * 128])
            im["a_row"] = _c(projT[4096 + h][None, :]); im["b_row"] = _c(projT[4104 + h][None, :])
            im["cw"] = _c(np.concatenate([conv[:, h * 128:(h + 1) * 128].T, conv[:, 1024 + h * 128:1024 + (h + 1) * 128].T,
                                          conv[:, 2048 + h * 128:2048 + (h + 1) * 128].T], axis=1))
            par = np.empty((128, 3), f32)
            par[:, 0] = np.asarray(gdn_a_log[l], f32)[h]; par[:, 1] = np.asarray(gdn_dt_bias[l], f32)[h]
            par[:, 2] = np.asarray(gdn_norm[l], f32)
            im["par"] = par
            ims.append(im)
        r = _run(build_gdn(NS), ims)
        oaT = np.concatenate([r[h]["oT"] for h in range(8)], axis=0)
        del r, ims
        wuq = np.asarray(mla_w_uq[l], f32); wukv = np.asarray(mla_w_ukv[l], f32)
        qnw = np.asarray(mla_q_norm[l], f32)
        qn4 = np.zeros((128, 4), f32)
        for kt in range(4):
            n = min(128, 448 - kt * 128)
            qn4[:n, kt] = qnw[kt * 128:kt * 128 + n]
        wsw = _c(np.concatenate([np.concatenate([wuq[:, h * 192 + 160:h * 192 + 192], wuq[:, h * 192 + 128:h * 192 + 160]], 1)
                                 for h in range(4)], 1))
        ims = []
        for cc in range(NCORES):
            kr = projT[4688:4752, tok[cc]]
            ims.append({"cqT": _c(projT[4112:4560, tok[cc]]), "ckvT": _c(projT[4560:4688, tok[cc]]), "krT": _c(kr),
                        "krsT": _c(np.concatenate([kr[32:], kr[:32]], 0)), "pos": _c(positions[0:1, tok[cc]]),
                        "inv2": inv2, "qnw": qn4, "kvnw": _c(np.asarray(mla_kv_norm[l], f32).reshape(128, 1)),
                        "w_uq": wuq, "w_uq_sw": wsw, "w_ukv": wukv, "ones": ones, "sgn": sgn})
        r = _run(build_mlaprep(TPC), ims)
        qTf = np.concatenate([r[cc]["qT"] for cc in range(NCORES)], axis=2)
        knTf = np.concatenate([r[cc]["knT"] for cc in range(NCORES)], axis=2)
        kpeTf = np.concatenate([r[cc]["kpeT"] for cc in range(NCORES)], axis=1)
        Vf = np.concatenate([r[cc]["V"] for cc in range(NCORES)], axis=0)
        del r, ims
        nsl = NS // 512 // NCORES
        ims = []
        for cc in range(NCORES):
            tiles = [cc + NCORES * i for i in range(nsl)]
            qsel = np.concatenate([np.arange(t * 512, (t + 1) * 512) for t in tiles])
            ims.append({"qT": _c(qTf[:, :, qsel]), "qidx": _c(qsel.astype(f32)[None, :]), "knT": knTf, "kpeT": kpeTf, "V": Vf,
                        "kidx": kidx, "ones": ones, "ident": ident})
        nkt = tuple(4 * NCORES * (i + 1) for i in range(nsl)); mfrom = tuple(4 * NCORES * i for i in range(nsl))
        r = _run(build_mla(NS, nkt, mfrom), ims)
        obT = np.empty((512, NS), f32)
        for cc in range(NCORES):
            for i in range(nsl):
                t = cc + NCORES * i
                obT[:, t * 512:(t + 1) * 512] = r[cc]["obT"][:, i * 512:(i + 1) * 512]
        del r, ims, qTf, knTf, kpeTf, Vf
        kfull = np.concatenate([np.zeros((128, 128), f32), projT[5264:5392]], axis=1)
        vfull = np.concatenate([np.zeros((128, 128), f32), projT[5392:5520]], axis=1)
        sinkc = _c(np.broadcast_to(np.asarray(swa_sinks[l], f32)[None, :], (128, 8)))
        ims = []
        for cc in range(NCORES):
            ks = slice(cc * TPC, cc * TPC + TPC + 128)
            ims.append({"qT": _c(projT[4752:5264, tok[cc]]), "kT": _c(kfull[:, ks]), "V": _c(vfull[:, ks].T),
                        "bias": swa_b, "bias_first": swa_bf if cc == 0 else swa_b, "sinkc": sinkc,
                        "onec": np.ones((128, 1), f32), "ident": ident})
        r = _run(build_swa(TPC), ims)
        ocT = np.concatenate([r[cc]["ocT"] for cc in range(NCORES)], axis=1)
        del r, ims, projT, kfull, vfull
        catT = np.concatenate([oaT, obT, ocT], axis=0)
        del oaT, obT, ocT
        vec = _cols([mix_post_norm[l], gate1, ffn_pre_norm[l], scale2, shift2])
        wo = np.asarray(w_out[l], f32)
        r = _run(build_outproj(TPC), [{"catT": _c(catT[:, tok[cc]]), "xT": _c(xT[:, tok[cc]]), "w_out": wo, "vecs": vec, "ones": ones}
                                      for cc in range(NCORES)])
        x1T = np.concatenate([r[cc]["x1T"] for cc in range(NCORES)], axis=1)
        h2T = np.concatenate([np.zeros((D, 2), f32)] + [r[cc]["h2T"] for cc in range(NCORES)], axis=1)
        del r, catT
        wu = np.asarray(ffn_w_up[l], f32); wd = np.asarray(ffn_w_down[l], f32)
        cwt = _c(np.asarray(ffn_conv[l], f32).T.reshape(2 * NFT, 128, 3).transpose(1, 0, 2))
        cbt = _c(np.asarray(ffn_conv_b[l], f32).reshape(2 * NFT, 128).T)
        vec = _cols([ffn_post_norm[l], gate2])
        r = _run(build_ffn(TPC), [{"h2T": _c(h2T[:, cc * TPC:cc * TPC + TPC + 2]), "x1T": _c(x1T[:, tok[cc]]), "w_up": wu, "w_down": wd,
                                   "cw": cwt, "cb": cbt, "vecs": vec, "ones": ones} for cc in range(NCORES)])
        xT = np.concatenate([r[cc]["x2T"] for cc in range(NCORES)], axis=1)
        del r, x1T, h2T
    return _c(xT.T)[None].astype(f32)
```

```python
import contextlib
import numpy as np
import concourse.bass as bass
import concourse.mybir as mybir
from concourse.bass_utils import run_bass_kernel_spmd

F32 = mybir.dt.float32
BF16 = mybir.dt.bfloat16
I32 = mybir.dt.int32
ALU = mybir.AluOpType
AF = mybir.ActivationFunctionType
AX = mybir.AxisListType

NCORES = 8
D = 2048
KT = 16
EPS = 1e-6
D_IN = 5520
D_FF = 5632
SEM_ROT = 2000


class T:
    __slots__ = ("ap", "w", "r", "dsem", "dcnt", "base")

    def __init__(self, ap, base=None):
        self.base = base
        if not isinstance(ap, bass.AP):
            ap = ap.ap()
        self.ap = ap
        self.w = None
        self.r = {}
        self.dsem = None
        self.dcnt = 0

    def __getitem__(self, idx):
        return self.ap[idx]


class _Rec:
    def __init__(self):
        self.calls = []

    def __getattr__(self, name):
        def f(*a, **k):
            self.calls.append((name, a, k))
            return None
        return f


class Sched:
    ENG = ("pe", "act", "dve", "pool", "sp")

    def __init__(self, nc):
        self.nc = nc
        self.es = contextlib.ExitStack()
        self.ops = {e: [] for e in self.ENG}
        self.cur = {}
        self.cnt = {e: 0 for e in self.ENG}
        self.waited = {e: {} for e in self.ENG}
        self.nsem = 0
        self.out_tokens = []
        self.nname = 0
        for e in self.ENG:
            self.cur[e] = self.new_sem("p_" + e)

    def new_sem(self, name):
        self.nsem += 1
        return self.es.enter_context(self.nc.semaphore(f"{name}_{self.nsem}"))

    def sb(self, shape, dtype=F32, name=None):
        self.nname += 1
        return T(self.es.enter_context(self.nc.sbuf_tensor(f"{name or 'sb'}_{self.nname}", list(shape), dtype)))

    def ps(self, shape, dtype=F32, name=None):
        self.nname += 1
        return T(self.es.enter_context(self.nc.psum_tensor(f"{name or 'ps'}_{self.nname}", list(shape), dtype)))

    def din(self, name, shape, dtype=F32):
        return T(self.nc.dram_tensor(name, list(shape), dtype, kind="ExternalInput").ap())

    def dout(self, name, shape, dtype=F32):
        return T(self.nc.dram_tensor(name, list(shape), dtype, kind="ExternalOutput").ap())

    def _deps(self, e, reads, writes):
        need = {}

        def add(tok):
            if tok is None:
                return
            s, v = tok
            if need.get(id(s), (None, 0))[1] < v:
                need[id(s)] = (s, v)

        reads = [t.base or t for t in reads]
        writes = [t.base or t for t in writes]
        for t in reads:
            add(t.w)
        for t in writes:
            add(t.w)
            for s_v in t.r.values():
                add(s_v)
        waits = []
        for k, (s, v) in need.items():
            if s is self.cur[e] and e == "pe":
                continue
            if self.waited[e].get(k, 0) >= v:
                continue
            self.waited[e][k] = v
            waits.append((s, v))
        return waits

    def _commit(self, tok, reads, writes):
        s, v = tok
        reads = [t.base or t for t in reads]
        writes = [t.base or t for t in writes]
        for t in reads:
            old = t.r.get(id(s))
            if old is None or old[1] < v:
                t.r[id(s)] = (s, v)
        for t in writes:
            t.w = tok
            t.r = {}

    def op(self, e, fn, reads=(), writes=()):
        rec = _Rec()
        fn(rec)
        return self.emit(e, rec.calls, reads, writes)

    def emit(self, e, calls, reads=(), writes=()):
        def fn(eng, calls=calls):
            ins = None
            for name, a, k in calls:
                ins = getattr(eng, name)(*a, **k)
            return ins
        waits = self._deps(e, reads, writes)
        if self.cnt[e] >= SEM_ROT:
            self.cur[e] = self.new_sem("p_" + e)
            self.cnt[e] = 0
        self.cnt[e] += 1
        tok = (self.cur[e], self.cnt[e])
        self.ops[e].append((waits, fn, tok[0], 1))
        self._commit(tok, reads, writes)
        return tok

    def pe(self, fn, reads=(), writes=()):
        return self.op("pe", fn, reads, writes)

    def act(self, fn, reads=(), writes=()):
        return self.op("act", fn, reads, writes)

    def dve(self, fn, reads=(), writes=()):
        return self.op("dve", fn, reads, writes)

    def pool(self, fn, reads=(), writes=()):
        return self.op("pool", fn, reads, writes)

    def mm(self, out_ap, lhsT_ap, rhs_ap, reads, writes, start=True, stop=True):
        return self.op("pe", lambda e: e.matmul(out_ap, lhsT_ap, rhs_ap, start=start, stop=stop), reads, writes)

    def mmg(self, out_ap, pairs, reads, writes):
        def fn(e):
            n = len(pairs)
            ins = None
            for i, (l, r) in enumerate(pairs):
                ins = e.matmul(out_ap, l, r, start=(i == 0), stop=(i == n - 1))
            return ins
        return self.op("pe", fn, reads, writes)

    def dma(self, e, out_t, in_t, out_ap=None, in_ap=None, sem_of=None, is_output=False):
        rl = [in_t] if in_t is not None else []
        wl = [out_t] if out_t is not None else []
        waits = self._deps(e, rl, wl)
        so = sem_of or out_t
        if so.dsem is None:
            so.dsem = self.new_sem("d")
        so.dcnt += 16
        tok = (so.dsem, so.dcnt)
        oa = out_ap if out_ap is not None else out_t.ap
        ia = in_ap if in_ap is not None else in_t.ap
        self.ops[e].append((waits, lambda eng: eng.dma_start(out=oa, in_=ia), tok[0], 16))
        self._commit(tok, rl, wl)
        if is_output:
            self.out_tokens.append(tok)
        return tok

    def finish(self):
        nc = self.nc
        fin = {}
        for s, v in self.out_tokens:
            if fin.get(id(s), (None, 0))[1] < v:
                fin[id(s)] = (s, v)
        final_waits = list(fin.values())
        ops = self.ops
        emap = {"pe": "tensor", "act": "scalar", "dve": "vector", "pool": "gpsimd", "sp": "sync"}

        def body(e):
            def f(eng):
                for waits, fn, sem, inc in ops[e]:
                    for s, v in waits:
                        eng.wait_ge(s, v)
                    fn(eng).then_inc(sem, inc)
                if e == "sp":
                    for s, v in final_waits:
                        eng.wait_ge(s, v)
            return f

        with nc.Block() as block:
            for e in self.ENG:
                getattr(block, emap[e])(body(e))
        self.es.close()


class Deferred:
    def __init__(self, S):
        self.S = S
        self.items = []

    def op(self, e, fn, reads=(), writes=()):
        rec = _Rec()
        fn(rec)
        self.items.append(("op", e, rec.calls, list(reads), list(writes)))

    def pe(self, fn, reads=(), writes=()):
        self.op("pe", fn, reads, writes)

    def act(self, fn, reads=(), writes=()):
        self.op("act", fn, reads, writes)

    def dve(self, fn, reads=(), writes=()):
        self.op("dve", fn, reads, writes)

    def pool(self, fn, reads=(), writes=()):
        self.op("pool", fn, reads, writes)

    def mm(self, out_ap, lhsT_ap, rhs_ap, reads, writes, start=True, stop=True):
        self.op("pe", lambda e: e.matmul(out_ap, lhsT_ap, rhs_ap, start=start, stop=stop), reads, writes)

    def dma(self, *a, **k):
        self.items.append(("dma", a, k))


def interleave(S, ds):
    ds = [d for d in ds if d is not None]
    pos = [0] * len(ds)
    live = True
    while live:
        live = False
        for i, d in enumerate(ds):
            if pos[i] < len(d.items):
                it = d.items[pos[i]]
                pos[i] += 1
                live = True
                if it[0] == "op":
                    S.emit(it[1], it[2], it[3], it[4])
                else:
                    S.dma(*it[1], **it[2])


def new_nc():
    return bass.Bass("TRN2", target_bir_lowering=False)


class Rot:
    def __init__(self, items):
        self.items = items
        self.i = 0

    def get(self):
        t = self.items[self.i % len(self.items)]
        self.i += 1
        return t


def rms_rstd(S, xs_list, n, ones, ps_ss, sqrot, rstd, dim):
    m = len(xs_list)
    for i, (t, ap) in enumerate(xs_list):
        p = ap.shape[0]
        sq = sqrot.get()
        S.act(lambda e, sq=sq, ap=ap, p=p: e.activation(sq[0:p, 0:n], ap, AF.Square), [t], [sq])
        S.mm(ps_ss[:, 0:n], ones[0:p, :], sq[0:p, 0:n], [ones, sq], [ps_ss], start=(i == 0), stop=(i == m - 1))
    S.dve(lambda e: e.tensor_scalar(rstd[:, 0:n], ps_ss[:, 0:n], 1.0 / dim, EPS, ALU.mult, ALU.add), [ps_ss], [rstd])
    S.act(lambda e: e.activation(rstd[:, 0:n], rstd[:, 0:n], AF.Sqrt), [rstd], [rstd])
    S.dve(lambda e: e.reciprocal(rstd[:, 0:n], rstd[:, 0:n]), [rstd], [rstd])


def build_mod():
    nc = new_nc()
    S = Sched(nc)
    NC = 1536
    c_d = S.din("c_col", [128, KT])
    w_d = [S.din(f"ada_w{l}", [D, NC]) for l in range(2)]
    b_d = [S.din(f"ada_b{l}", [1, NC]) for l in range(2)]
    o_d = [S.dout(f"mod{l}", [1, NC]) for l in range(2)]
    cc = S.sb([128, KT])
    S.dma("sp", cc, c_d)
    S.act(lambda e: e.activation(cc[:], cc[:], AF.Silu), [cc], [cc])
    wsb = [S.sb([128, KT, NC], name="adaw") for _ in range(1)]
    for l in range(2):
        w = wsb[0]
        S.dma("sp", w, w_d[l], in_ap=w_d[l].ap.rearrange("(kt p) c -> p kt c", p=128))
        bsb = S.sb([1, NC])
        S.dma("sp", bsb, b_d[l])
        res = S.sb([1, NC])
        for j in range(NC // 512):
            pp = S.ps([1, 512])
            S.mmg(pp[:], [(cc[:, kt:kt + 1], w[:, kt, j * 512:(j + 1) * 512]) for kt in range(KT)], [cc, w], [pp])
            S.dve(lambda e, pp=pp, j=j, res=res, bsb=bsb: e.tensor_tensor(
                res[:, j * 512:(j + 1) * 512], pp[:], bsb[:, j * 512:(j + 1) * 512], ALU.add), [pp, bsb], [res])
        S.dma("sp", o_d[l], res, is_output=True)
    S.finish()
    return nc


def modulated_norm(S, xs, n, gp, sh, ones, ps_ss, sqrot, rstd, tmprot, out_aps, out_ts):
    rms_rstd(S, [(xs, xs[:, kt, 0:n]) for kt in range(KT)], n, ones, ps_ss, sqrot, rstd, D)
    for kt in range(KT):
        tmp = tmprot.get()
        S.dve(lambda e, tmp=tmp, kt=kt: e.tensor_tensor(tmp[:, 0:n], xs[:, kt, 0:n], rstd[:, 0:n], ALU.mult),
              [xs, rstd], [tmp])
        S.act(lambda e, tmp=tmp, kt=kt: e.activation(out_aps[kt], tmp[:, 0:n], AF.Identity,
                                                     bias=sh[:, kt:kt + 1], scale=gp[:, kt:kt + 1]),
              [tmp, gp, sh], [out_ts[kt]])


def build_inproj(NT=2048, NO=D_IN):
    nc = new_nc()
    S = Sched(nc)
    NTT = NT // 512
    x_d = S.din("xT", [D, NT])
    w_d = S.din("w", [D, NO])
    vec_d = S.din("vecs", [128, 3, KT])
    ones_d = S.din("ones", [128, 128])
    o_d = S.dout("projT", [NO, NT])
    ones = S.sb([128, 128]); S.dma("sp", ones, ones_d)
    vec = S.sb([128, 3, KT]); S.dma("sp", vec, vec_d)
    gp = S.sb([128, KT]); sh = S.sb([128, KT])
    S.dve(lambda e: e.tensor_scalar(gp[:], vec[:, 1, :], 1.0, None, ALU.add), [vec], [gp])
    S.dve(lambda e: e.tensor_tensor(gp[:], gp[:], vec[:, 0, :], ALU.mult), [gp, vec], [gp])
    S.dve(lambda e: e.tensor_copy(sh[:], vec[:, 2, :]), [vec], [sh])
    xrot = Rot([S.sb([128, KT, 512], name="xs") for _ in range(2)])
    sqrot = Rot([S.sb([128, 512], name="sq") for _ in range(3)])
    tmprot = Rot([S.sb([128, 512], name="tmp") for _ in range(3)])
    rstd = S.sb([128, 512])
    ps_ss = S.ps([128, 512])
    hT = [S.sb([128, KT, 512], BF16, name="hT") for _ in range(NTT)]
    for tt in range(NTT):
        xs = xrot.get()
        S.dma("sp", xs, x_d, in_ap=x_d.ap[:, tt * 512:(tt + 1) * 512].rearrange("(kt p) n -> p kt n", p=128))
        modulated_norm(S, xs, 512, gp, sh, ones, ps_ss, sqrot, rstd, tmprot,
                       [hT[tt][:, kt, :] for kt in range(KT)], [hT[tt]] * KT)
    wrot = Rot([S.sb([128, KT, 512], BF16, name="wsl") for _ in range(2)])
    psrot = Rot([S.ps([128, 512], name="pso") for _ in range(4)])
    evrot = Rot([S.sb([128, 512], name="ev") for _ in range(4)])
    c0 = 0
    while c0 < NO:
        cw = min(512, NO - c0)
        ws = wrot.get()
        S.dma("pool", ws, w_d, out_ap=ws[:, :, 0:cw],
              in_ap=w_d.ap[:, c0:c0 + cw].rearrange("(kt p) c -> p kt c", p=128))
        m0 = 0
        while m0 < cw:
            M = min(128, cw - m0)
            for tt in range(NTT):
                pp = psrot.get()
                S.mmg(pp[0:M, :], [(ws[:, kt, m0:m0 + M], hT[tt][:, kt, :]) for kt in range(KT)], [ws, hT[tt]], [pp])
                ev = evrot.get()
                S.act(lambda e, ev=ev, pp=pp, M=M: e.copy(ev[0:M, :], pp[0:M, :]), [pp], [ev])
                S.dma("sp", None, ev, out_ap=o_d.ap[c0 + m0:c0 + m0 + M, tt * 512:(tt + 1) * 512], in_ap=ev[0:M, :],
                      sem_of=ev, is_output=True)
            m0 += M
        c0 += cw
    S.finish()
    return nc


def gdn_consts():
    idx = np.arange(128)
    same = (idx[:, None] // 64) == (idx[None, :] // 64)
    tri = (same & (idx[:, None] <= idx[None, :])).astype(np.float32)
    mus = (same & (idx[:, None] < idx[None, :])).astype(np.float32)
    mls = np.ascontiguousarray(mus.T)
    bd2 = np.zeros((128, 2), np.float32)
    bd2[:64, 0] = 1
    bd2[64:, 1] = 1
    return {"ident": np.eye(128, dtype=np.float32), "tri": tri, "mus": mus, "mls": mls, "bd2": bd2,
            "ones": np.ones((128, 128), np.float32)}


def build_gdn(NT=16384, PIPE=True):
    nc = new_nc()
    S = Sched(nc)
    NS = NT // 512
    qkv_d = [S.din(n, [128, NT]) for n in ("qT", "kT", "vT")]
    z_d = S.din("zT", [128, NT])
    a_d = S.din("a_row", [1, NT]); b_d = S.din("b_row", [1, NT])
    cw_d = S.din("cw", [128, 12])
    par_d = S.din("par", [128, 3])
    cst = {}
    for n, shp in (("ident", [128, 128]), ("tri", [128, 128]), ("mus", [128, 128]), ("mls", [128, 128]),
                   ("bd2", [128, 2]), ("ones", [128, 128])):
        d = S.din(n, shp)
        cst[n] = S.sb(shp, name=n)
        S.dma("sp", cst[n], d)
    ident, tri, mus, mls, bd2, ones = (cst[n] for n in ("ident", "tri", "mus", "mls", "bd2", "ones"))
    o_d = S.dout("oT", [128, NT])
    cw = S.sb([128, 12]); S.dma("sp", cw, cw_d)
    par = S.sb([128, 3]); S.dma("sp", par, par_d)
    negA = S.sb([128, 1])
    S.act(lambda e: e.activation(negA[:], par[:, 0:1], AF.Exp), [par], [negA])
    S.dve(lambda e: e.tensor_scalar(negA[:], negA[:], -1.0, None, ALU.mult), [negA], [negA])
    Sst = S.sb([128, 128], name="state")
    S.dve(lambda e: e.memset(Sst[:], 0.0), [], [Sst])

    rawrot = [Rot([S.sb([128, 515], name="raw") for _ in range(2)]) for _ in range(3)]
    zrot = Rot([S.sb([128, 512], name="z") for _ in range(3)])
    arot = Rot([S.sb([1, 512], name="ar") for _ in range(3)])
    brot = Rot([S.sb([1, 512], name="br") for _ in range(3)])
    crot = [Rot([S.sb([128, 512], name="c") for _ in range(3)]) for _ in range(3)]
    yrot = Rot([S.sb([128, 512], name="y") for _ in range(2)])
    sqrot = Rot([S.sb([128, 512], name="sq") for _ in range(2)])
    rnrot = Rot([S.sb([128, 512], name="rn") for _ in range(2)])
    orot = Rot([S.sb([128, 512], name="oslab") for _ in range(3)])
    rowt = Rot([S.sb([1, 512], name="rowt") for _ in range(2)])
    banks = [S.ps([128, 512], name="bank") for _ in range(8)]
    ps_big = Rot(banks[0:1])
    pp = Rot([T(bk.ap[:, 0:128], base=bk) for bk in banks[1:4]])
    pq = Rot([T(bk.ap[:, 0:128], base=bk) for bk in banks[4:7]])
    psm = Rot([T(banks[7].ap[:, j * 128:j * 128 + 4], base=banks[7]) for j in range(4)])
    mrot = Rot([S.sb([128, 128], name="m") for _ in range(20)])
    srot = Rot([S.sb([128, 128], name="ms") for _ in range(8)])
    prodrot = {n: Rot([S.sb([128, 128], name=n) for _ in range(3)]) for n in ("u", "wT", "qkT", "Ktail")}
    scrot = Rot([S.sb([128, 8], name="sc") for _ in range(4)])
    ssrot = Rot([S.sb([128, 8], name="sso") for _ in range(3)])

    def slab_prep(X, s):
        t0 = s * 512
        cs = []
        for qi in range(3):
            raw = rawrot[qi].get()
            if s == 0:
                X.dve(lambda e: e.memset(raw[:, 0:3], 0.0), [], [raw])
                X.dma("sp", raw, qkv_d[qi], out_ap=raw[:, 3:515], in_ap=qkv_d[qi].ap[:, 0:512])
            else:
                X.dma("sp", raw, qkv_d[qi], in_ap=qkv_d[qi].ap[:, t0 - 3:t0 + 512])
            y = yrot.get()
            X.dve(lambda e: e.tensor_scalar(y[:], raw[:, 3:515], cw[:, 4 * qi + 3:4 * qi + 4], None, ALU.mult), [raw, cw], [y])
            for j in (2, 1, 0):
                X.dve(lambda e: e.scalar_tensor_tensor(y[:], raw[:, j:j + 512], cw[:, 4 * qi + j:4 * qi + j + 1], y[:],
                                                       ALU.mult, ALU.add), [raw, cw, y], [y])
            c = crot[qi].get()
            X.act(lambda e: e.activation(c[:], y[:], AF.Silu), [y], [c])
            cs.append(c)
        cq, ck, cv = cs
        sz = zrot.get()
        X.dma("sp", sz, z_d, in_ap=z_d.ap[:, t0:t0 + 512])
        X.act(lambda e: e.activation(sz[:], sz[:], AF.Silu), [sz], [sz])
        for c, extra in ((cq, 128.0 ** -0.5), (ck, 1.0)):
            sq = sqrot.get(); pb = ps_big.get(); rn = rnrot.get()
            X.act(lambda e: e.activation(sq[:], c[:], AF.Square), [c], [sq])
            X.mm(pb[:], ones[:], sq[:], [ones, sq], [pb])
            X.dve(lambda e: e.tensor_scalar(rn[:], pb[:], EPS, None, ALU.add), [pb], [rn])
            X.act(lambda e: e.activation(rn[:], rn[:], AF.Sqrt), [rn], [rn])
            X.dve(lambda e: e.reciprocal(rn[:], rn[:]), [rn], [rn])
            X.dve(lambda e: e.scalar_tensor_tensor(c[:], c[:], extra, rn[:], ALU.mult, ALU.mult), [c, rn], [c])
        gr = arot.get(); br = brot.get(); rt = rowt.get()
        X.dma("sp", gr, a_d, in_ap=a_d.ap[:, t0:t0 + 512])
        X.dma("sp", br, b_d, in_ap=b_d.ap[:, t0:t0 + 512])
        X.dve(lambda e: e.tensor_scalar(gr[:], gr[:], par[0:1, 1:2], None, ALU.add), [gr, par], [gr])
        X.act(lambda e: e.activation(rt[:], gr[:], AF.Abs), [gr], [rt])
        X.act(lambda e: e.activation(rt[:], rt[:], AF.Exp, scale=-1.0), [rt], [rt])
        X.dve(lambda e: e.tensor_scalar(rt[:], rt[:], 1.0, None, ALU.add), [rt], [rt])
        X.act(lambda e: e.activation(rt[:], rt[:], AF.Ln), [rt], [rt])
        X.dve(lambda e: e.tensor_scalar(gr[:], gr[:], 0.0, None, ALU.max), [gr], [gr])
        X.dve(lambda e: e.tensor_tensor(gr[:], gr[:], rt[:], ALU.add), [gr, rt], [gr])
        X.dve(lambda e: e.tensor_scalar(gr[:], gr[:], negA[0:1, 0:1], None, ALU.mult), [gr, negA], [gr])
        X.act(lambda e: e.activation(br[:], br[:], AF.Sigmoid), [br], [br])
        return dict(cq=cq, ck=ck, cv=cv, sz=sz, gr=gr, br=br, oslab=orot.get(), t0=t0)

    def blk_prep(X, sl, b):
        cq, ck, cv, gr, br = sl["cq"], sl["ck"], sl["cv"], sl["gr"], sl["br"]
        M = mrot.get
        bl = slice(b * 128, b * 128 + 128)
        pc = psm.get()
        X.mm(pc[:, 0:1], gr[0:1, bl], ones[0:1, 0:1], [gr, ones], [pc])
        X.mm(pc[:, 1:2], br[0:1, bl], ones[0:1, 0:1], [br, ones], [pc])
        sc = scrot.get()
        X.dve(lambda e: e.tensor_copy(sc[:, 6:8], pc[:, 0:2]), [pc], [sc])
        Gb = M()
        X.dve(lambda e: e.tensor_scalar(Gb[:], ones[:], sc[:, 6:7], None, ALU.mult), [ones, sc], [Gb])
        pg = psm.get()
        X.mm(pg[:, 0:1], tri[:], sc[:, 6:7], [tri, sc], [pg])
        X.mm(pg[:, 1:3], Gb[:], bd2[:], [Gb, bd2], [pg])
        pgrow = pp.get()
        X.mm(pgrow[:], Gb[:], tri[:], [Gb, tri], [pgrow])
        X.dve(lambda e: e.tensor_copy(sc[:, 0:1], pg[:, 0:1]), [pg], [sc])
        X.act(lambda e: e.activation(sc[:, 1:3], pg[:, 1:3], AF.Exp), [pg], [sc])
        X.dve(lambda e: e.tensor_copy(sc[0:64, 3:4], pg[0:64, 1:2]), [pg], [sc])
        X.dve(lambda e: e.tensor_copy(sc[64:128, 3:4], pg[64:128, 2:3]), [pg], [sc])
        X.act(lambda e: e.activation(sc[:, 4:5], sc[:, 0:1], AF.Exp), [sc], [sc])
        X.dve(lambda e: e.tensor_tensor(sc[:, 5:6], sc[:, 3:4], sc[:, 0:1], ALU.subtract), [sc], [sc])
        X.act(lambda e: e.activation(sc[:, 5:6], sc[:, 5:6], AF.Exp), [sc], [sc])
        X.dve(lambda e: e.tensor_tensor(sc[:, 6:7], sc[:, 7:8], sc[:, 4:5], ALU.mult), [sc], [sc])
        tdm = M(); dU = M(); dL = M()
        X.dve(lambda e: e.tensor_scalar(tdm[:], pgrow[:], sc[:, 0:1], None, ALU.subtract), [pgrow, sc], [tdm])
        X.dve(lambda e: e.tensor_scalar(dU[:], tdm[:], 0.0, None, ALU.min), [tdm], [dU])
        X.dve(lambda e: e.tensor_scalar(dL[:], tdm[:], 0.0, -1.0, ALU.max, ALU.mult), [tdm], [dL])
        X.act(lambda e: e.activation(dU[:], dU[:], AF.Exp), [dU], [dU])
        X.act(lambda e: e.activation(dL[:], dL[:], AF.Exp), [dL], [dL])
        pbrow = pp.get()
        X.mm(pbrow[:], ones[0:1, :], br[0:1, bl], [ones, br], [pbrow])
        U = M(); L = M(); R = M()
        qkT = prodrot["qkT"].get()
        X.dve(lambda e: e.tensor_tensor(U[:], dU[:], mus[:], ALU.mult), [dU, mus], [U])
        X.dve(lambda e: e.tensor_tensor(U[:], pbrow[:], U[:], ALU.mult), [U, pbrow], [U])
        pG = pp.get()
        X.mm(pG[:], ck[:, bl], ck[:, bl], [ck], [pG])
        X.dve(lambda e: e.tensor_tensor(U[:], pG[:], U[:], ALU.mult), [U, pG], [U])
        X.dve(lambda e: e.tensor_tensor(dL[:], dL[:], mls[:], ALU.mult), [dL, mls], [dL])
        X.dve(lambda e: e.scalar_tensor_tensor(L[:], pG[:], sc[:, 7:8], dL[:], ALU.mult, ALU.mult), [pG, sc, dL], [L])
        pQK = pp.get()
        X.mm(pQK[:], ck[:, bl], cq[:, bl], [ck, cq], [pQK])
        X.dve(lambda e: e.tensor_tensor(dU[:], dU[:], tri[:], ALU.mult), [dU, tri], [dU])
        X.dve(lambda e: e.tensor_tensor(qkT[:], pQK[:], dU[:], ALU.mult), [dU, pQK], [qkT])
        X.act(lambda e: e.activation(R[:], U[:], AF.Copy, scale=-1.0), [U], [R])
        X.dve(lambda e: e.tensor_tensor(R[:], R[:], ident[:], ALU.add), [ident, R], [R])
        P, Q = U, L
        for k in range(1, 6):
            pQn = pp.get()
            X.mm(pQn[:], P[:], Q[:], [P, Q], [pQn])
            Qn = M()
            X.act(lambda e: e.copy(Qn[:], pQn[:]), [pQn], [Qn])
            if k < 5:
                pPn = pp.get()
                X.mm(pPn[:], Q[:], P[:], [P, Q], [pPn])
                Pn = M()
                X.dve(lambda e: e.tensor_copy(Pn[:], pPn[:]), [pPn], [Pn])
            pR = pp.get()
            X.mm(pR[:], Qn[:], R[:], [Qn, R], [pR])
            X.dve(lambda e: e.tensor_tensor(R[:], pR[:], R[:], ALU.add), [R, pR], [R])
            Q = Qn
            if k < 5:
                P = Pn
        pKT = pp.get()
        X.pe(lambda e: e.transpose(pKT[:], ck[:, bl], ident[:]), [ck, ident], [pKT])
        Kbg = M(); Vb = M()
        Ktail = prodrot["Ktail"].get()
        X.dve(lambda e: e.tensor_scalar(Kbg[:], pKT[:], sc[:, 6:7], None, ALU.mult), [pKT, sc], [Kbg])
        X.dve(lambda e: e.tensor_scalar(Ktail[:], pKT[:], sc[:, 5:6], None, ALU.mult), [pKT, sc], [Ktail])
        pVT = pp.get()
        X.pe(lambda e: e.transpose(pVT[:], cv[:, bl], ident[:]), [cv, ident], [pVT])
        X.dve(lambda e: e.tensor_scalar(Vb[:], pVT[:], sc[:, 7:8], None, ALU.mult), [pVT, sc], [Vb])
        pu = pp.get()
        X.mm(pu[:], R[:], Vb[:], [R, Vb], [pu])
        u = prodrot["u"].get(); wT = prodrot["wT"].get()
        X.act(lambda e: e.copy(u[:], pu[:]), [pu], [u])
        pw = pp.get()
        X.mm(pw[:], Kbg[:], R[:], [Kbg, R], [pw])
        X.dve(lambda e: e.tensor_copy(wT[:], pw[:]), [pw], [wT])
        return dict(u=u, wT=wT, qkT=qkT, Ktail=Ktail, sc=sc, bl=bl)

    def blk_seq(X, sl, P, last):
        cq, sz, oslab = sl["cq"], sl["sz"], sl["oslab"]
        u, wT, qkT, Ktail, sc, bl = P["u"], P["wT"], P["qkT"], P["Ktail"], P["sc"], P["bl"]
        M = srot.get
        vnew = M(); o = M(); ot = M()
        for c in range(2):
            rc = slice(c * 64, (c + 1) * 64)
            pv = pq.get()
            X.mm(pv[:], wT[:], Sst[:], [wT, Sst], [pv])
            X.dve(lambda e: e.scalar_tensor_tensor(vnew[rc, :], pv[rc, :], -1.0, u[rc, :], ALU.mult, ALU.add), [u, pv], [vnew])
            pS = pq.get()
            X.mm(pS[:], Ktail[rc, :], vnew[rc, :], [Ktail, vnew], [pS])
            po1 = pq.get()
            X.mm(po1[:], cq[:, bl], Sst[:], [cq, Sst], [po1])
            X.dve(lambda e: e.tensor_scalar(Sst[:], Sst[:], sc[:, 1 + c:2 + c], None, ALU.mult), [Sst, sc], [Sst])
            X.dve(lambda e: e.tensor_tensor(Sst[:], pS[:], Sst[:], ALU.add), [Sst, pS], [Sst])
            X.dve(lambda e: e.tensor_scalar(ot[rc, :], po1[rc, :], sc[rc, 4:5], None, ALU.mult), [po1, sc], [ot])
            po2 = pq.get()
            X.mm(po2[:], qkT[rc, :], vnew[rc, :], [qkT, vnew], [po2])
            X.dve(lambda e: e.tensor_tensor(o[rc, :], po2[rc, :], ot[rc, :], ALU.add), [ot, po2], [o])
        sso = ssrot.get(); osq = M()
        X.dve(lambda e: e.memset(sso[:, 0:1], 0.0), [], [sso])
        X.act(lambda e: e.activation(osq[:], o[:], AF.Square, accum_out=sso[:, 0:1]), [o, sso], [osq, sso])
        X.dve(lambda e: e.tensor_scalar(sso[:, 0:1], sso[:, 0:1], 1.0 / 128, EPS, ALU.mult, ALU.add), [sso], [sso])
        X.act(lambda e: e.activation(sso[:, 0:1], sso[:, 0:1], AF.Sqrt), [sso], [sso])
        X.dve(lambda e: e.reciprocal(sso[:, 0:1], sso[:, 0:1]), [sso], [sso])
        X.dve(lambda e: e.tensor_scalar(osq[:], o[:], sso[:, 0:1], None, ALU.mult), [o, sso], [osq])
        pOT = pq.get()
        X.pe(lambda e: e.transpose(pOT[:], osq[:], ident[:]), [osq, ident], [pOT])
        X.dve(lambda e: e.scalar_tensor_tensor(oslab[:, bl], pOT[:], par[:, 2:3], sz[:, bl], ALU.mult, ALU.mult),
              [pOT, par, sz], [oslab])
        if last:
            X.dma("sp", None, oslab, out_ap=o_d.ap[:, sl["t0"]:sl["t0"] + 512], sem_of=oslab, is_output=True)

    pending = None
    for s in range(NS):
        dprep = Deferred(S)
        sl = slab_prep(dprep, s)
        for b in range(4):
            if b > 0:
                dprep = Deferred(S)
            P = blk_prep(dprep, sl, b)
            if PIPE:
                interleave(S, [pending, dprep])
            else:
                interleave(S, [pending]); interleave(S, [dprep])
            pending = Deferred(S)
            blk_seq(pending, sl, P, b == 3)
    interleave(S, [pending])
    S.finish()
    return nc


def build_gdn_v1(NT=16384, STAGE=9):
    nc = new_nc()
    S = Sched(nc)
    NS = NT // 512
    qkv_d = [S.din(n, [128, NT]) for n in ("qT", "kT", "vT")]
    z_d = S.din("zT", [128, NT])
    a_d = S.din("a_row", [1, NT]); b_d = S.din("b_row", [1, NT])
    cw_d = S.din("cw", [128, 12])
    par_d = S.din("par", [128, 3])
    cst = {}
    for n, shp in (("ident", [128, 128]), ("tri", [128, 128]), ("mus", [128, 128]), ("mls", [128, 128]),
                   ("bd2", [128, 2]), ("ones", [128, 128])):
        d = S.din(n, shp)
        cst[n] = S.sb(shp, name=n)
        S.dma("sp", cst[n], d)
    ident, tri, mus, mls, bd2, ones = (cst[n] for n in ("ident", "tri", "mus", "mls", "bd2", "ones"))
    o_d = S.dout("oT", [128, NT])
    cw = S.sb([128, 12]); S.dma("sp", cw, cw_d)
    par = S.sb([128, 3]); S.dma("sp", par, par_d)
    negA = S.sb([128, 1])
    S.act(lambda e: e.activation(negA[:], par[:, 0:1], AF.Exp), [par], [negA])
    S.dve(lambda e: e.tensor_scalar(negA[:], negA[:], -1.0, None, ALU.mult), [negA], [negA])
    Sst = S.sb([128, 128], name="state")
    S.dve(lambda e: e.memset(Sst[:], 0.0), [], [Sst])

    rawrot = [Rot([S.sb([128, 515], name="raw") for _ in range(2)]) for _ in range(3)]
    zrot = Rot([S.sb([128, 512], name="z") for _ in range(2)])
    arot = Rot([S.sb([1, 512], name="ar") for _ in range(2)])
    brot = Rot([S.sb([1, 512], name="br") for _ in range(2)])
    crot = [Rot([S.sb([128, 512], name="c") for _ in range(2)]) for _ in range(3)]
    yrot = Rot([S.sb([128, 512], name="y") for _ in range(2)])
    sqrot = Rot([S.sb([128, 512], name="sq") for _ in range(2)])
    rnrot = Rot([S.sb([128, 512], name="rn") for _ in range(2)])
    orot = Rot([S.sb([128, 512], name="oslab") for _ in range(2)])
    rowt = Rot([S.sb([1, 512], name="rowt") for _ in range(2)])
    ps_big = Rot([S.ps([128, 512], name="psb") for _ in range(2)])
    banks = [S.ps([128, 512], name="bank") for _ in range(6)]
    pp = Rot([T(bk.ap[:, 0:128], base=bk) for bk in banks[:5]])
    psm = Rot([T(banks[5].ap[:, j * 128:j * 128 + 4], base=banks[5]) for j in range(4)])
    mrot = Rot([S.sb([128, 128], name="m") for _ in range(24)])
    crot_s = Rot([S.sb([128, 8], name="sc") for _ in range(4)])

    def M():
        return mrot.get()

    for s in range(NS):
        t0 = s * 512
        cs = []
        for qi in range(3):
            raw = rawrot[qi].get()
            if s == 0:
                S.dve(lambda e, raw=raw: e.memset(raw[:, 0:3], 0.0), [], [raw])
                S.dma("sp", raw, qkv_d[qi], out_ap=raw[:, 3:515], in_ap=qkv_d[qi].ap[:, 0:512])
            else:
                S.dma("sp", raw, qkv_d[qi], in_ap=qkv_d[qi].ap[:, t0 - 3:t0 + 512])
            y = yrot.get()
            S.dve(lambda e, y=y, raw=raw, qi=qi: e.tensor_scalar(y[:], raw[:, 3:515], cw[:, 4 * qi + 3:4 * qi + 4], None,
                                                                 ALU.mult), [raw, cw], [y])
            for j in (2, 1, 0):
                S.dve(lambda e, y=y, raw=raw, qi=qi, j=j: e.scalar_tensor_tensor(
                    y[:], raw[:, j:j + 512], cw[:, 4 * qi + j:4 * qi + j + 1], y[:], ALU.mult, ALU.add), [raw, cw, y], [y])
            c = crot[qi].get()
            S.act(lambda e, c=c, y=y: e.activation(c[:], y[:], AF.Silu), [y], [c])
            cs.append(c)
        cq, ck, cv = cs
        sz = zrot.get()
        S.dma("sp", sz, z_d, in_ap=z_d.ap[:, t0:t0 + 512])
        S.act(lambda e, sz=sz: e.activation(sz[:], sz[:], AF.Silu), [sz], [sz])
        for c, extra in ((cq, 128.0 ** -0.5), (ck, 1.0)):
            sq = sqrot.get(); pb = ps_big.get(); rn = rnrot.get()
            S.act(lambda e, sq=sq, c=c: e.activation(sq[:], c[:], AF.Square), [c], [sq])
            S.mm(pb[:], ones[:], sq[:], [ones, sq], [pb])
            S.dve(lambda e, rn=rn, pb=pb: e.tensor_scalar(rn[:], pb[:], EPS, None, ALU.add), [pb], [rn])
            S.act(lambda e, rn=rn: e.activation(rn[:], rn[:], AF.Sqrt), [rn], [rn])
            S.dve(lambda e, rn=rn: e.reciprocal(rn[:], rn[:]), [rn], [rn])
            S.dve(lambda e, c=c, rn=rn, extra=extra: e.scalar_tensor_tensor(c[:], c[:], extra, rn[:], ALU.mult, ALU.mult),
                  [c, rn], [c])
        gr = arot.get(); br = brot.get(); rt = rowt.get()
        S.dma("sp", gr, a_d, in_ap=a_d.ap[:, t0:t0 + 512])
        S.dma("sp", br, b_d, in_ap=b_d.ap[:, t0:t0 + 512])
        S.dve(lambda e, gr=gr: e.tensor_scalar(gr[:], gr[:], par[0:1, 1:2], None, ALU.add), [gr, par], [gr])
        S.dve(lambda e, gr=gr, rt=rt: e.tensor_scalar(rt[:], gr[:], 0.0, None, ALU.abs_max), [gr], [rt]) if False else None
        S.act(lambda e, gr=gr, rt=rt: e.activation(rt[:], gr[:], AF.Abs), [gr], [rt])
        S.act(lambda e, rt=rt: e.activation(rt[:], rt[:], AF.Exp, scale=-1.0), [rt], [rt])
        S.dve(lambda e, rt=rt: e.tensor_scalar(rt[:], rt[:], 1.0, None, ALU.add), [rt], [rt])
        S.act(lambda e, rt=rt: e.activation(rt[:], rt[:], AF.Ln), [rt], [rt])
        S.dve(lambda e, gr=gr: e.tensor_scalar(gr[:], gr[:], 0.0, None, ALU.max), [gr], [gr])
        S.dve(lambda e, gr=gr, rt=rt: e.tensor_tensor(gr[:], gr[:], rt[:], ALU.add), [gr, rt], [gr])
        S.dve(lambda e, gr=gr: e.tensor_scalar(gr[:], gr[:], negA[0:1, 0:1], None, ALU.mult), [gr, negA], [gr])
        S.act(lambda e, br=br: e.activation(br[:], br[:], AF.Sigmoid), [br], [br])
        oslab = orot.get()
        if STAGE < 2:
            S.dve(lambda e, oslab=oslab, cq=cq, ck=ck, cv=cv, sz=sz: e.tensor_tensor(oslab[:], cq[:], ck[:], ALU.add), [cq, ck, cv, sz, gr, br], [oslab])
        for b in range(4 if STAGE >= 2 else 0):
            c0 = b * 128
            bl = slice(c0, c0 + 128)
            pc = psm.get()
            S.mm(pc[:, 0:1], gr[0:1, bl], ones[0:1, 0:1], [gr, ones], [pc])
            S.mm(pc[:, 1:2], br[0:1, bl], ones[0:1, 0:1], [br, ones], [pc])
            sc = crot_s.get()
            S.dve(lambda e, sc=sc, pc=pc: e.tensor_copy(sc[:, 6:8], pc[:, 0:2]), [pc], [sc])
            Gb = M()
            S.dve(lambda e, Gb=Gb, sc=sc: e.tensor_scalar(Gb[:], ones[:], sc[:, 6:7], None, ALU.mult), [ones, sc], [Gb])
            pg = psm.get()
            S.mm(pg[:, 0:1], tri[:], sc[:, 6:7], [tri, sc], [pg])
            S.mm(pg[:, 1:3], Gb[:], bd2[:], [Gb, bd2], [pg])
            pgrow = pp.get()
            S.mm(pgrow[:], Gb[:], tri[:], [Gb, tri], [pgrow])
            pbrow = pp.get()
            S.mm(pbrow[:], ones[0:1, :], br[0:1, bl], [ones, br], [pbrow])
            S.dve(lambda e, sc=sc, pg=pg: e.tensor_copy(sc[:, 0:1], pg[:, 0:1]), [pg], [sc])
            S.act(lambda e, sc=sc, pg=pg: e.activation(sc[:, 1:3], pg[:, 1:3], AF.Exp), [pg], [sc])
            S.dve(lambda e, sc=sc, pg=pg: e.tensor_copy(sc[0:64, 3:4], pg[0:64, 1:2]), [pg], [sc])
            S.dve(lambda e, sc=sc, pg=pg: e.tensor_copy(sc[64:128, 3:4], pg[64:128, 2:3]), [pg], [sc])
            S.act(lambda e, sc=sc: e.activation(sc[:, 4:5], sc[:, 0:1], AF.Exp), [sc], [sc])
            S.dve(lambda e, sc=sc: e.tensor_tensor(sc[:, 5:6], sc[:, 3:4], sc[:, 0:1], ALU.subtract), [sc], [sc])
            S.act(lambda e, sc=sc: e.activation(sc[:, 5:6], sc[:, 5:6], AF.Exp), [sc], [sc])
            S.dve(lambda e, sc=sc: e.tensor_tensor(sc[:, 6:7], sc[:, 7:8], sc[:, 4:5], ALU.mult), [sc], [sc])
            tdm = M(); dU = M(); dL = M()
            S.dve(lambda e, tdm=tdm, pgrow=pgrow, sc=sc: e.tensor_scalar(tdm[:], pgrow[:], sc[:, 0:1], None, ALU.subtract),
                  [pgrow, sc], [tdm])
            S.dve(lambda e, tdm=tdm, dU=dU: e.tensor_scalar(dU[:], tdm[:], 0.0, None, ALU.min), [tdm], [dU])
            S.dve(lambda e, tdm=tdm, dL=dL: e.tensor_scalar(dL[:], tdm[:], 0.0, -1.0, ALU.max, ALU.mult), [tdm], [dL])
            S.act(lambda e, dU=dU: e.activation(dU[:], dU[:], AF.Exp), [dU], [dU])
            S.act(lambda e, dL=dL: e.activation(dL[:], dL[:], AF.Exp), [dL], [dL])
            if STAGE < 3:
                S.dve(lambda e, oslab=oslab, dU=dU, bl=bl: e.tensor_copy(oslab[:, bl], dU[:]), [dU, dL, sc], [oslab])
                continue
            pG = pp.get(); pQK = pp.get()
            S.mm(pG[:], ck[:, bl], ck[:, bl], [ck], [pG])
            S.mm(pQK[:], ck[:, bl], cq[:, bl], [ck, cq], [pQK])
            U = M(); L = M(); qkT = M(); R = M()
            import os
            SUB = int(os.environ.get("GDN_SUB", "9"))
            if SUB == 1:
                S.dve(lambda e, oslab=oslab, pG=pG, bl=bl: e.tensor_copy(oslab[:, bl], pG[:]), [pG, pQK], [oslab])
                continue
            if SUB == 2:
                S.dve(lambda e, U=U, dU=dU: e.tensor_tensor(U[:], dU[:], mus[:], ALU.mult), [dU, mus], [U])
                S.dve(lambda e, U=U, pbrow=pbrow: e.tensor_tensor(U[:], pbrow[:], U[:], ALU.mult), [U, pbrow], [U])
                S.dve(lambda e, U=U, pG=pG: e.tensor_tensor(U[:], pG[:], U[:], ALU.mult), [U, pG], [U])
                S.dve(lambda e, oslab=oslab, U=U, bl=bl: e.tensor_copy(oslab[:, bl], U[:]), [U, pQK], [oslab])
                continue
            if SUB == 3:
                S.dve(lambda e, dL=dL: e.tensor_tensor(dL[:], dL[:], mls[:], ALU.mult), [dL, mls], [dL])
                S.dve(lambda e, L=L, pG=pG, sc=sc, dL=dL: e.scalar_tensor_tensor(L[:], pG[:], sc[:, 7:8], dL[:], ALU.mult, ALU.mult),
                      [pG, sc, dL], [L])
                S.dve(lambda e, oslab=oslab, L=L, bl=bl: e.tensor_copy(oslab[:, bl], L[:]), [L, pQK], [oslab])
                continue
            S.dve(lambda e, U=U, dU=dU: e.tensor_tensor(U[:], dU[:], mus[:], ALU.mult), [dU, mus], [U])
            S.dve(lambda e, U=U, pbrow=pbrow: e.tensor_tensor(U[:], pbrow[:], U[:], ALU.mult), [U, pbrow], [U])
            S.dve(lambda e, U=U, pG=pG: e.tensor_tensor(U[:], pG[:], U[:], ALU.mult), [U, pG], [U])
            S.dve(lambda e, dL=dL: e.tensor_tensor(dL[:], dL[:], mls[:], ALU.mult), [dL, mls], [dL])
            S.dve(lambda e, L=L, pG=pG, sc=sc, dL=dL: e.scalar_tensor_tensor(L[:], pG[:], sc[:, 7:8], dL[:], ALU.mult, ALU.mult),
                  [pG, sc, dL], [L])
            S.dve(lambda e, dU=dU: e.tensor_tensor(dU[:], dU[:], tri[:], ALU.mult), [dU, tri], [dU])
            S.dve(lambda e, qkT=qkT, pQK=pQK, dU=dU: e.tensor_tensor(qkT[:], pQK[:], dU[:], ALU.mult), [dU, pQK], [qkT])
            if SUB == 4:
                S.dve(lambda e, oslab=oslab, qkT=qkT, bl=bl: e.tensor_copy(oslab[:, bl], qkT[:]), [U, L, qkT], [oslab])
                continue
            if SUB == 7:
                S.dve(lambda e, R=R, U=U: e.tensor_copy(R[:], U[:]), [U], [R])
                S.dve(lambda e, oslab=oslab, R=R, bl=bl: e.tensor_copy(oslab[:, bl], R[:]), [U, L, qkT, R], [oslab])
                continue
            if SUB == 6:
                S.dve(lambda e, R=R: e.memset(R[:], 1.0), [], [R])
                S.dve(lambda e, oslab=oslab, R=R, bl=bl: e.tensor_copy(oslab[:, bl], R[:]), [U, L, qkT, R], [oslab])
                continue
            S.act(lambda e, R=R, U=U: e.activation(R[:], U[:], AF.Copy, scale=-1.0), [U], [R])
            S.dve(lambda e, R=R: e.tensor_tensor(R[:], R[:], ident[:], ALU.add), [ident, R], [R])
            if SUB == 5:
                S.dve(lambda e, oslab=oslab, qkT=qkT, bl=bl: e.tensor_copy(oslab[:, bl], qkT[:]), [U, L, qkT, R], [oslab])
                continue
            if STAGE < 4:
                S.dve(lambda e, oslab=oslab, R=R, bl=bl: e.tensor_copy(oslab[:, bl], R[:]), [R, L, qkT], [oslab])
                continue
            P, Q = U, L
            for k in range(1, 6):
                pQn = pp.get()
                S.mm(pQn[:], P[:], Q[:], [P, Q], [pQn])
                Qn = M()
                S.act(lambda e, Qn=Qn, pQn=pQn: e.copy(Qn[:], pQn[:]), [pQn], [Qn])
                if k < 5:
                    pPn = pp.get()
                    S.mm(pPn[:], Q[:], P[:], [P, Q], [pPn])
                    Pn = M()
                    S.dve(lambda e, Pn=Pn, pPn=pPn: e.tensor_copy(Pn[:], pPn[:]), [pPn], [Pn])
                pR = pp.get()
                S.mm(pR[:], Qn[:], R[:], [Qn, R], [pR])
                S.dve(lambda e, R=R, pR=pR: e.tensor_tensor(R[:], pR[:], R[:], ALU.add), [R, pR], [R])
                Q = Qn
                if k < 5:
                    P = Pn
            if STAGE < 5:
                S.dve(lambda e, oslab=oslab, R=R, bl=bl: e.tensor_copy(oslab[:, bl], R[:]), [R, L, qkT], [oslab])
                continue
            if SUB == 10:
                S.dve(lambda e, oslab=oslab, R=R, bl=bl: e.tensor_copy(oslab[:, bl], R[:]), [R, L, qkT], [oslab])
                continue
            pKT = pp.get(); pVT = pp.get()
            S.pe(lambda e, pKT=pKT: e.transpose(pKT[:], ck[:, bl], ident[:]), [ck, ident], [pKT])
            S.pe(lambda e, pVT=pVT: e.transpose(pVT[:], cv[:, bl], ident[:]), [cv, ident], [pVT])
            Kbg = M(); Ktail = M(); Vb = M()
            S.dve(lambda e, Kbg=Kbg, pKT=pKT, sc=sc: e.tensor_scalar(Kbg[:], pKT[:], sc[:, 6:7], None, ALU.mult), [pKT, sc], [Kbg])
            S.dve(lambda e, Ktail=Ktail, pKT=pKT, sc=sc: e.tensor_scalar(Ktail[:], pKT[:], sc[:, 5:6], None, ALU.mult),
                  [pKT, sc], [Ktail])
            S.dve(lambda e, Vb=Vb, pVT=pVT, sc=sc: e.tensor_scalar(Vb[:], pVT[:], sc[:, 7:8], None, ALU.mult), [pVT, sc], [Vb])
            if SUB == 11:
                S.dve(lambda e, oslab=oslab, Kbg=Kbg, bl=bl: e.tensor_copy(oslab[:, bl], Kbg[:]), [R, L, qkT, Kbg, Ktail, Vb], [oslab])
                continue
            pu = pp.get(); pw = pp.get()
            S.mm(pu[:], R[:], Vb[:], [R, Vb], [pu])
            S.mm(pw[:], Kbg[:], R[:], [Kbg, R], [pw])
            u = M(); wT = M()
            S.act(lambda e, u=u, pu=pu: e.copy(u[:], pu[:]), [pu], [u])
            S.dve(lambda e, wT=wT, pw=pw: e.tensor_copy(wT[:], pw[:]), [pw], [wT])
            if STAGE < 6:
                S.dve(lambda e, oslab=oslab, u=u, bl=bl: e.tensor_copy(oslab[:, bl], u[:]), [u, wT, Ktail], [oslab])
                continue
            vnew = M(); o = M(); ot = M()
            for c in range(2):
                rc = slice(c * 64, (c + 1) * 64)
                pv = pp.get()
                S.mm(pv[:], wT[:], Sst[:], [wT, Sst], [pv])
                S.dve(lambda e, vnew=vnew, u=u, pv=pv, rc=rc: e.scalar_tensor_tensor(vnew[rc, :], pv[rc, :], -1.0, u[rc, :], ALU.mult, ALU.add),
                      [u, pv], [vnew])
                po1 = pp.get(); po2 = pp.get()
                S.mm(po1[:], cq[:, bl], Sst[:], [cq, Sst], [po1])
                S.mm(po2[:], qkT[rc, :], vnew[rc, :], [qkT, vnew], [po2])
                S.dve(lambda e, ot=ot, po1=po1, sc=sc, rc=rc: e.tensor_scalar(ot[rc, :], po1[rc, :], sc[rc, 4:5], None, ALU.mult),
                      [po1, sc], [ot])
                S.dve(lambda e, o=o, ot=ot, po2=po2, rc=rc: e.tensor_tensor(o[rc, :], po2[rc, :], ot[rc, :], ALU.add),
                      [ot, po2], [o])
                pS = pp.get()
                S.mm(pS[:], Ktail[rc, :], vnew[rc, :], [Ktail, vnew], [pS])
                S.dve(lambda e, sc=sc, c=c: e.tensor_scalar(Sst[:], Sst[:], sc[:, 1 + c:2 + c], None, ALU.mult), [Sst, sc], [Sst])
                S.dve(lambda e, pS=pS: e.tensor_tensor(Sst[:], pS[:], Sst[:], ALU.add), [Sst, pS], [Sst])
            if STAGE < 7:
                S.dve(lambda e, oslab=oslab, o=o, bl=bl: e.tensor_copy(oslab[:, bl], o[:]), [o, Sst], [oslab])
                continue
            sso = crot_s.get(); osq = M()
            S.dve(lambda e, sso=sso: e.memset(sso[:, 0:1], 0.0), [], [sso])
            S.act(lambda e, osq=osq, o=o, sso=sso: e.activation(osq[:], o[:], AF.Square, accum_out=sso[:, 0:1]), [o, sso], [osq, sso])
            S.dve(lambda e, sso=sso: e.tensor_scalar(sso[:, 0:1], sso[:, 0:1], 1.0 / 128, EPS, ALU.mult, ALU.add), [sso], [sso])
            S.act(lambda e, sso=sso: e.activation(sso[:, 0:1], sso[:, 0:1], AF.Sqrt), [sso], [sso])
            S.dve(lambda e, sso=sso: e.reciprocal(sso[:, 0:1], sso[:, 0:1]), [sso], [sso])
            S.dve(lambda e, osq=osq, o=o, sso=sso: e.tensor_scalar(osq[:], o[:], sso[:, 0:1], None, ALU.mult), [o, sso], [osq])
            pOT = pp.get()
            S.pe(lambda e, pOT=pOT, osq=osq: e.transpose(pOT[:], osq[:], ident[:]), [osq, ident], [pOT])
            S.dve(lambda e, oslab=oslab, pOT=pOT, sz=sz, bl=bl: e.scalar_tensor_tensor(
                oslab[:, bl], pOT[:], par[:, 2:3], sz[:, bl], ALU.mult, ALU.mult), [pOT, par, sz], [oslab])
        S.dma("sp", None, oslab, out_ap=o_d.ap[:, t0:t0 + 512], sem_of=oslab, is_output=True)
    S.finish()
    return nc


TWO_PI = 6.283185307179586
CW1 = 6.28125
CW2 = TWO_PI - CW1


def build_mlaprep(NT=2048):
    nc = new_nc()
    S = Sched(nc)
    NTT = NT // 512
    cq_d = S.din("cqT", [448, NT]); ckv_d = S.din("ckvT", [128, NT])
    kr_d = S.din("krT", [64, NT]); krs_d = S.din("krsT", [64, NT])
    pos_d = S.din("pos", [1, NT], I32)
    inv_d = S.din("inv2", [1, 64])
    qnw_d = S.din("qnw", [128, 4]); kvnw_d = S.din("kvnw", [128, 1])
    wq_d = S.din("w_uq", [448, 768]); wqs_d = S.din("w_uq_sw", [448, 256])
    wkv_d = S.din("w_ukv", [128, 1024])
    ones_d = S.din("ones", [128, 128])
    sgn_d = S.din("sgn", [64, 1])
    q_o = S.dout("qT", [4, 192, NT]); kn_o = S.dout("knT", [4, 128, NT]); kpe_o = S.dout("kpeT", [64, NT])
    v_o = S.dout("V", [NT, 512])
    ones = S.sb([128, 128]); S.dma("sp", ones, ones_d)
    inv2 = S.sb([1, 64]); S.dma("sp", inv2, inv_d)
    qnw = S.sb([128, 4]); S.dma("sp", qnw, qnw_d)
    kvnw = S.sb([128, 1]); S.dma("sp", kvnw, kvnw_d)
    sgn = S.sb([64, 1]); S.dma("sp", sgn, sgn_d)
    KS = [128, 128, 128, 64]
    wq = S.sb([128, 4, 768], BF16); wqs = S.sb([128, 4, 256], BF16)
    for kt in range(4):
        S.dma("pool", wq, wq_d, out_ap=wq[0:KS[kt], kt, :], in_ap=wq_d.ap[kt * 128:kt * 128 + KS[kt], :])
        S.dma("pool", wqs, wqs_d, out_ap=wqs[0:KS[kt], kt, :], in_ap=wqs_d.ap[kt * 128:kt * 128 + KS[kt], :])
    wkv = S.sb([128, 1024], BF16); S.dma("pool", wkv, wkv_d)
    wv = S.sb([128, 512], BF16)
    for h in range(4):
        S.dma("pool", wv, wkv_d, out_ap=wv[:, h * 128:(h + 1) * 128], in_ap=wkv_d.ap[:, h * 256 + 128:h * 256 + 256])
    sqrot = Rot([S.sb([128, 512], name="sq") for _ in range(2)])
    rstd = S.sb([128, 512])
    banks = [S.ps([128, 512], name="bank") for _ in range(8)]
    ps_ss = banks[0]
    prot = Rot(banks[1:8])
    xq = Rot([S.sb([128, 4, 512], name="xq") for _ in range(2)])
    xkv = Rot([S.sb([128, 512], name="xkv") for _ in range(2)])
    xr = Rot([S.sb([64, 512], name="xr") for _ in range(2)])
    xrs = Rot([S.sb([64, 512], name="xrs") for _ in range(2)])
    posr = Rot([S.sb([1, 512], name="posr") for _ in range(2)])
    posi = Rot([S.sb([1, 512], I32, name="posi") for _ in range(2)])
    cqn = Rot([S.sb([128, 4, 512], BF16, name="cqn") for _ in range(2)])
    kvn = Rot([S.sb([128, 512], BF16, name="kvn") for _ in range(2)])
    tmpf = Rot([S.sb([128, 512], name="tmpf") for _ in range(3)])
    tmpi = S.sb([64, 512], I32)
    cs = [S.sb([64, 512], name="cos2"), S.sb([64, 512], name="sin2")]
    ev = Rot([S.sb([128, 512], name="ev") for _ in range(4)])

    def evac_out(pp_, M, dst_ap, scale=None):
        e_ = ev.get()
        if scale is None:
            S.act(lambda e: e.copy(e_[0:M, :], pp_[0:M, :]), [pp_], [e_])
        else:
            S.dve(lambda e: e.tensor_scalar(e_[0:M, :], pp_[0:M, :], scale, None, ALU.mult), [pp_], [e_])
        S.dma("sp", None, e_, out_ap=dst_ap, in_ap=e_[0:M, :], sem_of=e_, is_output=True)

    def rope_apply(x_ap, xs_ap, x_t, xs_t, dst):
        t1 = tmpf.get()
        S.dve(lambda e: e.tensor_tensor(t1[0:64, :], x_ap, cs[0][:], ALU.mult), [x_t, cs[0]], [t1])
        S.dve(lambda e: e.tensor_tensor(dst[0:64, :], xs_ap, cs[1][:], ALU.mult), [xs_t, cs[1]], [dst])
        S.dve(lambda e: e.tensor_tensor(dst[0:64, :], dst[0:64, :], t1[0:64, :], ALU.add), [dst, t1], [dst])

    for tt in range(NTT):
        ts = slice(tt * 512, (tt + 1) * 512)
        x = xq.get()
        for kt in range(4):
            S.dma("sp", x, cq_d, out_ap=x[0:KS[kt], kt, :], in_ap=cq_d.ap[kt * 128:kt * 128 + KS[kt], ts])
        xk = xkv.get(); S.dma("sp", xk, ckv_d, in_ap=ckv_d.ap[:, ts])
        r_ = xr.get(); S.dma("sp", r_, kr_d, in_ap=kr_d.ap[:, ts])
        rs_ = xrs.get(); S.dma("sp", rs_, krs_d, in_ap=krs_d.ap[:, ts])
        pi_ = posi.get(); S.dma("sp", pi_, pos_d, in_ap=pos_d.ap[:, ts])
        pr = posr.get()
        S.dve(lambda e: e.tensor_copy(pr[:], pi_[:]), [pi_], [pr])
        pang = prot.get()
        S.mm(pang[0:64, :], inv2[0:1, :], pr[0:1, :], [inv2, pr], [pang])
        for ci, shift in ((0, np.pi / 2), (1, 0.0)):
            a_ = tmpf.get(); kf = tmpf.get()
            S.dve(lambda e: e.tensor_scalar(a_[0:64, :], pang[0:64, :], shift, None, ALU.add), [pang], [a_])
            S.dve(lambda e: e.tensor_scalar(tmpi[:], a_[0:64, :], 1.0 / TWO_PI, None, ALU.mult), [a_], [tmpi])
            S.dve(lambda e: e.tensor_copy(kf[0:64, :], tmpi[:]), [tmpi], [kf])
            S.dve(lambda e: e.scalar_tensor_tensor(a_[0:64, :], kf[0:64, :], -CW1, a_[0:64, :], ALU.mult, ALU.add), [kf, a_], [a_])
            S.dve(lambda e: e.scalar_tensor_tensor(a_[0:64, :], kf[0:64, :], -CW2, a_[0:64, :], ALU.mult, ALU.add), [kf, a_], [a_])
            S.dve(lambda e: e.tensor_scalar(a_[0:64, :], a_[0:64, :], 3.1415925, -3.1415925, ALU.min, ALU.max), [a_], [a_])
            S.act(lambda e: e.activation(cs[ci][:], a_[0:64, :], AF.Sin), [a_], [cs[ci]])
        S.dve(lambda e: e.tensor_scalar(cs[1][:], cs[1][:], sgn[:, 0:1], None, ALU.mult), [cs[1], sgn], [cs[1]])
        rms_rstd(S, [(x, x[0:KS[kt], kt, :]) for kt in range(4)], 512, ones, ps_ss, sqrot, rstd, 448)
        cn = cqn.get()
        for kt in range(4):
            t1 = tmpf.get()
            S.dve(lambda e: e.tensor_tensor(t1[0:KS[kt], :], x[0:KS[kt], kt, :], rstd[0:KS[kt], :], ALU.mult), [x, rstd], [t1])
            S.dve(lambda e: e.tensor_scalar(cn[0:KS[kt], kt, :], t1[0:KS[kt], :], qnw[0:KS[kt], kt:kt + 1], None, ALU.mult),
                  [t1, qnw], [cn])
        qs = 192.0 ** -0.5
        for h in range(4):
            pn = prot.get()
            S.mmg(pn[:], [(wq[0:KS[kt], kt, h * 192:h * 192 + 128], cn[0:KS[kt], kt, :]) for kt in range(4)], [wq, cn], [pn])
            evac_out(pn, 128, q_o.ap[h, 0:128, ts], scale=qs)
            px = prot.get(); pxs = prot.get()
            S.mmg(px[0:64, :], [(wq[0:KS[kt], kt, h * 192 + 128:h * 192 + 192], cn[0:KS[kt], kt, :]) for kt in range(4)], [wq, cn], [px])
            S.mmg(pxs[0:64, :], [(wqs[0:KS[kt], kt, h * 64:(h + 1) * 64], cn[0:KS[kt], kt, :]) for kt in range(4)], [wqs, cn], [pxs])
            xsb = tmpf.get()
            S.act(lambda e: e.copy(xsb[0:64, :], pxs[0:64, :]), [pxs], [xsb])
            d_ = ev.get()
            rope_apply(px[0:64, :], xsb[0:64, :], px, xsb, d_)
            S.dve(lambda e: e.tensor_scalar(d_[0:64, :], d_[0:64, :], qs, None, ALU.mult), [d_], [d_])
            S.dma("sp", None, d_, out_ap=q_o.ap[h, 128:192, ts], in_ap=d_[0:64, :], sem_of=d_, is_output=True)
        rms_rstd(S, [(xk, xk[:, :])], 512, ones, ps_ss, sqrot, rstd, 128)
        kn_ = kvn.get()
        t1 = tmpf.get()
        S.dve(lambda e: e.tensor_tensor(t1[:], xk[:], rstd[:], ALU.mult), [xk, rstd], [t1])
        S.dve(lambda e: e.tensor_scalar(kn_[:], t1[:], kvnw[:, 0:1], None, ALU.mult), [t1, kvnw], [kn_])
        for h in range(4):
            pk = prot.get()
            S.mm(pk[:], wkv[:, h * 256:h * 256 + 128], kn_[:], [wkv, kn_], [pk])
            evac_out(pk, 128, kn_o.ap[h, :, ts])
        for j in range(4):
            pv = prot.get()
            S.mm(pv[:], kn_[:, j * 128:(j + 1) * 128], wv[:], [kn_, wv], [pv])
            evac_out(pv, 128, v_o.ap[tt * 512 + j * 128:tt * 512 + (j + 1) * 128, :])
        d_ = ev.get()
        rope_apply(r_[:], rs_[:], r_, rs_, d_)
        S.dma("sp", None, d_, out_ap=kpe_o.ap[:, ts], in_ap=d_[0:64, :], sem_of=d_, is_output=True)
    S.finish()
    return nc


def build_mla(NK=16384, nkt=(32, 64, 96, 128), mask_from=(0, 32, 64, 96)):
    nc = new_nc()
    S = Sched(nc)
    NSL = len(nkt)
    NQ = NSL * 512
    NKT = NK // 128
    q_d = S.din("qT", [4, 192, NQ]); qi_d = S.din("qidx", [1, NQ])
    kn_d = S.din("knT", [4, 128, NK]); kpe_d = S.din("kpeT", [64, NK]); v_d = S.din("V", [NK, 512])
    ki_d = S.din("kidx", [128, NKT])
    ones_d = S.din("ones", [128, 128]); id_d = S.din("ident", [128, 128])
    o_d = S.dout("obT", [512, NQ])
    ones = S.sb([128, 128]); S.dma("sp", ones, ones_d)
    identb = S.sb([128, 128], BF16); S.dma("pool", identb, id_d)
    kidx = S.sb([128, NKT]); S.dma("sp", kidx, ki_d)
    qir = S.sb([1, NQ]); S.dma("sp", qir, qi_d)
    kpe = S.sb([64, NK], BF16); S.dma("pool", kpe, kpe_d)
    banks = [S.ps([128, 512], name="bank") for _ in range(8)]
    st_rot = Rot(banks[0:3]); acc_rot = Rot(banks[3:5]); sum_rot = Rot(banks[5:7]); misc = Rot(banks[7:8])
    onesb = S.sb([128, 128], BF16); S.dma("pool", onesb, ones_d)
    qib = S.sb([128, NSL, 512])
    for i in range(NSL):
        pq = misc.get()
        S.mm(pq[:], ones[0:1, :], qir[0:1, i * 512:(i + 1) * 512], [ones, qir], [pq])
        S.act(lambda e: e.copy(qib[:, i, :], pq[:]), [pq], [qib])
    knh = S.sb([128, NK], BF16, name="knh")
    vh = S.sb([128, NKT, 128], BF16, name="vh")
    qn = S.sb([128, NQ], BF16); qr = S.sb([64, NQ], BF16)
    prot = Rot([S.sb([128, 512], BF16, name="pT") for _ in range(3)])
    mrot = Rot([S.sb([128, 512], BF16, name="mask") for _ in range(2)])
    accs = Rot([S.sb([128, 512], name="accs") for _ in range(2)])
    orot = Rot([S.sb([128, 512], name="o") for _ in range(2)])
    rinv = S.sb([128, 512])
    for h in range(4):
        S.dma("pool", knh, kn_d, in_ap=kn_d.ap[h])
        for k0 in range(0, NKT, 32):
            k1 = min(NKT, k0 + 32)
            S.dma("pool", vh, v_d, out_ap=vh[:, k0:k1, :],
                  in_ap=v_d.ap[k0 * 128:k1 * 128, h * 128:(h + 1) * 128].rearrange("(kt p) d -> p kt d", p=128))
        S.dma("pool", qn, q_d, in_ap=q_d.ap[h, 0:128, :])
        S.dma("pool", qr, q_d, in_ap=q_d.ap[h, 128:192, :])
        for i in range(NSL):
            qs_ = slice(i * 512, (i + 1) * 512)
            acc = acc_rot.get(); asum = sum_rot.get()
            for kt in range(nkt[i]):
                ks = slice(kt * 128, (kt + 1) * 128)
                masked = kt >= mask_from[i]
                st = st_rot.get()
                pairs = [(knh[:, ks], qn[:, qs_]), (kpe[:, ks], qr[:, qs_])]
                rd = [knh, kpe, qn, qr]
                if masked:
                    mk = mrot.get()
                    S.dve(lambda e: e.tensor_scalar(mk[:], qib[:, i, :], kidx[:, kt:kt + 1], -30000.0, ALU.is_lt, ALU.mult),
                          [qib, kidx], [mk])
                    pairs.append((identb[:], mk[:]))
                    rd = rd + [identb, mk]
                S.mmg(st[:], pairs, rd, [st])
                pT = prot.get()
                S.act(lambda e: e.activation(pT[:], st[:], AF.Exp), [st], [pT])
                S.mm(acc[:], vh[:, kt, :], pT[:], [vh, pT], [acc], start=(kt == 0), stop=(kt == nkt[i] - 1))
                S.mm(asum[:], onesb[:], pT[:], [onesb, pT], [asum], start=(kt == 0), stop=(kt == nkt[i] - 1))
            S.dve(lambda e: e.reciprocal(rinv[:], asum[:]), [asum], [rinv])
            o_ = orot.get()
            S.dve(lambda e: e.tensor_tensor(o_[:], acc[:], rinv[:], ALU.mult), [acc, rinv], [o_])
            S.dma("sp", None, o_, out_ap=o_d.ap[h * 128:(h + 1) * 128, qs_], sem_of=o_, is_output=True)
    S.finish()
    return nc


def swa_bias_tables():
    W = 128
    qi = np.arange(W)[:, None]; kj = np.arange(2 * W)[None, :]
    dist = (qi + W - kj).astype(np.float32)
    valid = (dist >= 0) & (dist < W)
    slopes = (2.0 ** (-8.0 * (np.arange(8, dtype=np.float32) + 1.0) / 8)).astype(np.float32)
    b = np.where(valid[:, None, :], -slopes[None, :, None] * dist[:, None, :], -30000.0).astype(np.float32)
    bf = b.copy(); bf[:, :, :W] = -30000.0
    return np.ascontiguousarray(b), np.ascontiguousarray(bf)


def build_swa(NT=2048):
    nc = new_nc()
    S = Sched(nc)
    NB = NT // 128
    q_d = S.din("qT", [512, NT]); k_d = S.din("kT", [128, 128 + NT]); v_d = S.din("V", [128 + NT, 128])
    b_d = S.din("bias", [128, 8, 256]); bf_d = S.din("bias_first", [128, 8, 256])
    sk_d = S.din("sinkc", [128, 8]); one_d = S.din("onec", [128, 1]); id_d = S.din("ident", [128, 128])
    o_d = S.dout("ocT", [512, NT])
    bias = S.sb([128, 8, 256]); S.dma("sp", bias, b_d)
    biasf = S.sb([128, 8, 256]); S.dma("sp", biasf, bf_d)
    sinkc = S.sb([128, 8]); S.dma("sp", sinkc, sk_d)
    onec = S.sb([128, 1]); S.dma("sp", onec, one_d)
    identb = S.sb([128, 128], BF16); S.dma("pool", identb, id_d)
    q64 = S.sb([64, 8, NT], BF16); S.dma("pool", q64, q_d, in_ap=q_d.ap.rearrange("(h d) n -> d h n", d=64))
    k64 = S.sb([64, 2, 128 + NT], BF16); S.dma("pool", k64, k_d, in_ap=k_d.ap.rearrange("(h d) n -> d h n", d=64))
    vsb = S.sb([128, NB + 1, 128], BF16); S.dma("pool", vsb, v_d, in_ap=v_d.ap.rearrange("(b p) d -> p b d", p=128))
    sp_rot = Rot([S.ps([128, 512], name="sps") for _ in range(2)])
    pt_rot = Rot([S.ps([128, 256], BF16, name="ptp") for _ in range(2)])
    op_rot = Rot([S.ps([128, 512], name="ops") for _ in range(2)])
    s_rot = Rot([S.sb([128, 256], name="s") for _ in range(2)])
    p_rot = Rot([S.sb([128, 256], name="p") for _ in range(2)])
    pn_rot = Rot([S.sb([128, 256], BF16, name="pn") for _ in range(2)])
    pnt_rot = Rot([S.sb([128, 256], BF16, name="pnt") for _ in range(2)])
    c_rot = Rot([S.sb([128, 8], name="col") for _ in range(4)])
    ost = Rot([S.sb([64, 8, 128], name="ost") for _ in range(2)])
    for n in range(NB):
        bt = biasf if n == 0 else bias
        og = ost.get()
        for h in range(8):
            kv = h // 4
            sp_ = sp_rot.get()
            S.mm(sp_[:, 0:256], q64[:, h, n * 128:(n + 1) * 128], k64[:, kv, n * 128:n * 128 + 256], [q64, k64], [sp_])
            s_ = s_rot.get(); c_ = c_rot.get()
            S.dve(lambda e: e.scalar_tensor_tensor(s_[:], sp_[:, 0:256], 0.125, bt[:, h, :], ALU.mult, ALU.add), [sp_, bt], [s_])
            S.dve(lambda e: e.tensor_reduce(c_[:, 0:1], s_[:], AX.X, ALU.max), [s_], [c_])
            S.dve(lambda e: e.tensor_tensor(c_[:, 0:1], c_[:, 0:1], sinkc[:, h:h + 1], ALU.max), [c_, sinkc], [c_])
            S.dve(lambda e: e.tensor_scalar(c_[:, 1:2], c_[:, 0:1], -1.0, None, ALU.mult), [c_], [c_])
            S.dve(lambda e: e.memset(c_[:, 2:3], 0.0), [], [c_])
            p_ = p_rot.get()
            S.act(lambda e: e.activation(p_[:], s_[:], AF.Exp, bias=c_[:, 1:2], scale=onec[:, 0:1], accum_out=c_[:, 2:3]),
                  [s_, c_, onec], [p_, c_])
            S.act(lambda e: e.activation(c_[:, 3:4], sinkc[:, h:h + 1], AF.Exp, bias=c_[:, 1:2], scale=onec[:, 0:1]),
                  [sinkc, c_, onec], [c_])
            S.dve(lambda e: e.tensor_tensor(c_[:, 4:5], c_[:, 2:3], c_[:, 3:4], ALU.add), [c_], [c_])
            S.dve(lambda e: e.reciprocal(c_[:, 4:5], c_[:, 4:5]), [c_], [c_])
            pn = pn_rot.get()
            S.dve(lambda e: e.tensor_scalar(pn[:], p_[:], c_[:, 4:5], None, ALU.mult), [p_, c_], [pn])
            ptp = pt_rot.get()
            S.pe(lambda e: e.transpose(ptp[:, 0:128], pn[:, 0:128], identb[:]), [pn, identb], [ptp])
            S.pe(lambda e: e.transpose(ptp[:, 128:256], pn[:, 128:256], identb[:]), [pn, identb], [ptp])
            pnt = pnt_rot.get()
            S.act(lambda e: e.copy(pnt[:], ptp[:]), [ptp], [pnt])
            ops = op_rot.get()
            S.mmg(ops[0:64, 0:128], [(vsb[:, n, kv * 64:(kv + 1) * 64], pnt[:, 0:128]),
                                     (vsb[:, n + 1, kv * 64:(kv + 1) * 64], pnt[:, 128:256])], [vsb, pnt], [ops])
            S.act(lambda e: e.copy(og[:, h, :], ops[0:64, 0:128]), [ops], [og])
        S.dma("sp", None, og, out_ap=o_d.ap[:, n * 128:(n + 1) * 128].rearrange("(h d) n -> d h n", d=64), sem_of=og, is_output=True)
    S.finish()
    return nc


def build_outproj(NT=2048):
    nc = new_nc()
    S = Sched(nc)
    NQ = NT // 512
    cat_d = S.din("catT", [D, NT]); x_d = S.din("xT", [D, NT]); w_d = S.din("w_out", [D, D])
    vec_d = S.din("vecs", [128, 5, KT])
    ones_d = S.din("ones", [128, 128])
    x1_o = S.dout("x1T", [D, NT]); h2_o = S.dout("h2T", [D, NT])
    ones = S.sb([128, 128]); S.dma("sp", ones, ones_d)
    vec = S.sb([128, 5, KT]); S.dma("sp", vec, vec_d)
    pg = S.sb([128, KT]); gp2 = S.sb([128, KT]); sh2 = S.sb([128, KT])
    S.dve(lambda e: e.tensor_tensor(pg[:], vec[:, 0, :], vec[:, 1, :], ALU.mult), [vec], [pg])
    S.dve(lambda e: e.tensor_scalar(gp2[:], vec[:, 3, :], 1.0, None, ALU.add), [vec], [gp2])
    S.dve(lambda e: e.tensor_tensor(gp2[:], gp2[:], vec[:, 2, :], ALU.mult), [gp2, vec], [gp2])
    S.dve(lambda e: e.tensor_copy(sh2[:], vec[:, 4, :]), [vec], [sh2])
    cc = S.sb([128, KT, 512], BF16, name="cc")
    wrot = Rot([S.sb([128, KT, 512], BF16, name="wsl") for _ in range(2)])
    mix = S.sb([128, KT, 512], name="mix")
    xs = S.sb([128, KT, 512], name="xs")
    sqrot = Rot([S.sb([128, 512], name="sq") for _ in range(3)])
    tmprot = Rot([S.sb([128, 512], name="tmp") for _ in range(3)])
    rstd = S.sb([128, 512])
    ps_ss = S.ps([128, 512])
    psrot = Rot([S.ps([128, 512], name="pso") for _ in range(4)])
    for tq in range(NQ):
        ts = slice(tq * 512, (tq + 1) * 512)
        S.dma("pool", cc, cat_d, in_ap=cat_d.ap[:, ts].rearrange("(kt p) n -> p kt n", p=128))
        S.dma("sp", xs, x_d, in_ap=x_d.ap[:, ts].rearrange("(kt p) n -> p kt n", p=128))
        for sl in range(4):
            ws = wrot.get()
            S.dma("pool", ws, w_d, in_ap=w_d.ap[:, sl * 512:(sl + 1) * 512].rearrange("(kt p) c -> p kt c", p=128))
            for j in range(4):
                ot = sl * 4 + j
                pp_ = psrot.get()
                S.mmg(pp_[:], [(ws[:, kt, j * 128:(j + 1) * 128], cc[:, kt, :]) for kt in range(KT)], [ws, cc], [pp_])
                S.act(lambda e: e.copy(mix[:, ot, :], pp_[:]), [pp_], [mix])
        rms_rstd(S, [(mix, mix[:, kt, :]) for kt in range(KT)], 512, ones, ps_ss, sqrot, rstd, D)
        for kt in range(KT):
            t1 = tmprot.get()
            S.dve(lambda e: e.tensor_tensor(t1[:], mix[:, kt, :], rstd[:], ALU.mult), [mix, rstd], [t1])
            S.dve(lambda e: e.scalar_tensor_tensor(xs[:, kt, :], t1[:], pg[:, kt:kt + 1], xs[:, kt, :], ALU.mult, ALU.add),
                  [t1, pg, xs], [xs])
        S.dma("sp", None, xs, out_ap=x1_o.ap[:, ts].rearrange("(kt p) n -> p kt n", p=128), sem_of=xs, is_output=True)
        modulated_norm(S, xs, 512, gp2, sh2, ones, ps_ss, sqrot, rstd, tmprot, [mix[:, kt, :] for kt in range(KT)], [mix] * KT)
        S.dma("sp", None, mix, out_ap=h2_o.ap[:, ts].rearrange("(kt p) n -> p kt n", p=128), sem_of=mix, is_output=True)
    S.finish()
    return nc


NFT = D_FF // 128


def build_ffn(NT=2048):
    nc = new_nc()
    S = Sched(nc)
    NQ = NT // 512
    h2_d = S.din("h2T", [D, 2 + NT]); x1_d = S.din("x1T", [D, NT])
    wu_d = S.din("w_up", [D, 2 * D_FF]); wd_d = S.din("w_down", [D_FF, D])
    cw_d = S.din("cw", [128, 2 * NFT, 3]); cb_d = S.din("cb", [128, 2 * NFT])
    vec_d = S.din("vecs", [128, 2, KT])
    ones_d = S.din("ones", [128, 128])
    o_d = S.dout("x2T", [D, NT])
    ones = S.sb([128, 128]); S.dma("sp", ones, ones_d)
    vec = S.sb([128, 2, KT]); S.dma("sp", vec, vec_d)
    cw = S.sb([128, 2 * NFT, 3]); S.dma("sp", cw, cw_d)
    cb = S.sb([128, 2 * NFT]); S.dma("sp", cb, cb_d)
    pg = S.sb([128, KT])
    S.dve(lambda e: e.tensor_tensor(pg[:], vec[:, 0, :], vec[:, 1, :], ALU.mult), [vec], [pg])
    h2q = S.sb([128, KT, 514], BF16, name="h2q")
    actT = S.sb([128, NFT, 512], BF16, name="actT")
    wgu = [Rot([S.sb([128, KT, 128], BF16, name="wgu") for _ in range(2)]) for _ in range(2)]
    wdr = Rot([S.sb([128, NFT, 128], BF16, name="wd") for _ in range(2)])
    y = S.sb([128, KT, 512], name="y")
    urot = Rot([S.sb([128, 514], name="u") for _ in range(4)])
    crot = Rot([S.sb([128, 512], name="c") for _ in range(4)])
    x1r = Rot([S.sb([128, 512], name="x1") for _ in range(3)])
    sqrot = Rot([S.sb([128, 512], name="sq") for _ in range(3)])
    tmprot = Rot([S.sb([128, 512], name="tmp") for _ in range(3)])
    rstd = S.sb([128, 512])
    ps_ss = S.ps([128, 512])
    psrot = Rot([S.ps([128, 512], name="pso") for _ in range(4)])
    phrot = Rot([S.ps([128, 512], name="psh") for _ in range(2)])
    for tq in range(NQ):
        ts = slice(tq * 512, (tq + 1) * 512)
        S.dma("pool", h2q, h2_d, in_ap=h2_d.ap[:, tq * 512:tq * 512 + 514].rearrange("(kt p) n -> p kt n", p=128))
        for ft in range(NFT):
            cs_ = []
            for gi in range(2):
                f = gi * NFT + ft
                w = wgu[gi].get()
                S.dma("pool", w, wu_d, in_ap=wu_d.ap[:, f * 128:(f + 1) * 128].rearrange("(kt p) c -> p kt c", p=128))
                pm = psrot.get(); ph = phrot.get()
                S.mmg(pm[:], [(w[:, kt, :], h2q[:, kt, 2:514]) for kt in range(KT)], [w, h2q], [pm])
                S.mmg(ph[:, 0:2], [(w[:, kt, :], h2q[:, kt, 0:2]) for kt in range(KT)], [w, h2q], [ph])
                u = urot.get()
                S.act(lambda e: e.copy(u[:, 2:514], pm[:]), [pm], [u])
                S.dve(lambda e: e.tensor_copy(u[:, 0:2], ph[:, 0:2]), [ph], [u])
                c = crot.get()
                S.dve(lambda e: e.tensor_scalar(c[:], u[:, 2:514], cw[:, f, 2:3], cb[:, f:f + 1], ALU.mult, ALU.add), [u, cw, cb], [c])
                S.dve(lambda e: e.scalar_tensor_tensor(c[:], u[:, 1:513], cw[:, f, 1:2], c[:], ALU.mult, ALU.add), [u, cw, c], [c])
                S.dve(lambda e: e.scalar_tensor_tensor(c[:], u[:, 0:512], cw[:, f, 0:1], c[:], ALU.mult, ALU.add), [u, cw, c], [c])
                cs_.append(c)
            cg, cu = cs_
            S.act(lambda e: e.activation(cg[:], cg[:], AF.Gelu_apprx_tanh), [cg], [cg])
            S.dve(lambda e: e.tensor_tensor(actT[:, ft, :], cg[:], cu[:], ALU.mult), [cg, cu], [actT])
        for ot in range(KT):
            wd = wdr.get()
            S.dma("pool", wd, wd_d, in_ap=wd_d.ap[:, ot * 128:(ot + 1) * 128].rearrange("(ft p) c -> p ft c", p=128))
            pp_ = psrot.get()
            S.mmg(pp_[:], [(wd[:, ft, :], actT[:, ft, :]) for ft in range(NFT)], [wd, actT], [pp_])
            S.act(lambda e: e.copy(y[:, ot, :], pp_[:]), [pp_], [y])
        rms_rstd(S, [(y, y[:, kt, :]) for kt in range(KT)], 512, ones, ps_ss, sqrot, rstd, D)
        for kt in range(KT):
            t1 = tmprot.get(); x1 = x1r.get()
            S.dma("sp", x1, x1_d, in_ap=x1_d.ap[kt * 128:(kt + 1) * 128, ts])
            S.dve(lambda e: e.tensor_tensor(t1[:], y[:, kt, :], rstd[:], ALU.mult), [y, rstd], [t1])
            S.dve(lambda e: e.scalar_tensor_tensor(x1[:], t1[:], pg[:, kt:kt + 1], x1[:], ALU.mult, ALU.add), [t1, pg, x1], [x1])
            S.dma("sp", None, x1, out_ap=o_d.ap[kt * 128:(kt + 1) * 128, ts], sem_of=x1, is_output=True)
    S.finish()
    return nc


_NC_CACHE = {}


def _prog(name, builder, *args):
    return builder(*args)


def _col(v):
    return np.ascontiguousarray(np.asarray(v, np.float32).reshape(KT, 128).T)


def _cols(vs):
    return np.ascontiguousarray(np.stack([_col(v) for v in vs], 1))


def _run(nc, in_maps):
    res = run_bass_kernel_spmd(nc, in_maps, core_ids=list(range(NCORES)))
    return res.results


def _c(a):
    return np.ascontiguousarray(a)


def kernel(x, c, positions, ada_w, ada_b, mix_pre_norm, mix_post_norm, w_in, w_out,
           gdn_conv, gdn_a_log, gdn_dt_bias, gdn_norm, mla_q_norm, mla_w_uq, mla_kv_norm,
           mla_w_ukv, swa_sinks, ffn_pre_norm, ffn_post_norm, ffn_w_up, ffn_conv, ffn_conv_b,
           ffn_w_down):
    f32 = np.float32
    x = np.asarray(x, f32)
    NS = x.shape[1]
    TPC = NS // NCORES
    ones = np.ones((128, 128), f32)
    ident = np.eye(128, dtype=f32)
    xT = _c(x[0].T)
    positions = np.asarray(positions).astype(np.int32)
    tok = [slice(cc * TPC, (cc + 1) * TPC) for cc in range(NCORES)]

    ims = []
    for cc in range(NCORES):
        sl = slice(cc * 1536, (cc + 1) * 1536)
        ims.append({"c_col": _col(np.asarray(c, f32)[0]),
                    "ada_w0": _c(np.asarray(ada_w[0], f32)[:, sl]), "ada_w1": _c(np.asarray(ada_w[1], f32)[:, sl]),
                    "ada_b0": _c(np.asarray(ada_b[0], f32)[None, sl]), "ada_b1": _c(np.asarray(ada_b[1], f32)[None, sl])})
    r = _run(build_mod(), ims)
    mods = [np.concatenate([r[cc][f"mod{l}"][0] for cc in range(NCORES)]).reshape(6, D) for l in range(2)]

    inv = (10000.0 ** (-np.arange(32, dtype=f32) / 32)).astype(f32)
    inv2 = np.concatenate([inv, inv])[None, :].astype(f32)
    sgn = np.ones((64, 1), f32); sgn[:32] = -1
    swa_b, swa_bf = swa_bias_tables()
    gcst = gdn_consts()
    kidx = _c(np.arange(NS, dtype=f32).reshape(-1, 128).T)
    for l in range(2):
        shift1, scale1, gate1, shift2, scale2, gate2 = mods[l]
        w = np.asarray(w_in[l], f32)
        vec = _cols([mix_pre_norm[l], scale1, shift1])
        r = _run(build_inproj(TPC, D_IN), [{"xT": _c(xT[:, tok[cc]]), "w": w, "vecs": vec, "ones": ones} for cc in range(NCORES)])
        projT = np.concatenate([r[cc]["projT"] for cc in range(NCORES)], axis=1)
        del r
        conv = np.asarray(gdn_conv[l], f32)
        ims = []
        for h in range(8):
            im = dict(gcst)
            im["qT"] = _c(projT[h * 128:(h + 1) * 128]); im["kT"] = _c(projT[1024 + h * 128:1024 + (h + 1) * 128])
            im["vT"] = _c(projT[2048 + h * 128:2048 + (h + 1) * 128]); im["zT"] = _c(projT[3072 + h * 128:3072 + (h + 1) * 128])
            im["a_row"] = _c(projT[4096 + h][None, :]); im["b_row"] = _c(projT[4104 + h][None, :])
            im["cw"] = _c(np.concatenate([conv[:, h * 128:(h + 1) * 128].T, conv[:, 1024 + h * 128:1024 + (h + 1) * 128].T,
                                          conv[:, 2048 + h * 128:2048 + (h + 1) * 128].T], axis=1))
            par = np.empty((128, 3), f32)
            par[:, 0] = np.asarray(gdn_a_log[l], f32)[h]; par[:, 1] = np.asarray(gdn_dt_bias[l], f32)[h]
            par[:, 2] = np.asarray(gdn_norm[l], f32)
            im["par"] = par
            ims.append(im)
        r = _run(build_gdn(NS), ims)
        oaT = np.concatenate([r[h]["oT"] for h in range(8)], axis=0)
        del r, ims
        wuq = np.asarray(mla_w_uq[l], f32); wukv = np.asarray(mla_w_ukv[l], f32)
        qnw = np.asarray(mla_q_norm[l], f32)
        qn4 = np.zeros((128, 4), f32)
        for kt in range(4):
            n = min(128, 448 - kt * 128)
            qn4[:n, kt] = qnw[kt * 128:kt * 128 + n]
        wsw = _c(np.concatenate([np.concatenate([wuq[:, h * 192 + 160:h * 192 + 192], wuq[:, h * 192 + 128:h * 192 + 160]], 1)
                                 for h in range(4)], 1))
        ims = []
        for cc in range(NCORES):
            kr = projT[4688:4752, tok[cc]]
            ims.append({"cqT": _c(projT[4112:4560, tok[cc]]), "ckvT": _c(projT[4560:4688, tok[cc]]), "krT": _c(kr),
                        "krsT": _c(np.concatenate([kr[32:], kr[:32]], 0)), "pos": _c(positions[0:1, tok[cc]]),
                        "inv2": inv2, "qnw": qn4, "kvnw": _c(np.asarray(mla_kv_norm[l], f32).reshape(128, 1)),
                        "w_uq": wuq, "w_uq_sw": wsw, "w_ukv": wukv, "ones": ones, "sgn": sgn})
        r = _run(build_mlaprep(TPC), ims)
        qTf = np.concatenate([r[cc]["qT"] for cc in range(NCORES)], axis=2)
        knTf = np.concatenate([r[cc]["knT"] for cc in range(NCORES)], axis=2)
        kpeTf = np.concatenate([r[cc]["kpeT"] for cc in range(NCORES)], axis=1)
        Vf = np.concatenate([r[cc]["V"] for cc in range(NCORES)], axis=0)
        del r, ims
        nsl = NS // 512 // NCORES
        ims = []
        for cc in range(NCORES):
            tiles = [cc + NCORES * i for i in range(nsl)]
            qsel = np.concatenate([np.arange(t * 512, (t + 1) * 512) for t in tiles])
            ims.append({"qT": _c(qTf[:, :, qsel]), "qidx": _c(qsel.astype(f32)[None, :]), "knT": knTf, "kpeT": kpeTf, "V": Vf,
                        "kidx": kidx, "ones": ones, "ident": ident})
        nkt = tuple(4 * NCORES * (i + 1) for i in range(nsl)); mfrom = tuple(4 * NCORES * i for i in range(nsl))
        r = _run(build_mla(NS, nkt, mfrom), ims)
        obT = np.empty((512, NS), f32)
        for cc in range(NCORES):
            for i in range(nsl):
                t = cc + NCORES * i
                obT[:, t * 512:(t + 1) * 512] = r[cc]["obT"][:, i * 512:(i + 1) * 512]
        del r, ims, qTf, knTf, kpeTf, Vf
        kfull = np.concatenate([np.zeros((128, 128), f32), projT[5264:5392]], axis=1)
        vfull = np.concatenate([np.zeros((128, 128), f32), projT[5392:5520]], axis=1)
        sinkc = _c(np.broadcast_to(np.asarray(swa_sinks[l], f32)[None, :], (128, 8)))
        ims = []
        for cc in range(NCORES):
            ks = slice(cc * TPC, cc * TPC + TPC + 128)
            ims.append({"qT": _c(projT[4752:5264, tok[cc]]), "kT": _c(kfull[:, ks]), "V": _c(vfull[:, ks].T),
                        "bias": swa_b, "bias_first": swa_bf if cc == 0 else swa_b, "sinkc": sinkc,
                        "onec": np.ones((128, 1), f32), "ident": ident})
        r = _run(build_swa(TPC), ims)
        ocT = np.concatenate([r[cc]["ocT"] for cc in range(NCORES)], axis=1)
        del r, ims, projT, kfull, vfull
        catT = np.concatenate([oaT, obT, ocT], axis=0)
        del oaT, obT, ocT
        vec = _cols([mix_post_norm[l], gate1, ffn_pre_norm[l], scale2, shift2])
        wo = np.asarray(w_out[l], f32)
        r = _run(build_outproj(TPC), [{"catT": _c(catT[:, tok[cc]]), "xT": _c(xT[:, tok[cc]]), "w_out": wo, "vecs": vec, "ones": ones}
                                      for cc in range(NCORES)])
        x1T = np.concatenate([r[cc]["x1T"] for cc in range(NCORES)], axis=1)
        h2T = np.concatenate([np.zeros((D, 2), f32)] + [r[cc]["h2T"] for cc in range(NCORES)], axis=1)
        del r, catT
        wu = np.asarray(ffn_w_up[l], f32); wd = np.asarray(ffn_w_down[l], f32)
        cwt = _c(np.asarray(ffn_conv[l], f32).T.reshape(2 * NFT, 128, 3).transpose(1, 0, 2))
        cbt = _c(np.asarray(ffn_conv_b[l], f32).reshape(2 * NFT, 128).T)
        vec = _cols([ffn_post_norm[l], gate2])
        r = _run(build_ffn(TPC), [{"h2T": _c(h2T[:, cc * TPC:cc * TPC + TPC + 2]), "x1T": _c(x1T[:, tok[cc]]), "w_up": wu, "w_down": wd,
                                   "cw": cwt, "cb": cbt, "vecs": vec, "ones": ones} for cc in range(NCORES)])
        xT = np.concatenate([r[cc]["x2T"] for cc in range(NCORES)], axis=1)
        del r, x1T, h2T
    return _c(xT.T)[None].astype(f32)
```

```python
import contextlib
import numpy as np
import concourse.bass as bass
import concourse.mybir as mybir
from concourse.bass_utils import run_bass_kernel_spmd

F32 = mybir.dt.float32
BF16 = mybir.dt.bfloat16
I32 = mybir.dt.int32
ALU = mybir.AluOpType
AF = mybir.ActivationFunctionType
AX = mybir.AxisListType

NCORES = 8
D = 2048
KT = 16
EPS = 1e-6
D_IN = 5520
D_FF = 5632
SEM_ROT = 2000


class T:
    __slots__ = ("ap", "w", "r", "dsem", "dcnt", "base")

    def __init__(self, ap, base=None):
        self.base = base
        if not isinstance(ap, bass.AP):
            ap = ap.ap()
        self.ap = ap
        self.w = None
        self.r = {}
        self.dsem = None
        self.dcnt = 0

    def __getitem__(self, idx):
        return self.ap[idx]


class _Rec:
    def __init__(self):
        self.calls = []

    def __getattr__(self, name):
        def f(*a, **k):
            self.calls.append((name, a, k))
            return None
        return f


class Sched:
    ENG = ("pe", "act", "dve", "pool", "sp")

    def __init__(self, nc):
        self.nc = nc
        self.es = contextlib.ExitStack()
        self.ops = {e: [] for e in self.ENG}
        self.cur = {}
        self.cnt = {e: 0 for e in self.ENG}
        self.waited = {e: {} for e in self.ENG}
        self.nsem = 0
        self.out_tokens = []
        self.nname = 0
        for e in self.ENG:
            self.cur[e] = self.new_sem("p_" + e)

    def new_sem(self, name):
        self.nsem += 1
        return self.es.enter_context(self.nc.semaphore(f"{name}_{self.nsem}"))

    def sb(self, shape, dtype=F32, name=None):
        self.nname += 1
        return T(self.es.enter_context(self.nc.sbuf_tensor(f"{name or 'sb'}_{self.nname}", list(shape), dtype)))

    def ps(self, shape, dtype=F32, name=None):
        self.nname += 1
        return T(self.es.enter_context(self.nc.psum_tensor(f"{name or 'ps'}_{self.nname}", list(shape), dtype)))

    def din(self, name, shape, dtype=F32):
        return T(self.nc.dram_tensor(name, list(shape), dtype, kind="ExternalInput").ap())

    def dout(self, name, shape, dtype=F32):
        return T(self.nc.dram_tensor(name, list(shape), dtype, kind="ExternalOutput").ap())

    def _deps(self, e, reads, writes):
        need = {}

        def add(tok):
            if tok is None:
                return
            s, v = tok
            if need.get(id(s), (None, 0))[1] < v:
                need[id(s)] = (s, v)

        reads = [t.base or t for t in reads]
        writes = [t.base or t for t in writes]
        for t in reads:
            add(t.w)
        for t in writes:
            add(t.w)
            for s_v in t.r.values():
                add(s_v)
        waits = []
        for k, (s, v) in need.items():
            if s is self.cur[e] and e == "pe":
                continue
            if self.waited[e].get(k, 0) >= v:
                continue
            self.waited[e][k] = v
            waits.append((s, v))
        return waits

    def _commit(self, tok, reads, writes):
        s, v = tok
        reads = [t.base or t for t in reads]
        writes = [t.base or t for t in writes]
        for t in reads:
            old = t.r.get(id(s))
            if old is None or old[1] < v:
                t.r[id(s)] = (s, v)
        for t in writes:
            t.w = tok
            t.r = {}

    def op(self, e, fn, reads=(), writes=()):
        rec = _Rec()
        fn(rec)
        return self.emit(e, rec.calls, reads, writes)

    def emit(self, e, calls, reads=(), writes=()):
        def fn(eng, calls=calls):
            ins = None
            for name, a, k in calls:
                ins = getattr(eng, name)(*a, **k)
            return ins
        waits = self._deps(e, reads, writes)
        if self.cnt[e] >= SEM_ROT:
            self.cur[e] = self.new_sem("p_" + e)
            self.cnt[e] = 0
        self.cnt[e] += 1
        tok = (self.cur[e], self.cnt[e])
        self.ops[e].append((waits, fn, tok[0], 1))
        self._commit(tok, reads, writes)
        return tok

    def pe(self, fn, reads=(), writes=()):
        return self.op("pe", fn, reads, writes)

    def act(self, fn, reads=(), writes=()):
        return self.op("act", fn, reads, writes)

    def dve(self, fn, reads=(), writes=()):
        return self.op("dve", fn, reads, writes)

    def pool(self, fn, reads=(), writes=()):
        return self.op("pool", fn, reads, writes)

    def mm(self, out_ap, lhsT_ap, rhs_ap, reads, writes, start=True, stop=True):
        return self.op("pe", lambda e: e.matmul(out_ap, lhsT_ap, rhs_ap, start=start, stop=stop), reads, writes)

    def mmg(self, out_ap, pairs, reads, writes):
        def fn(e):
            n = len(pairs)
            ins = None
            for i, (l, r) in enumerate(pairs):
                ins = e.matmul(out_ap, l, r, start=(i == 0), stop=(i == n - 1))
            return ins
        return self.op("pe", fn, reads, writes)

    def dma(self, e, out_t, in_t, out_ap=None, in_ap=None, sem_of=None, is_output=False):
        rl = [in_t] if in_t is not None else []
        wl = [out_t] if out_t is not None else []
        waits = self._deps(e, rl, wl)
        so = sem_of or out_t
        if so.dsem is None:
            so.dsem = self.new_sem("d")
        so.dcnt += 16
        tok = (so.dsem, so.dcnt)
        oa = out_ap if out_ap is not None else out_t.ap
        ia = in_ap if in_ap is not None else in_t.ap
        self.ops[e].append((waits, lambda eng: eng.dma_start(out=oa, in_=ia), tok[0], 16))
        self._commit(tok, rl, wl)
        if is_output:
            self.out_tokens.append(tok)
        return tok

    def finish(self):
        nc = self.nc
        fin = {}
        for s, v in self.out_tokens:
            if fin.get(id(s), (None, 0))[1] < v:
                fin[id(s)] = (s, v)
        final_waits = list(fin.values())
        ops = self.ops
        emap = {"pe": "tensor", "act": "scalar", "dve": "vector", "pool": "gpsimd", "sp": "sync"}

        def body(e):
            def f(eng):
                for waits, fn, sem, inc in ops[e]:
                    for s, v in waits:
                        eng.wait_ge(s, v)
                    fn(eng).then_inc(sem, inc)
                if e == "sp":
                    for s, v in final_waits:
                        eng.wait_ge(s, v)
            return f

        with nc.Block() as block:
            for e in self.ENG:
                getattr(block, emap[e])(body(e))
        self.es.close()


class Deferred:
    def __init__(self, S):
        self.S = S
        self.items = []

    def op(self, e, fn, reads=(), writes=()):
        rec = _Rec()
        fn(rec)
        self.items.append(("op", e, rec.calls, list(reads), list(writes)))

    def pe(self, fn, reads=(), writes=()):
        self.op("pe", fn, reads, writes)

    def act(self, fn, reads=(), writes=()):
        self.op("act", fn, reads, writes)

    def dve(self, fn, reads=(), writes=()):
        self.op("dve", fn, reads, writes)

    def pool(self, fn, reads=(), writes=()):
        self.op("pool", fn, reads, writes)

    def mm(self, out_ap, lhsT_ap, rhs_ap, reads, writes, start=True, stop=True):
        self.op("pe", lambda e: e.matmul(out_ap, lhsT_ap, rhs_ap, start=start, stop=stop), reads, writes)

    def dma(self, *a, **k):
        self.items.append(("dma", a, k))


def interleave(S, ds):
    ds = [d for d in ds if d is not None]
    pos = [0] * len(ds)
    live = True
    while live:
        live = False
        for i, d in enumerate(ds):
            if pos[i] < len(d.items):
                it = d.items[pos[i]]
                pos[i] += 1
                live = True
                if it[0] == "op":
                    S.emit(it[1], it[2], it[3], it[4])
                else:
                    S.dma(*it[1], **it[2])


def new_nc():
    return bass.Bass("TRN2", target_bir_lowering=False)


class Rot:
    def __init__(self, items):
        self.items = items
        self.i = 0

    def get(self):
        t = self.items[self.i % len(self.items)]
        self.i += 1
        return t


def rms_rstd(S, xs_list, n, ones, ps_ss, sqrot, rstd, dim):
    m = len(xs_list)
    for i, (t, ap) in enumerate(xs_list):
        p = ap.shape[0]
        sq = sqrot.get()
        S.act(lambda e, sq=sq, ap=ap, p=p: e.activation(sq[0:p, 0:n], ap, AF.Square), [t], [sq])
        S.mm(ps_ss[:, 0:n], ones[0:p, :], sq[0:p, 0:n], [ones, sq], [ps_ss], start=(i == 0), stop=(i == m - 1))
    S.dve(lambda e: e.tensor_scalar(rstd[:, 0:n], ps_ss[:, 0:n], 1.0 / dim, EPS, ALU.mult, ALU.add), [ps_ss], [rstd])
    S.act(lambda e: e.activation(rstd[:, 0:n], rstd[:, 0:n], AF.Sqrt), [rstd], [rstd])
    S.dve(lambda e: e.reciprocal(rstd[:, 0:n], rstd[:, 0:n]), [rstd], [rstd])


def build_mod():
    nc = new_nc()
    S = Sched(nc)
    NC = 1536
    c_d = S.din("c_col", [128, KT])
    w_d = [S.din(f"ada_w{l}", [D, NC]) for l in range(2)]
    b_d = [S.din(f"ada_b{l}", [1, NC]) for l in range(2)]
    o_d = [S.dout(f"mod{l}", [1, NC]) for l in range(2)]
    cc = S.sb([128, KT])
    S.dma("sp", cc, c_d)
    S.act(lambda e: e.activation(cc[:], cc[:], AF.Silu), [cc], [cc])
    wsb = [S.sb([128, KT, NC], name="adaw") for _ in range(1)]
    for l in range(2):
        w = wsb[0]
        S.dma("sp", w, w_d[l], in_ap=w_d[l].ap.rearrange("(kt p) c -> p kt c", p=128))
        bsb = S.sb([1, NC])
        S.dma("sp", bsb, b_d[l])
        res = S.sb([1, NC])
        for j in range(NC // 512):
            pp = S.ps([1, 512])
            S.mmg(pp[:], [(cc[:, kt:kt + 1], w[:, kt, j * 512:(j + 1) * 512]) for kt in range(KT)], [cc, w], [pp])
            S.dve(lambda e, pp=pp, j=j, res=res, bsb=bsb: e.tensor_tensor(
                res[:, j * 512:(j + 1) * 512], pp[:], bsb[:, j * 512:(j + 1) * 512], ALU.add), [pp, bsb], [res])
        S.dma("sp", o_d[l], res, is_output=True)
    S.finish()
    return nc


def modulated_norm(S, xs, n, gp, sh, ones, ps_ss, sqrot, rstd, tmprot, out_aps, out_ts):
    rms_rstd(S, [(xs, xs[:, kt, 0:n]) for kt in range(KT)], n, ones, ps_ss, sqrot, rstd, D)
    for kt in range(KT):
        tmp = tmprot.get()
        S.dve(lambda e, tmp=tmp, kt=kt: e.tensor_tensor(tmp[:, 0:n], xs[:, kt, 0:n], rstd[:, 0:n], ALU.mult),
              [xs, rstd], [tmp])
        S.act(lambda e, tmp=tmp, kt=kt: e.activation(out_aps[kt], tmp[:, 0:n], AF.Identity,
                                                     bias=sh[:, kt:kt + 1], scale=gp[:, kt:kt + 1]),
              [tmp, gp, sh], [out_ts[kt]])


def build_inproj(NT=2048, NO=D_IN):
    nc = new_nc()
    S = Sched(nc)
    NTT = NT // 512
    x_d = S.din("xT", [D, NT])
    w_d = S.din("w", [D, NO])
    vec_d = S.din("vecs", [128, 3, KT])
    ones_d = S.din("ones", [128, 128])
    o_d = S.dout("projT", [NO, NT])
    ones = S.sb([128, 128]); S.dma("sp", ones, ones_d)
    vec = S.sb([128, 3, KT]); S.dma("sp", vec, vec_d)
    gp = S.sb([128, KT]); sh = S.sb([128, KT])
    S.dve(lambda e: e.tensor_scalar(gp[:], vec[:, 1, :], 1.0, None, ALU.add), [vec], [gp])
    S.dve(lambda e: e.tensor_tensor(gp[:], gp[:], vec[:, 0, :], ALU.mult), [gp, vec], [gp])
    S.dve(lambda e: e.tensor_copy(sh[:], vec[:, 2, :]), [vec], [sh])
    xrot = Rot([S.sb([128, KT, 512], name="xs") for _ in range(2)])
    sqrot = Rot([S.sb([128, 512], name="sq") for _ in range(3)])
    tmprot = Rot([S.sb([128, 512], name="tmp") for _ in range(3)])
    rstd = S.sb([128, 512])
    ps_ss = S.ps([128, 512])
    hT = [S.sb([128, KT, 512], BF16, name="hT") for _ in range(NTT)]
    for tt in range(NTT):
        xs = xrot.get()
        S.dma("sp", xs, x_d, in_ap=x_d.ap[:, tt * 512:(tt + 1) * 512].rearrange("(kt p) n -> p kt n", p=128))
        modulated_norm(S, xs, 512, gp, sh, ones, ps_ss, sqrot, rstd, tmprot,
                       [hT[tt][:, kt, :] for kt in range(KT)], [hT[tt]] * KT)
    wrot = Rot([S.sb([128, KT, 512], BF16, name="wsl") for _ in range(2)])
    psrot = Rot([S.ps([128, 512], name="pso") for _ in range(4)])
    evrot = Rot([S.sb([128, 512], name="ev") for _ in range(4)])
    c0 = 0
    while c0 < NO:
        cw = min(512, NO - c0)
        ws = wrot.get()
        S.dma("pool", ws, w_d, out_ap=ws[:, :, 0:cw],
              in_ap=w_d.ap[:, c0:c0 + cw].rearrange("(kt p) c -> p kt c", p=128))
        m0 = 0
        while m0 < cw:
            M = min(128, cw - m0)
            for tt in range(NTT):
                pp = psrot.get()
                S.mmg(pp[0:M, :], [(ws[:, kt, m0:m0 + M], hT[tt][:, kt, :]) for kt in range(KT)], [ws, hT[tt]], [pp])
                ev = evrot.get()
                S.act(lambda e, ev=ev, pp=pp, M=M: e.copy(ev[0:M, :], pp[0:M, :]), [pp], [ev])
                S.dma("sp", None, ev, out_ap=o_d.ap[c0 + m0:c0 + m0 + M, tt * 512:(tt + 1) * 512], in_ap=ev[0:M, :],
                      sem_of=ev, is_output=True)
            m0 += M
        c0 += cw
    S.finish()
    return nc


def gdn_consts():
    idx = np.arange(128)
    same = (idx[:, None] // 64) == (idx[None, :] // 64)
    tri = (same & (idx[:, None] <= idx[None, :])).astype(np.float32)
    mus = (same & (idx[:, None] < idx[None, :])).astype(np.float32)
    mls = np.ascontiguousarray(mus.T)
    bd2 = np.zeros((128, 2), np.float32)
    bd2[:64, 0] = 1
    bd2[64:, 1] = 1
    return {"ident": np.eye(128, dtype=np.float32), "tri": tri, "mus": mus, "mls": mls, "bd2": bd2,
            "ones": np.ones((128, 128), np.float32)}


def build_gdn(NT=16384, PIPE=True):
    nc = new_nc()
    S = Sched(nc)
    NS = NT // 512
    qkv_d = [S.din(n, [128, NT]) for n in ("qT", "kT", "vT")]
    z_d = S.din("zT", [128, NT])
    a_d = S.din("a_row", [1, NT]); b_d = S.din("b_row", [1, NT])
    cw_d = S.din("cw", [128, 12])
    par_d = S.din("par", [128, 3])
    cst = {}
    for n, shp in (("ident", [128, 128]), ("tri", [128, 128]), ("mus", [128, 128]), ("mls", [128, 128]),
                   ("bd2", [128, 2]), ("ones", [128, 128])):
        d = S.din(n, shp)
        cst[n] = S.sb(shp, name=n)
        S.dma("sp", cst[n], d)
    ident, tri, mus, mls, bd2, ones = (cst[n] for n in ("ident", "tri", "mus", "mls", "bd2", "ones"))
    o_d = S.dout("oT", [128, NT])
    cw = S.sb([128, 12]); S.dma("sp", cw, cw_d)
    par = S.sb([128, 3]); S.dma("sp", par, par_d)
    negA = S.sb([128, 1])
    S.act(lambda e: e.activation(negA[:], par[:, 0:1], AF.Exp), [par], [negA])
    S.dve(lambda e: e.tensor_scalar(negA[:], negA[:], -1.0, None, ALU.mult), [negA], [negA])
    Sst = S.sb([128, 128], name="state")
    S.dve(lambda e: e.memset(Sst[:], 0.0), [], [Sst])

    rawrot = [Rot([S.sb([128, 515], name="raw") for _ in range(2)]) for _ in range(3)]
    zrot = Rot([S.sb([128, 512], name="z") for _ in range(3)])
    arot = Rot([S.sb([1, 512], name="ar") for _ in range(3)])
    brot = Rot([S.sb([1, 512], name="br") for _ in range(3)])
    crot = [Rot([S.sb([128, 512], name="c") for _ in range(3)]) for _ in range(3)]
    yrot = Rot([S.sb([128, 512], name="y") for _ in range(2)])
    sqrot = Rot([S.sb([128, 512], name="sq") for _ in range(2)])
    rnrot = Rot([S.sb([128, 512], name="rn") for _ in range(2)])
    orot = Rot([S.sb([128, 512], name="oslab") for _ in range(3)])
    rowt = Rot([S.sb([1, 512], name="rowt") for _ in range(2)])
    banks = [S.ps([128, 512], name="bank") for _ in range(8)]
    ps_big = Rot(banks[0:1])
    pp = Rot([T(bk.ap[:, 0:128], base=bk) for bk in banks[1:4]])
    pq = Rot([T(bk.ap[:, 0:128], base=bk) for bk in banks[4:7]])
    psm = Rot([T(banks[7].ap[:, j * 128:j * 128 + 4], base=banks[7]) for j in range(4)])
    mrot = Rot([S.sb([128, 128], name="m") for _ in range(20)])
    srot = Rot([S.sb([128, 128], name="ms") for _ in range(8)])
    prodrot = {n: Rot([S.sb([128, 128], name=n) for _ in range(3)]) for n in ("u", "wT", "qkT", "Ktail")}
    scrot = Rot([S.sb([128, 8], name="sc") for _ in range(4)])
    ssrot = Rot([S.sb([128, 8], name="sso") for _ in range(3)])

    def slab_prep(X, s):
        t0 = s * 512
        cs = []
        for qi in range(3):
            raw = rawrot[qi].get()
            if s == 0:
                X.dve(lambda e: e.memset(raw[:, 0:3], 0.0), [], [raw])
                X.dma("sp", raw, qkv_d[qi], out_ap=raw[:, 3:515], in_ap=qkv_d[qi].ap[:, 0:512])
            else:
                X.dma("sp", raw, qkv_d[qi], in_ap=qkv_d[qi].ap[:, t0 - 3:t0 + 512])
            y = yrot.get()
            X.dve(lambda e: e.tensor_scalar(y[:], raw[:, 3:515], cw[:, 4 * qi + 3:4 * qi + 4], None, ALU.mult), [raw, cw], [y])
            for j in (2, 1, 0):
                X.dve(lambda e: e.scalar_tensor_tensor(y[:], raw[:, j:j + 512], cw[:, 4 * qi + j:4 * qi + j + 1], y[:],
                                                       ALU.mult, ALU.add), [raw, cw, y], [y])
            c = crot[qi].get()
            X.act(lambda e: e.activation(c[:], y[:], AF.Silu), [y], [c])
            cs.append(c)
        cq, ck, cv = cs
        sz = zrot.get()
        X.dma("sp", sz, z_d, in_ap=z_d.ap[:, t0:t0 + 512])
        X.act(lambda e: e.activation(sz[:], sz[:], AF.Silu), [sz], [sz])
        for c, extra in ((cq, 128.0 ** -0.5), (ck, 1.0)):
            sq = sqrot.get(); pb = ps_big.get(); rn = rnrot.get()
            X.act(lambda e: e.activation(sq[:], c[:], AF.Square), [c], [sq])
            X.mm(pb[:], ones[:], sq[:], [ones, sq], [pb])
            X.dve(lambda e: e.tensor_scalar(rn[:], pb[:], EPS, None, ALU.add), [pb], [rn])
            X.act(lambda e: e.activation(rn[:], rn[:], AF.Sqrt), [rn], [rn])
            X.dve(lambda e: e.reciprocal(rn[:], rn[:]), [rn], [rn])
            X.dve(lambda e: e.scalar_tensor_tensor(c[:], c[:], extra, rn[:], ALU.mult, ALU.mult), [c, rn], [c])
        gr = arot.get(); br = brot.get(); rt = rowt.get()
        X.dma("sp", gr, a_d, in_ap=a_d.ap[:, t0:t0 + 512])
        X.dma("sp", br, b_d, in_ap=b_d.ap[:, t0:t0 + 512])
        X.dve(lambda e: e.tensor_scalar(gr[:], gr[:], par[0:1, 1:2], None, ALU.add), [gr, par], [gr])
        X.act(lambda e: e.activation(rt[:], gr[:], AF.Abs), [gr], [rt])
        X.act(lambda e: e.activation(rt[:], rt[:], AF.Exp, scale=-1.0), [rt], [rt])
        X.dve(lambda e: e.tensor_scalar(rt[:], rt[:], 1.0, None, ALU.add), [rt], [rt])
        X.act(lambda e: e.activation(rt[:], rt[:], AF.Ln), [rt], [rt])
        X.dve(lambda e: e.tensor_scalar(gr[:], gr[:], 0.0, None, ALU.max), [gr], [gr])
        X.dve(lambda e: e.tensor_tensor(gr[:], gr[:], rt[:], ALU.add), [gr, rt], [gr])
        X.dve(lambda e: e.tensor_scalar(gr[:], gr[:], negA[0:1, 0:1], None, ALU.mult), [gr, negA], [gr])
        X.act(lambda e: e.activation(br[:], br[:], AF.Sigmoid), [br], [br])
        return dict(cq=cq, ck=ck, cv=cv, sz=sz, gr=gr, br=br, oslab=orot.get(), t0=t0)

    def blk_prep(X, sl, b):
        cq, ck, cv, gr, br = sl["cq"], sl["ck"], sl["cv"], sl["gr"], sl["br"]
        M = mrot.get
        bl = slice(b * 128, b * 128 + 128)
        pc = psm.get()
        X.mm(pc[:, 0:1], gr[0:1, bl], ones[0:1, 0:1], [gr, ones], [pc])
        X.mm(pc[:, 1:2], br[0:1, bl], ones[0:1, 0:1], [br, ones], [pc])
        sc = scrot.get()
        X.dve(lambda e: e.tensor_copy(sc[:, 6:8], pc[:, 0:2]), [pc], [sc])
        Gb = M()
        X.dve(lambda e: e.tensor_scalar(Gb[:], ones[:], sc[:, 6:7], None, ALU.mult), [ones, sc], [Gb])
        pg = psm.get()
        X.mm(pg[:, 0:1], tri[:], sc[:, 6:7], [tri, sc], [pg])
        X.mm(pg[:, 1:3], Gb[:], bd2[:], [Gb, bd2], [pg])
        pgrow = pp.get()
        X.mm(pgrow[:], Gb[:], tri[:], [Gb, tri], [pgrow])
        X.dve(lambda e: e.tensor_copy(sc[:, 0:1], pg[:, 0:1]), [pg], [sc])
        X.act(lambda e: e.activation(sc[:, 1:3], pg[:, 1:3], AF.Exp), [pg], [sc])
        X.dve(lambda e: e.tensor_copy(sc[0:64, 3:4], pg[0:64, 1:2]), [pg], [sc])
        X.dve(lambda e: e.tensor_copy(sc[64:128, 3:4], pg[64:128, 2:3]), [pg], [sc])
        X.act(lambda e: e.activation(sc[:, 4:5], sc[:, 0:1], AF.Exp), [sc], [sc])
        X.dve(lambda e: e.tensor_tensor(sc[:, 5:6], sc[:, 3:4], sc[:, 0:1], ALU.subtract), [sc], [sc])
        X.act(lambda e: e.activation(sc[:, 5:6], sc[:, 5:6], AF.Exp), [sc], [sc])
        X.dve(lambda e: e.tensor_tensor(sc[:, 6:7], sc[:, 7:8], sc[:, 4:5], ALU.mult), [sc], [sc])
        tdm = M(); dU = M(); dL = M()
        X.dve(lambda e: e.tensor_scalar(tdm[:], pgrow[:], sc[:, 0:1], None, ALU.subtract), [pgrow, sc], [tdm])
        X.dve(lambda e: e.tensor_scalar(dU[:], tdm[:], 0.0, None, ALU.min), [tdm], [dU])
        X.dve(lambda e: e.tensor_scalar(dL[:], tdm[:], 0.0, -1.0, ALU.max, ALU.mult), [tdm], [dL])
        X.act(lambda e: e.activation(dU[:], dU[:], AF.Exp), [dU], [dU])
        X.act(lambda e: e.activation(dL[:], dL[:], AF.Exp), [dL], [dL])
        pbrow = pp.get()
        X.mm(pbrow[:], ones[0:1, :], br[0:1, bl], [ones, br], [pbrow])
        U = M(); L = M(); R = M()
        qkT = prodrot["qkT"].get()
        X.dve(lambda e: e.tensor_tensor(U[:], dU[:], mus[:], ALU.mult), [dU, mus], [U])
        X.dve(lambda e: e.tensor_tensor(U[:], pbrow[:], U[:], ALU.mult), [U, pbrow], [U])
        pG = pp.get()
        X.mm(pG[:], ck[:, bl], ck[:, bl], [ck], [pG])
        X.dve(lambda e: e.tensor_tensor(U[:], pG[:], U[:], ALU.mult), [U, pG], [U])
        X.dve(lambda e: e.tensor_tensor(dL[:], dL[:], mls[:], ALU.mult), [dL, mls], [dL])
        X.dve(lambda e: e.scalar_tensor_tensor(L[:], pG[:], sc[:, 7:8], dL[:], ALU.mult, ALU.mult), [pG, sc, dL], [L])
        pQK = pp.get()
        X.mm(pQK[:], ck[:, bl], cq[:, bl], [ck, cq], [pQK])
        X.dve(lambda e: e.tensor_tensor(dU[:], dU[:], tri[:], ALU.mult), [dU, tri], [dU])
        X.dve(lambda e: e.tensor_tensor(qkT[:], pQK[:], dU[:], ALU.mult), [dU, pQK], [qkT])
        X.act(lambda e: e.activation(R[:], U[:], AF.Copy, scale=-1.0), [U], [R])
        X.dve(lambda e: e.tensor_tensor(R[:], R[:], ident[:], ALU.add), [ident, R], [R])
        P, Q = U, L
        for k in range(1, 6):
            pQn = pp.get()
            X.mm(pQn[:], P[:], Q[:], [P, Q], [pQn])
            Qn = M()
            X.act(lambda e: e.copy(Qn[:], pQn[:]), [pQn], [Qn])
            if k < 5:
                pPn = pp.get()
                X.mm(pPn[:], Q[:], P[:], [P, Q], [pPn])
                Pn = M()
                X.dve(lambda e: e.tensor_copy(Pn[:], pPn[:]), [pPn], [Pn])
            pR = pp.get()
            X.mm(pR[:], Qn[:], R[:], [Qn, R], [pR])
            X.dve(lambda e: e.tensor_tensor(R[:], pR[:], R[:], ALU.add), [R, pR], [R])
            Q = Qn
            if k < 5:
                P = Pn
        pKT = pp.get()
        X.pe(lambda e: e.transpose(pKT[:], ck[:, bl], ident[:]), [ck, ident], [pKT])
        Kbg = M(); Vb = M()
        Ktail = prodrot["Ktail"].get()
        X.dve(lambda e: e.tensor_scalar(Kbg[:], pKT[:], sc[:, 6:7], None, ALU.mult), [pKT, sc], [Kbg])
        X.dve(lambda e: e.tensor_scalar(Ktail[:], pKT[:], sc[:, 5:6], None, ALU.mult), [pKT, sc], [Ktail])
        pVT = pp.get()
        X.pe(lambda e: e.transpose(pVT[:], cv[:, bl], ident[:]), [cv, ident], [pVT])
        X.dve(lambda e: e.tensor_scalar(Vb[:], pVT[:], sc[:, 7:8], None, ALU.mult), [pVT, sc], [Vb])
        pu = pp.get()
        X.mm(pu[:], R[:], Vb[:], [R, Vb], [pu])
        u = prodrot["u"].get(); wT = prodrot["wT"].get()
        X.act(lambda e: e.copy(u[:], pu[:]), [pu], [u])
        pw = pp.get()
        X.mm(pw[:], Kbg[:], R[:], [Kbg, R], [pw])
        X.dve(lambda e: e.tensor_copy(wT[:], pw[:]), [pw], [wT])
        return dict(u=u, wT=wT, qkT=qkT, Ktail=Ktail, sc=sc, bl=bl)

    def blk_seq(X, sl, P, last):
        cq, sz, oslab = sl["cq"], sl["sz"], sl["oslab"]
        u, wT, qkT, Ktail, sc, bl = P["u"], P["wT"], P["qkT"], P["Ktail"], P["sc"], P["bl"]
        M = srot.get
        vnew = M(); o = M(); ot = M()
        for c in range(2):
            rc = slice(c * 64, (c + 1) * 64)
            pv = pq.get()
            X.mm(pv[:], wT[:], Sst[:], [wT, Sst], [pv])
            X.dve(lambda e: e.scalar_tensor_tensor(vnew[rc, :], pv[rc, :], -1.0, u[rc, :], ALU.mult, ALU.add), [u, pv], [vnew])
            pS = pq.get()
            X.mm(pS[:], Ktail[rc, :], vnew[rc, :], [Ktail, vnew], [pS])
            po1 = pq.get()
            X.mm(po1[:], cq[:, bl], Sst[:], [cq, Sst], [po1])
            X.dve(lambda e: e.tensor_scalar(Sst[:], Sst[:], sc[:, 1 + c:2 + c], None, ALU.mult), [Sst, sc], [Sst])
            X.dve(lambda e: e.tensor_tensor(Sst[:], pS[:], Sst[:], ALU.add), [Sst, pS], [Sst])
            X.dve(lambda e: e.tensor_scalar(ot[rc, :], po1[rc, :], sc[rc, 4:5], None, ALU.mult), [po1, sc], [ot])
            po2 = pq.get()
            X.mm(po2[:], qkT[rc, :], vnew[rc, :], [qkT, vnew], [po2])
            X.dve(lambda e: e.tensor_tensor(o[rc, :], po2[rc, :], ot[rc, :], ALU.add), [ot, po2], [o])
        sso = ssrot.get(); osq = M()
        X.dve(lambda e: e.memset(sso[:, 0:1], 0.0), [], [sso])
        X.act(lambda e: e.activation(osq[:], o[:], AF.Square, accum_out=sso[:, 0:1]), [o, sso], [osq, sso])
        X.dve(lambda e: e.tensor_scalar(sso[:, 0:1], sso[:, 0:1], 1.0 / 128, EPS, ALU.mult, ALU.add), [sso], [sso])
        X.act(lambda e: e.activation(sso[:, 0:1], sso[:, 0:1], AF.Sqrt), [sso], [sso])
        X.dve(lambda e: e.reciprocal(sso[:, 0:1], sso[:, 0:1]), [sso], [sso])
        X.dve(lambda e: e.tensor_scalar(osq[:], o[:], sso[:, 0:1], None, ALU.mult), [o, sso], [osq])
        pOT = pq.get()
        X.pe(lambda e: e.transpose(pOT[:], osq[:], ident[:]), [osq, ident], [pOT])
        X.dve(lambda e: e.scalar_tensor_tensor(oslab[:, bl], pOT[:], par[:, 2:3], sz[:, bl], ALU.mult, ALU.mult),
              [pOT, par, sz], [oslab])
        if last:
            X.dma("sp", None, oslab, out_ap=o_d.ap[:, sl["t0"]:sl["t0"] + 512], sem_of=oslab, is_output=True)

    pending = None
    for s in range(NS):
        dprep = Deferred(S)
        sl = slab_prep(dprep, s)
        for b in range(4):
            if b > 0:
                dprep = Deferred(S)
            P = blk_prep(dprep, sl, b)
            if PIPE:
                interleave(S, [pending, dprep])
            else:
                interleave(S, [pending]); interleave(S, [dprep])
            pending = Deferred(S)
            blk_seq(pending, sl, P, b == 3)
    interleave(S, [pending])
    S.finish()
    return nc


def build_gdn_v1(NT=16384, STAGE=9):
    nc = new_nc()
    S = Sched(nc)
    NS = NT // 512
    qkv_d = [S.din(n, [128, NT]) for n in ("qT", "kT", "vT")]
    z_d = S.din("zT", [128, NT])
    a_d = S.din("a_row", [1, NT]); b_d = S.din("b_row", [1, NT])
    cw_d = S.din("cw", [128, 12])
    par_d = S.din("par", [128, 3])
    cst = {}
    for n, shp in (("ident", [128, 128]), ("tri", [128, 128]), ("mus", [128, 128]), ("mls", [128, 128]),
                   ("bd2", [128, 2]), ("ones", [128, 128])):
        d = S.din(n, shp)
        cst[n] = S.sb(shp, name=n)
        S.dma("sp", cst[n], d)
    ident, tri, mus, mls, bd2, ones = (cst[n] for n in ("ident", "tri", "mus", "mls", "bd2", "ones"))
    o_d = S.dout("oT", [128, NT])
    cw = S.sb([128, 12]); S.dma("sp", cw, cw_d)
    par = S.sb([128, 3]); S.dma("sp", par, par_d)
    negA = S.sb([128, 1])
    S.act(lambda e: e.activation(negA[:], par[:, 0:1], AF.Exp), [par], [negA])
    S.dve(lambda e: e.tensor_scalar(negA[:], negA[:], -1.0, None, ALU.mult), [negA], [negA])
    Sst = S.sb([128, 128], name="state")
    S.dve(lambda e: e.memset(Sst[:], 0.0), [], [Sst])

    rawrot = [Rot([S.sb([128, 515], name="raw") for _ in range(2)]) for _ in range(3)]
    zrot = Rot([S.sb([128, 512], name="z") for _ in range(2)])
    arot = Rot([S.sb([1, 512], name="ar") for _ in range(2)])
    brot = Rot([S.sb([1, 512], name="br") for _ in range(2)])
    crot = [Rot([S.sb([128, 512], name="c") for _ in range(2)]) for _ in range(3)]
    yrot = Rot([S.sb([128, 512], name="y") for _ in range(2)])
    sqrot = Rot([S.sb([128, 512], name="sq") for _ in range(2)])
    rnrot = Rot([S.sb([128, 512], name="rn") for _ in range(2)])
    orot = Rot([S.sb([128, 512], name="oslab") for _ in range(2)])
    rowt = Rot([S.sb([1, 512], name="rowt") for _ in range(2)])
    ps_big = Rot([S.ps([128, 512], name="psb") for _ in range(2)])
    banks = [S.ps([128, 512], name="bank") for _ in range(6)]
    pp = Rot([T(bk.ap[:, 0:128], base=bk) for bk in banks[:5]])
    psm = Rot([T(banks[5].ap[:, j * 128:j * 128 + 4], base=banks[5]) for j in range(4)])
    mrot = Rot([S.sb([128, 128], name="m") for _ in range(24)])
    crot_s = Rot([S.sb([128, 8], name="sc") for _ in range(4)])

    def M():
        return mrot.get()

    for s in range(NS):
        t0 = s * 512
        cs = []
        for qi in range(3):
            raw = rawrot[qi].get()
            if s == 0:
                S.dve(lambda e, raw=raw: e.memset(raw[:, 0:3], 0.0), [], [raw])
                S.dma("sp", raw, qkv_d[qi], out_ap=raw[:, 3:515], in_ap=qkv_d[qi].ap[:, 0:512])
            else:
                S.dma("sp", raw, qkv_d[qi], in_ap=qkv_d[qi].ap[:, t0 - 3:t0 + 512])
            y = yrot.get()
            S.dve(lambda e, y=y, raw=raw, qi=qi: e.tensor_scalar(y[:], raw[:, 3:515], cw[:, 4 * qi + 3:4 * qi + 4], None,
                                                                 ALU.mult), [raw, cw], [y])
            for j in (2, 1, 0):
                S.dve(lambda e, y=y, raw=raw, qi=qi, j=j: e.scalar_tensor_tensor(
                    y[:], raw[:, j:j + 512], cw[:, 4 * qi + j:4 * qi + j + 1], y[:], ALU.mult, ALU.add), [raw, cw, y], [y])
            c = crot[qi].get()
            S.act(lambda e, c=c, y=y: e.activation(c[:], y[:], AF.Silu), [y], [c])
            cs.append(c)
        cq, ck, cv = cs
        sz = zrot.get()
        S.dma("sp", sz, z_d, in_ap=z_d.ap[:, t0:t0 + 512])
        S.act(lambda e, sz=sz: e.activation(sz[:], sz[:], AF.Silu), [sz], [sz])
        for c, extra in ((cq, 128.0 ** -0.5), (ck, 1.0)):
            sq = sqrot.get(); pb = ps_big.get(); rn = rnrot.get()
            S.act(lambda e, sq=sq, c=c: e.activation(sq[:], c[:], AF.Square), [c], [sq])
            S.mm(pb[:], ones[:], sq[:], [ones, sq], [pb])
            S.dve(lambda e, rn=rn, pb=pb: e.tensor_scalar(rn[:], pb[:], EPS, None, ALU.add), [pb], [rn])
            S.act(lambda e, rn=rn: e.activation(rn[:], rn[:], AF.Sqrt), [rn], [rn])
            S.dve(lambda e, rn=rn: e.reciprocal(rn[:], rn[:]), [rn], [rn])
            S.dve(lambda e, c=c, rn=rn, extra=extra: e.scalar_tensor_tensor(c[:], c[:], extra, rn[:], ALU.mult, ALU.mult),
                  [c, rn], [c])
        gr = arot.get(); br = brot.get(); rt = rowt.get()
        S.dma("sp", gr, a_d, in_ap=a_d.ap[:, t0:t0 + 512])
        S.dma("sp", br, b_d, in_ap=b_d.ap[:, t0:t0 + 512])
        S.dve(lambda e, gr=gr: e.tensor_scalar(gr[:], gr[:], par[0:1, 1:2], None, ALU.add), [gr, par], [gr])
        S.dve(lambda e, gr=gr, rt=rt: e.tensor_scalar(rt[:], gr[:], 0.0, None, ALU.abs_max), [gr], [rt]) if False else None
        S.act(lambda e, gr=gr, rt=rt: e.activation(rt[:], gr[:], AF.Abs), [gr], [rt])
        S.act(lambda e, rt=rt: e.activation(rt[:], rt[:], AF.Exp, scale=-1.0), [rt], [rt])
        S.dve(lambda e, rt=rt: e.tensor_scalar(rt[:], rt[:], 1.0, None, ALU.add), [rt], [rt])
        S.act(lambda e, rt=rt: e.activation(rt[:], rt[:], AF.Ln), [rt], [rt])
        S.dve(lambda e, gr=gr: e.tensor_scalar(gr[:], gr[:], 0.0, None, ALU.max), [gr], [gr])
        S.dve(lambda e, gr=gr, rt=rt: e.tensor_tensor(gr[:], gr[:], rt[:], ALU.add), [gr, rt], [gr])
        S.dve(lambda e, gr=gr: e.tensor_scalar(gr[:], gr[:], negA[0:1, 0:1], None, ALU.mult), [gr, negA], [gr])
        S.act(lambda e, br=br: e.activation(br[:], br[:], AF.Sigmoid), [br], [br])
        oslab = orot.get()
        if STAGE < 2:
            S.dve(lambda e, oslab=oslab, cq=cq, ck=ck, cv=cv, sz=sz: e.tensor_tensor(oslab[:], cq[:], ck[:], ALU.add), [cq, ck, cv, sz, gr, br], [oslab])
        for b in range(4 if STAGE >= 2 else 0):
            c0 = b * 128
            bl = slice(c0, c0 + 128)
            pc = psm.get()
            S.mm(pc[:, 0:1], gr[0:1, bl], ones[0:1, 0:1], [gr, ones], [pc])
            S.mm(pc[:, 1:2], br[0:1, bl], ones[0:1, 0:1], [br, ones], [pc])
            sc = crot_s.get()
            S.dve(lambda e, sc=sc, pc=pc: e.tensor_copy(sc[:, 6:8], pc[:, 0:2]), [pc], [sc])
            Gb = M()
            S.dve(lambda e, Gb=Gb, sc=sc: e.tensor_scalar(Gb[:], ones[:], sc[:, 6:7], None, ALU.mult), [ones, sc], [Gb])
            pg = psm.get()
            S.mm(pg[:, 0:1], tri[:], sc[:, 6:7], [tri, sc], [pg])
            S.mm(pg[:, 1:3], Gb[:], bd2[:], [Gb, bd2], [pg])
            pgrow = pp.get()
            S.mm(pgrow[:], Gb[:], tri[:], [Gb, tri], [pgrow])
            pbrow = pp.get()
            S.mm(pbrow[:], ones[0:1, :], br[0:1, bl], [ones, br], [pbrow])
            S.dve(lambda e, sc=sc, pg=pg: e.tensor_copy(sc[:, 0:1], pg[:, 0:1]), [pg], [sc])
            S.act(lambda e, sc=sc, pg=pg: e.activation(sc[:, 1:3], pg[:, 1:3], AF.Exp), [pg], [sc])
            S.dve(lambda e, sc=sc, pg=pg: e.tensor_copy(sc[0:64, 3:4], pg[0:64, 1:2]), [pg], [sc])
            S.dve(lambda e, sc=sc, pg=pg: e.tensor_copy(sc[64:128, 3:4], pg[64:128, 2:3]), [pg], [sc])
            S.act(lambda e, sc=sc: e.activation(sc[:, 4:5], sc[:, 0:1], AF.Exp), [sc], [sc])
            S.dve(lambda e, sc=sc: e.tensor_tensor(sc[:, 5:6], sc[:, 3:4], sc[:, 0:1], ALU.subtract), [sc], [sc])
            S.act(lambda e, sc=sc: e.activation(sc[:, 5:6], sc[:, 5:6], AF.Exp), [sc], [sc])
            S.dve(lambda e, sc=sc: e.tensor_tensor(sc[:, 6:7], sc[:, 7:8], sc[:, 4:5], ALU.mult), [sc], [sc])
            tdm = M(); dU = M(); dL = M()
            S.dve(lambda e, tdm=tdm, pgrow=pgrow, sc=sc: e.tensor_scalar(tdm[:], pgrow[:], sc[:, 0:1], None, ALU.subtract),
                  [pgrow, sc], [tdm])
            S.dve(lambda e, tdm=tdm, dU=dU: e.tensor_scalar(dU[:], tdm[:], 0.0, None, ALU.min), [tdm], [dU])
            S.dve(lambda e, tdm=tdm, dL=dL: e.tensor_scalar(dL[:], tdm[:], 0.0, -1.0, ALU.max, ALU.mult), [tdm], [dL])
            S.act(lambda e, dU=dU: e.activation(dU[:], dU[:], AF.Exp), [dU], [dU])
            S.act(lambda e, dL=dL: e.activation(dL[:], dL[:], AF.Exp), [dL], [dL])
            if STAGE < 3:
                S.dve(lambda e, oslab=oslab, dU=dU, bl=bl: e.tensor_copy(oslab[:, bl], dU[:]), [dU, dL, sc], [oslab])
                continue
            pG = pp.get(); pQK = pp.get()
            S.mm(pG[:], ck[:, bl], ck[:, bl], [ck], [pG])
            S.mm(pQK[:], ck[:, bl], cq[:, bl], [ck, cq], [pQK])
            U = M(); L = M(); qkT = M(); R = M()
            import os
            SUB = int(os.environ.get("GDN_SUB", "9"))
            if SUB == 1:
                S.dve(lambda e, oslab=oslab, pG=pG, bl=bl: e.tensor_copy(oslab[:, bl], pG[:]), [pG, pQK], [oslab])
                continue
            if SUB == 2:
                S.dve(lambda e, U=U, dU=dU: e.tensor_tensor(U[:], dU[:], mus[:], ALU.mult), [dU, mus], [U])
                S.dve(lambda e, U=U, pbrow=pbrow: e.tensor_tensor(U[:], pbrow[:], U[:], ALU.mult), [U, pbrow], [U])
                S.dve(lambda e, U=U, pG=pG: e.tensor_tensor(U[:], pG[:], U[:], ALU.mult), [U, pG], [U])
                S.dve(lambda e, oslab=oslab, U=U, bl=bl: e.tensor_copy(oslab[:, bl], U[:]), [U, pQK], [oslab])
                continue
            if SUB == 3:
                S.dve(lambda e, dL=dL: e.tensor_tensor(dL[:], dL[:], mls[:], ALU.mult), [dL, mls], [dL])
                S.dve(lambda e, L=L, pG=pG, sc=sc, dL=dL: e.scalar_tensor_tensor(L[:], pG[:], sc[:, 7:8], dL[:], ALU.mult, ALU.mult),
                      [pG, sc, dL], [L])
                S.dve(lambda e, oslab=oslab, L=L, bl=bl: e.tensor_copy(oslab[:, bl], L[:]), [L, pQK], [oslab])
                continue
            S.dve(lambda e, U=U, dU=dU: e.tensor_tensor(U[:], dU[:], mus[:], ALU.mult), [dU, mus], [U])
            S.dve(lambda e, U=U, pbrow=pbrow: e.tensor_tensor(U[:], pbrow[:], U[:], ALU.mult), [U, pbrow], [U])
            S.dve(lambda e, U=U, pG=pG: e.tensor_tensor(U[:], pG[:], U[:], ALU.mult), [U, pG], [U])
            S.dve(lambda e, dL=dL: e.tensor_tensor(dL[:], dL[:], mls[:], ALU.mult), [dL, mls], [dL])
            S.dve(lambda e, L=L, pG=pG, sc=sc, dL=dL: e.scalar_tensor_tensor(L[:], pG[:], sc[:, 7:8], dL[:], ALU.mult, ALU.mult),
                  [pG, sc, dL], [L])
            S.dve(lambda e, dU=dU: e.tensor_tensor(dU[:], dU[:], tri[:], ALU.mult), [dU, tri], [dU])
            S.dve(lambda e, qkT=qkT, pQK=pQK, dU=dU: e.tensor_tensor(qkT[:], pQK[:], dU[:], ALU.mult), [dU, pQK], [qkT])
            if SUB == 4:
                S.dve(lambda e, oslab=oslab, qkT=qkT, bl=bl: e.tensor_copy(oslab[:, bl], qkT[:]), [U, L, qkT], [oslab])
                continue
            if SUB == 7:
                S.dve(lambda e, R=R, U=U: e.tensor_copy(R[:], U[:]), [U], [R])
                S.dve(lambda e, oslab=oslab, R=R, bl=bl: e.tensor_copy(oslab[:, bl], R[:]), [U, L, qkT, R], [oslab])
                continue
            if SUB == 6:
                S.dve(lambda e, R=R: e.memset(R[:], 1.0), [], [R])
                S.dve(lambda e, oslab=oslab, R=R, bl=bl: e.tensor_copy(oslab[:, bl], R[:]), [U, L, qkT, R], [oslab])
                continue
            S.act(lambda e, R=R, U=U: e.activation(R[:], U[:], AF.Copy, scale=-1.0), [U], [R])
            S.dve(lambda e, R=R: e.tensor_tensor(R[:], R[:], ident[:], ALU.add), [ident, R], [R])
            if SUB == 5:
                S.dve(lambda e, oslab=oslab, qkT=qkT, bl=bl: e.tensor_copy(oslab[:, bl], qkT[:]), [U, L, qkT, R], [oslab])
                continue
            if STAGE < 4:
                S.dve(lambda e, oslab=oslab, R=R, bl=bl: e.tensor_copy(oslab[:, bl], R[:]), [R, L, qkT], [oslab])
                continue
            P, Q = U, L
            for k in range(1, 6):
                pQn = pp.get()
                S.mm(pQn[:], P[:], Q[:], [P, Q], [pQn])
                Qn = M()
                S.act(lambda e, Qn=Qn, pQn=pQn: e.copy(Qn[:], pQn[:]), [pQn], [Qn])
                if k < 5:
                    pPn = pp.get()
                    S.mm(pPn[:], Q[:], P[:], [P, Q], [pPn])
                    Pn = M()
                    S.dve(lambda e, Pn=Pn, pPn=pPn: e.tensor_copy(Pn[:], pPn[:]), [pPn], [Pn])
                pR = pp.get()
                S.mm(pR[:], Qn[:], R[:], [Qn, R], [pR])
                S.dve(lambda e, R=R, pR=pR: e.tensor_tensor(R[:], pR[:], R[:], ALU.add), [R, pR], [R])
                Q = Qn
                if k < 5:
                    P = Pn
            if STAGE < 5:
                S.dve(lambda e, oslab=oslab, R=R, bl=bl: e.tensor_copy(oslab[:, bl], R[:]), [R, L, qkT], [oslab])
                continue
            if SUB == 10:
                S.dve(lambda e, oslab=oslab, R=R, bl=bl: e.tensor_copy(oslab[:, bl], R[:]), [R, L, qkT], [oslab])
                continue
            pKT = pp.get(); pVT = pp.get()
            S.pe(lambda e, pKT=pKT: e.transpose(pKT[:], ck[:, bl], ident[:]), [ck, ident], [pKT])
            S.pe(lambda e, pVT=pVT: e.transpose(pVT[:], cv[:, bl], ident[:]), [cv, ident], [pVT])
            Kbg = M(); Ktail = M(); Vb = M()
            S.dve(lambda e, Kbg=Kbg, pKT=pKT, sc=sc: e.tensor_scalar(Kbg[:], pKT[:], sc[:, 6:7], None, ALU.mult), [pKT, sc], [Kbg])
            S.dve(lambda e, Ktail=Ktail, pKT=pKT, sc=sc: e.tensor_scalar(Ktail[:], pKT[:], sc[:, 5:6], None, ALU.mult),
                  [pKT, sc], [Ktail])
            S.dve(lambda e, Vb=Vb, pVT=pVT, sc=sc: e.tensor_scalar(Vb[:], pVT[:], sc[:, 7:8], None, ALU.mult), [pVT, sc], [Vb])
            if SUB == 11:
                S.dve(lambda e, oslab=oslab, Kbg=Kbg, bl=bl: e.tensor_copy(oslab[:, bl], Kbg[:]), [R, L, qkT, Kbg, Ktail, Vb], [oslab])
                continue
            pu = pp.get(); pw = pp.get()
            S.mm(pu[:], R[:], Vb[:], [R, Vb], [pu])
            S.mm(pw[:], Kbg[:], R[:], [Kbg, R], [pw])
            u = M(); wT = M()
            S.act(lambda e, u=u, pu=pu: e.copy(u[:], pu[:]), [pu], [u])
            S.dve(lambda e, wT=wT, pw=pw: e.tensor_copy(wT[:], pw[:]), [pw], [wT])
            if STAGE < 6:
                S.dve(lambda e, oslab=oslab, u=u, bl=bl: e.tensor_copy(oslab[:, bl], u[:]), [u, wT, Ktail], [oslab])
                continue
            vnew = M(); o = M(); ot = M()
            for c in range(2):
                rc = slice(c * 64, (c + 1) * 64)
                pv = pp.get()
                S.mm(pv[:], wT[:], Sst[:], [wT, Sst], [pv])
                S.dve(lambda e, vnew=vnew, u=u, pv=pv, rc=rc: e.scalar_tensor_tensor(vnew[rc, :], pv[rc, :], -1.0, u[rc, :], ALU.mult, ALU.add),
                      [u, pv], [vnew])
                po1 = pp.get(); po2 = pp.get()
                S.mm(po1[:], cq[:, bl], Sst[:], [cq, Sst], [po1])
                S.mm(po2[:], qkT[rc, :], vnew[rc, :], [qkT, vnew], [po2])
                S.dve(lambda e, ot=ot, po1=po1, sc=sc, rc=rc: e.tensor_scalar(ot[rc, :], po1[rc, :], sc[rc, 4:5], None, ALU.mult),
                      [po1, sc], [ot])
                S.dve(lambda e, o=o, ot=ot, po2=po2, rc=rc: e.tensor_tensor(o[rc, :], po2[rc, :], ot[rc, :], ALU.add),
                      [ot, po2], [o])
                pS = pp.get()
                S.mm(pS[:], Ktail[rc, :], vnew[rc, :], [Ktail, vnew], [pS])
                S.dve(lambda e, sc=sc, c=c: e.tensor_scalar(Sst[:], Sst[:], sc[:, 1 + c:2 + c], None, ALU.mult), [Sst, sc], [Sst])
                S.dve(lambda e, pS=pS: e.tensor_tensor(Sst[:], pS[:], Sst[:], ALU.add), [Sst, pS], [Sst])
            if STAGE < 7:
                S.dve(lambda e, oslab=oslab, o=o, bl=bl: e.tensor_copy(oslab[:, bl], o[:]), [o, Sst], [oslab])
                continue
            sso = crot_s.get(); osq = M()
            S.dve(lambda e, sso=sso: e.memset(sso[:, 0:1], 0.0), [], [sso])
            S.act(lambda e, osq=osq, o=o, sso=sso: e.activation(osq[:], o[:], AF.Square, accum_out=sso[:, 0:1]), [o, sso], [osq, sso])
            S.dve(lambda e, sso=sso: e.tensor_scalar(sso[:, 0:1], sso[:, 0:1], 1.0 / 128, EPS, ALU.mult, ALU.add), [sso], [sso])
            S.act(lambda e, sso=sso: e.activation(sso[:, 0:1], sso[:, 0:1], AF.Sqrt), [sso], [sso])
            S.dve(lambda e, sso=sso: e.reciprocal(sso[:, 0:1], sso[:, 0:1]), [sso], [sso])
            S.dve(lambda e, osq=osq, o=o, sso=sso: e.tensor_scalar(osq[:], o[:], sso[:, 0:1], None, ALU.mult), [o, sso], [osq])
            pOT = pp.get()
            S.pe(lambda e, pOT=pOT, osq=osq: e.transpose(pOT[:], osq[:], ident[:]), [osq, ident], [pOT])
            S.dve(lambda e, oslab=oslab, pOT=pOT, sz=sz, bl=bl: e.scalar_tensor_tensor(
                oslab[:, bl], pOT[:], par[:, 2:3], sz[:, bl], ALU.mult, ALU.mult), [pOT, par, sz], [oslab])
        S.dma("sp", None, oslab, out_ap=o_d.ap[:, t0:t0 + 512], sem_of=oslab, is_output=True)
    S.finish()
    return nc


TWO_PI = 6.283185307179586
CW1 = 6.28125
CW2 = TWO_PI - CW1


def build_mlaprep(NT=2048):
    nc = new_nc()
    S = Sched(nc)
    NTT = NT // 512
    cq_d = S.din("cqT", [448, NT]); ckv_d = S.din("ckvT", [128, NT])
    kr_d = S.din("krT", [64, NT]); krs_d = S.din("krsT", [64, NT])
    pos_d = S.din("pos", [1, NT], I32)
    inv_d = S.din("inv2", [1, 64])
    qnw_d = S.din("qnw", [128, 4]); kvnw_d = S.din("kvnw", [128, 1])
    wq_d = S.din("w_uq", [448, 768]); wqs_d = S.din("w_uq_sw", [448, 256])
    wkv_d = S.din("w_ukv", [128, 1024])
    ones_d = S.din("ones", [128, 128])
    sgn_d = S.din("sgn", [64, 1])
    q_o = S.dout("qT", [4, 192, NT]); kn_o = S.dout("knT", [4, 128, NT]); kpe_o = S.dout("kpeT", [64, NT])
    v_o = S.dout("V", [NT, 512])
    ones = S.sb([128, 128]); S.dma("sp", ones, ones_d)
    inv2 = S.sb([1, 64]); S.dma("sp", inv2, inv_d)
    qnw = S.sb([128, 4]); S.dma("sp", qnw, qnw_d)
    kvnw = S.sb([128, 1]); S.dma("sp", kvnw, kvnw_d)
    sgn = S.sb([64, 1]); S.dma("sp", sgn, sgn_d)
    KS = [128, 128, 128, 64]
    wq = S.sb([128, 4, 768], BF16); wqs = S.sb([128, 4, 256], BF16)
    for kt in range(4):
        S.dma("pool", wq, wq_d, out_ap=wq[0:KS[kt], kt, :], in_ap=wq_d.ap[kt * 128:kt * 128 + KS[kt], :])
        S.dma("pool", wqs, wqs_d, out_ap=wqs[0:KS[kt], kt, :], in_ap=wqs_d.ap[kt * 128:kt * 128 + KS[kt], :])
    wkv = S.sb([128, 1024], BF16); S.dma("pool", wkv, wkv_d)
    wv = S.sb([128, 512], BF16)
    for h in range(4):
        S.dma("pool", wv, wkv_d, out_ap=wv[:, h * 128:(h + 1) * 128], in_ap=wkv_d.ap[:, h * 256 + 128:h * 256 + 256])
    sqrot = Rot([S.sb([128, 512], name="sq") for _ in range(2)])
    rstd = S.sb([128, 512])
    banks = [S.ps([128, 512], name="bank") for _ in range(8)]
    ps_ss = banks[0]
    prot = Rot(banks[1:8])
    xq = Rot([S.sb([128, 4, 512], name="xq") for _ in range(2)])
    xkv = Rot([S.sb([128, 512], name="xkv") for _ in range(2)])
    xr = Rot([S.sb([64, 512], name="xr") for _ in range(2)])
    xrs = Rot([S.sb([64, 512], name="xrs") for _ in range(2)])
    posr = Rot([S.sb([1, 512], name="posr") for _ in range(2)])
    posi = Rot([S.sb([1, 512], I32, name="posi") for _ in range(2)])
    cqn = Rot([S.sb([128, 4, 512], BF16, name="cqn") for _ in range(2)])
    kvn = Rot([S.sb([128, 512], BF16, name="kvn") for _ in range(2)])
    tmpf = Rot([S.sb([128, 512], name="tmpf") for _ in range(3)])
    tmpi = S.sb([64, 512], I32)
    cs = [S.sb([64, 512], name="cos2"), S.sb([64, 512], name="sin2")]
    ev = Rot([S.sb([128, 512], name="ev") for _ in range(4)])

    def evac_out(pp_, M, dst_ap, scale=None):
        e_ = ev.get()
        if scale is None:
            S.act(lambda e: e.copy(e_[0:M, :], pp_[0:M, :]), [pp_], [e_])
        else:
            S.dve(lambda e: e.tensor_scalar(e_[0:M, :], pp_[0:M, :], scale, None, ALU.mult), [pp_], [e_])
        S.dma("sp", None, e_, out_ap=dst_ap, in_ap=e_[0:M, :], sem_of=e_, is_output=True)

    def rope_apply(x_ap, xs_ap, x_t, xs_t, dst):
        t1 = tmpf.get()
        S.dve(lambda e: e.tensor_tensor(t1[0:64, :], x_ap, cs[0][:], ALU.mult), [x_t, cs[0]], [t1])
        S.dve(lambda e: e.tensor_tensor(dst[0:64, :], xs_ap, cs[1][:], ALU.mult), [xs_t, cs[1]], [dst])
        S.dve(lambda e: e.tensor_tensor(dst[0:64, :], dst[0:64, :], t1[0:64, :], ALU.add), [dst, t1], [dst])

    for tt in range(NTT):
        ts = slice(tt * 512, (tt + 1) * 512)
        x = xq.get()
        for kt in range(4):
            S.dma("sp", x, cq_d, out_ap=x[0:KS[kt], kt, :], in_ap=cq_d.ap[kt * 128:kt * 128 + KS[kt], ts])
        xk = xkv.get(); S.dma("sp", xk, ckv_d, in_ap=ckv_d.ap[:, ts])
        r_ = xr.get(); S.dma("sp", r_, kr_d, in_ap=kr_d.ap[:, ts])
        rs_ = xrs.get(); S.dma("sp", rs_, krs_d, in_ap=krs_d.ap[:, ts])
        pi_ = posi.get(); S.dma("sp", pi_, pos_d, in_ap=pos_d.ap[:, ts])
        pr = posr.get()
        S.dve(lambda e: e.tensor_copy(pr[:], pi_[:]), [pi_], [pr])
        pang = prot.get()
        S.mm(pang[0:64, :], inv2[0:1, :], pr[0:1, :], [inv2, pr], [pang])
        for ci, shift in ((0, np.pi / 2), (1, 0.0)):
            a_ = tmpf.get(); kf = tmpf.get()
            S.dve(lambda e: e.tensor_scalar(a_[0:64, :], pang[0:64, :], shift, None, ALU.add), [pang], [a_])
            S.dve(lambda e: e.tensor_scalar(tmpi[:], a_[0:64, :], 1.0 / TWO_PI, None, ALU.mult), [a_], [tmpi])
            S.dve(lambda e: e.tensor_copy(kf[0:64, :], tmpi[:]), [tmpi], [kf])
            S.dve(lambda e: e.scalar_tensor_tensor(a_[0:64, :], kf[0:64, :], -CW1, a_[0:64, :], ALU.mult, ALU.add), [kf, a_], [a_])
            S.dve(lambda e: e.scalar_tensor_tensor(a_[0:64, :], kf[0:64, :], -CW2, a_[0:64, :], ALU.mult, ALU.add), [kf, a_], [a_])
            S.dve(lambda e: e.tensor_scalar(a_[0:64, :], a_[0:64, :], 3.1415925, -3.1415925, ALU.min, ALU.max), [a_], [a_])
            S.act(lambda e: e.activation(cs[ci][:], a_[0:64, :], AF.Sin), [a_], [cs[ci]])
        S.dve(lambda e: e.tensor_scalar(cs[1][:], cs[1][:], sgn[:, 0:1], None, ALU.mult), [cs[1], sgn], [cs[1]])
        rms_rstd(S, [(x, x[0:KS[kt], kt, :]) for kt in range(4)], 512, ones, ps_ss, sqrot, rstd, 448)
        cn = cqn.get()
        for kt in range(4):
            t1 = tmpf.get()
            S.dve(lambda e: e.tensor_tensor(t1[0:KS[kt], :], x[0:KS[kt], kt, :], rstd[0:KS[kt], :], ALU.mult), [x, rstd], [t1])
            S.dve(lambda e: e.tensor_scalar(cn[0:KS[kt], kt, :], t1[0:KS[kt], :], qnw[0:KS[kt], kt:kt + 1], None, ALU.mult),
                  [t1, qnw], [cn])
        qs = 192.0 ** -0.5
        for h in range(4):
            pn = prot.get()
            S.mmg(pn[:], [(wq[0:KS[kt], kt, h * 192:h * 192 + 128], cn[0:KS[kt], kt, :]) for kt in range(4)], [wq, cn], [pn])
            evac_out(pn, 128, q_o.ap[h, 0:128, ts], scale=qs)
            px = prot.get(); pxs = prot.get()
            S.mmg(px[0:64, :], [(wq[0:KS[kt], kt, h * 192 + 128:h * 192 + 192], cn[0:KS[kt], kt, :]) for kt in range(4)], [wq, cn], [px])
            S.mmg(pxs[0:64, :], [(wqs[0:KS[kt], kt, h * 64:(h + 1) * 64], cn[0:KS[kt], kt, :]) for kt in range(4)], [wqs, cn], [pxs])
            xsb = tmpf.get()
            S.act(lambda e: e.copy(xsb[0:64, :], pxs[0:64, :]), [pxs], [xsb])
            d_ = ev.get()
            rope_apply(px[0:64, :], xsb[0:64, :], px, xsb, d_)
            S.dve(lambda e: e.tensor_scalar(d_[0:64, :], d_[0:64, :], qs, None, ALU.mult), [d_], [d_])
            S.dma("sp", None, d_, out_ap=q_o.ap[h, 128:192, ts], in_ap=d_[0:64, :], sem_of=d_, is_output=True)
        rms_rstd(S, [(xk, xk[:, :])], 512, ones, ps_ss, sqrot, rstd, 128)
        kn_ = kvn.get()
        t1 = tmpf.get()
        S.dve(lambda e: e.tensor_tensor(t1[:], xk[:], rstd[:], ALU.mult), [xk, rstd], [t1])
        S.dve(lambda e: e.tensor_scalar(kn_[:], t1[:], kvnw[:, 0:1], None, ALU.mult), [t1, kvnw], [kn_])
        for h in range(4):
            pk = prot.get()
            S.mm(pk[:], wkv[:, h * 256:h * 256 + 128], kn_[:], [wkv, kn_], [pk])
            evac_out(pk, 128, kn_o.ap[h, :, ts])
        for j in range(4):
            pv = prot.get()
            S.mm(pv[:], kn_[:, j * 128:(j + 1) * 128], wv[:], [kn_, wv], [pv])
            evac_out(pv, 128, v_o.ap[tt * 512 + j * 128:tt * 512 + (j + 1) * 128, :])
        d_ = ev.get()
        rope_apply(r_[:], rs_[:], r_, rs_, d_)
        S.dma("sp", None, d_, out_ap=kpe_o.ap[:, ts], in_ap=d_[0:64, :], sem_of=d_, is_output=True)
    S.finish()
    return nc


def build_mla(NK=16384, nkt=(32, 64, 96, 128), mask_from=(0, 32, 64, 96)):
    nc = new_nc()
    S = Sched(nc)
    NSL = len(nkt)
    NQ = NSL * 512
    NKT = NK // 128
    q_d = S.din("qT", [4, 192, NQ]); qi_d = S.din("qidx", [1, NQ])
    kn_d = S.din("knT", [4, 128, NK]); kpe_d = S.din("kpeT", [64, NK]); v_d = S.din("V", [NK, 512])
    ki_d = S.din("kidx", [128, NKT])
    ones_d = S.din("ones", [128, 128]); id_d = S.din("ident", [128, 128])
    o_d = S.dout("obT", [512, NQ])
    ones = S.sb([128, 128]); S.dma("sp", ones, ones_d)
    identb = S.sb([128, 128], BF16); S.dma("pool", identb, id_d)
    kidx = S.sb([128, NKT]); S.dma("sp", kidx, ki_d)
    qir = S.sb([1, NQ]); S.dma("sp", qir, qi_d)
    kpe = S.sb([64, NK], BF16); S.dma("pool", kpe, kpe_d)
    banks = [S.ps([128, 512], name="bank") for _ in range(8)]
    st_rot = Rot(banks[0:3]); acc_rot = Rot(banks[3:5]); sum_rot = Rot(banks[5:7]); misc = Rot(banks[7:8])
    onesb = S.sb([128, 128], BF16); S.dma("pool", onesb, ones_d)
    qib = S.sb([128, NSL, 512])
    for i in range(NSL):
        pq = misc.get()
        S.mm(pq[:], ones[0:1, :], qir[0:1, i * 512:(i + 1) * 512], [ones, qir], [pq])
        S.act(lambda e: e.copy(qib[:, i, :], pq[:]), [pq], [qib])
    knh = S.sb([128, NK], BF16, name="knh")
    vh = S.sb([128, NKT, 128], BF16, name="vh")
    qn = S.sb([128, NQ], BF16); qr = S.sb([64, NQ], BF16)
    prot = Rot([S.sb([128, 512], BF16, name="pT") for _ in range(3)])
    mrot = Rot([S.sb([128, 512], BF16, name="mask") for _ in range(2)])
    accs = Rot([S.sb([128, 512], name="accs") for _ in range(2)])
    orot = Rot([S.sb([128, 512], name="o") for _ in range(2)])
    rinv = S.sb([128, 512])
    for h in range(4):
        S.dma("pool", knh, kn_d, in_ap=kn_d.ap[h])
        for k0 in range(0, NKT, 32):
            k1 = min(NKT, k0 + 32)
            S.dma("pool", vh, v_d, out_ap=vh[:, k0:k1, :],
                  in_ap=v_d.ap[k0 * 128:k1 * 128, h * 128:(h + 1) * 128].rearrange("(kt p) d -> p kt d", p=128))
        S.dma("pool", qn, q_d, in_ap=q_d.ap[h, 0:128, :])
        S.dma("pool", qr, q_d, in_ap=q_d.ap[h, 128:192, :])
        for i in range(NSL):
            qs_ = slice(i * 512, (i + 1) * 512)
            acc = acc_rot.get(); asum = sum_rot.get()
            prev = None
            for kt in range(nkt[i]):
                ks = slice(kt * 128, (kt + 1) * 128)
                masked = kt >= mask_from[i]
                st = st_rot.get()
                pairs = [(knh[:, ks], qn[:, qs_]), (kpe[:, ks], qr[:, qs_])]
                rd = [knh, kpe, qn, qr]
                if masked:
                    mk = mrot.get()
                    S.dve(lambda e: e.tensor_scalar(mk[:], qib[:, i, :], kidx[:, kt:kt + 1], -30000.0, ALU.is_lt, ALU.mult),
                          [qib, kidx], [mk])
                    pairs.append((identb[:], mk[:]))
                    rd = rd + [identb, mk]
                S.mmg(st[:], pairs, rd, [st])
                pT = prot.get()
                S.act(lambda e: e.activation(pT[:], st[:], AF.Exp), [st], [pT])
                if prev is not None:
                    pk, ppT = prev
                    S.mm(acc[:], vh[:, pk, :], ppT[:], [vh, ppT], [acc], start=(pk == 0), stop=False)
                    S.mm(asum[:], onesb[:], ppT[:], [onesb, ppT], [asum], start=(pk == 0), stop=False)
                prev = (kt, pT)
            pk, ppT = prev
            S.mm(acc[:], vh[:, pk, :], ppT[:], [vh, ppT], [acc], start=(pk == 0), stop=True)
            S.mm(asum[:], onesb[:], ppT[:], [onesb, ppT], [asum], start=(pk == 0), stop=True)
            S.dve(lambda e: e.reciprocal(rinv[:], asum[:]), [asum], [rinv])
            o_ = orot.get()
            S.dve(lambda e: e.tensor_tensor(o_[:], acc[:], rinv[:], ALU.mult), [acc, rinv], [o_])
            S.dma("sp", None, o_, out_ap=o_d.ap[h * 128:(h + 1) * 128, qs_], sem_of=o_, is_output=True)
    S.finish()
    return nc


def swa_bias_tables():
    W = 128
    qi = np.arange(W)[:, None]; kj = np.arange(2 * W)[None, :]
    dist = (qi + W - kj).astype(np.float32)
    valid = (dist >= 0) & (dist < W)
    slopes = (2.0 ** (-8.0 * (np.arange(8, dtype=np.float32) + 1.0) / 8)).astype(np.float32)
    b = np.where(valid[:, None, :], -slopes[None, :, None] * dist[:, None, :], -30000.0).astype(np.float32)
    bf = b.copy(); bf[:, :, :W] = -30000.0
    return np.ascontiguousarray(b), np.ascontiguousarray(bf)


def build_swa(NT=2048):
    nc = new_nc()
    S = Sched(nc)
    NB = NT // 128
    q_d = S.din("qT", [512, NT]); k_d = S.din("kT", [128, 128 + NT]); v_d = S.din("V", [128 + NT, 128])
    b_d = S.din("bias", [128, 8, 256]); bf_d = S.din("bias_first", [128, 8, 256])
    sk_d = S.din("sinkc", [128, 8]); one_d = S.din("onec", [128, 1]); id_d = S.din("ident", [128, 128])
    o_d = S.dout("ocT", [512, NT])
    bias = S.sb([128, 8, 256]); S.dma("sp", bias, b_d)
    biasf = S.sb([128, 8, 256]); S.dma("sp", biasf, bf_d)
    sinkc = S.sb([128, 8]); S.dma("sp", sinkc, sk_d)
    onec = S.sb([128, 1]); S.dma("sp", onec, one_d)
    identb = S.sb([128, 128], BF16); S.dma("pool", identb, id_d)
    q64 = S.sb([64, 8, NT], BF16); S.dma("pool", q64, q_d, in_ap=q_d.ap.rearrange("(h d) n -> d h n", d=64))
    k64 = S.sb([64, 2, 128 + NT], BF16); S.dma("pool", k64, k_d, in_ap=k_d.ap.rearrange("(h d) n -> d h n", d=64))
    vsb = S.sb([128, NB + 1, 128], BF16); S.dma("pool", vsb, v_d, in_ap=v_d.ap.rearrange("(b p) d -> p b d", p=128))
    sp_rot = Rot([S.ps([128, 512], name="sps") for _ in range(2)])
    pt_rot = Rot([S.ps([128, 256], BF16, name="ptp") for _ in range(2)])
    op_rot = Rot([S.ps([128, 512], name="ops") for _ in range(2)])
    s_rot = Rot([S.sb([128, 256], name="s") for _ in range(2)])
    p_rot = Rot([S.sb([128, 256], name="p") for _ in range(2)])
    pn_rot = Rot([S.sb([128, 256], BF16, name="pn") for _ in range(2)])
    pnt_rot = Rot([S.sb([128, 256], BF16, name="pnt") for _ in range(2)])
    c_rot = Rot([S.sb([128, 8], name="col") for _ in range(4)])
    ost = Rot([S.sb([64, 8, 128], name="ost") for _ in range(2)])
    for n in range(NB):
        bt = biasf if n == 0 else bias
        og = ost.get()
        for h in range(8):
            kv = h // 4
            sp_ = sp_rot.get()
            S.mm(sp_[:, 0:256], q64[:, h, n * 128:(n + 1) * 128], k64[:, kv, n * 128:n * 128 + 256], [q64, k64], [sp_])
            s_ = s_rot.get(); c_ = c_rot.get()
            S.dve(lambda e: e.scalar_tensor_tensor(s_[:], sp_[:, 0:256], 0.125, bt[:, h, :], ALU.mult, ALU.add), [sp_, bt], [s_])
            S.dve(lambda e: e.tensor_reduce(c_[:, 0:1], s_[:], AX.X, ALU.max), [s_], [c_])
            S.dve(lambda e: e.tensor_tensor(c_[:, 0:1], c_[:, 0:1], sinkc[:, h:h + 1], ALU.max), [c_, sinkc], [c_])
            S.dve(lambda e: e.tensor_scalar(c_[:, 1:2], c_[:, 0:1], -1.0, None, ALU.mult), [c_], [c_])
            S.dve(lambda e: e.memset(c_[:, 2:3], 0.0), [], [c_])
            p_ = p_rot.get()
            S.act(lambda e: e.activation(p_[:], s_[:], AF.Exp, bias=c_[:, 1:2], scale=onec[:, 0:1], accum_out=c_[:, 2:3]),
                  [s_, c_, onec], [p_, c_])
            S.act(lambda e: e.activation(c_[:, 3:4], sinkc[:, h:h + 1], AF.Exp, bias=c_[:, 1:2], scale=onec[:, 0:1]),
                  [sinkc, c_, onec], [c_])
            S.dve(lambda e: e.tensor_tensor(c_[:, 4:5], c_[:, 2:3], c_[:, 3:4], ALU.add), [c_], [c_])
            S.dve(lambda e: e.reciprocal(c_[:, 4:5], c_[:, 4:5]), [c_], [c_])
            pn = pn_rot.get()
            S.dve(lambda e: e.tensor_scalar(pn[:], p_[:], c_[:, 4:5], None, ALU.mult), [p_, c_], [pn])
            ptp = pt_rot.get()
            S.pe(lambda e: e.transpose(ptp[:, 0:128], pn[:, 0:128], identb[:]), [pn, identb], [ptp])
            S.pe(lambda e: e.transpose(ptp[:, 128:256], pn[:, 128:256], identb[:]), [pn, identb], [ptp])
            pnt = pnt_rot.get()
            S.act(lambda e: e.copy(pnt[:], ptp[:]), [ptp], [pnt])
            ops = op_rot.get()
            S.mmg(ops[0:64, 0:128], [(vsb[:, n, kv * 64:(kv + 1) * 64], pnt[:, 0:128]),
                                     (vsb[:, n + 1, kv * 64:(kv + 1) * 64], pnt[:, 128:256])], [vsb, pnt], [ops])
            S.act(lambda e: e.copy(og[:, h, :], ops[0:64, 0:128]), [ops], [og])
        S.dma("sp", None, og, out_ap=o_d.ap[:, n * 128:(n + 1) * 128].rearrange("(h d) n -> d h n", d=64), sem_of=og, is_output=True)
    S.finish()
    return nc


def build_outproj(NT=2048):
    nc = new_nc()
    S = Sched(nc)
    NQ = NT // 512
    cat_d = S.din("catT", [D, NT]); x_d = S.din("xT", [D, NT]); w_d = S.din("w_out", [D, D])
    vec_d = S.din("vecs", [128, 5, KT])
    ones_d = S.din("ones", [128, 128])
    x1_o = S.dout("x1T", [D, NT]); h2_o = S.dout("h2T", [D, NT])
    ones = S.sb([128, 128]); S.dma("sp", ones, ones_d)
    vec = S.sb([128, 5, KT]); S.dma("sp", vec, vec_d)
    pg = S.sb([128, KT]); gp2 = S.sb([128, KT]); sh2 = S.sb([128, KT])
    S.dve(lambda e: e.tensor_tensor(pg[:], vec[:, 0, :], vec[:, 1, :], ALU.mult), [vec], [pg])
    S.dve(lambda e: e.tensor_scalar(gp2[:], vec[:, 3, :], 1.0, None, ALU.add), [vec], [gp2])
    S.dve(lambda e: e.tensor_tensor(gp2[:], gp2[:], vec[:, 2, :], ALU.mult), [gp2, vec], [gp2])
    S.dve(lambda e: e.tensor_copy(sh2[:], vec[:, 4, :]), [vec], [sh2])
    cc = S.sb([128, KT, 512], BF16, name="cc")
    wrot = Rot([S.sb([128, KT, 512], BF16, name="wsl") for _ in range(2)])
    mix = S.sb([128, KT, 512], name="mix")
    xs = S.sb([128, KT, 512], name="xs")
    sqrot = Rot([S.sb([128, 512], name="sq") for _ in range(3)])
    tmprot = Rot([S.sb([128, 512], name="tmp") for _ in range(3)])
    rstd = S.sb([128, 512])
    ps_ss = S.ps([128, 512])
    psrot = Rot([S.ps([128, 512], name="pso") for _ in range(4)])
    for tq in range(NQ):
        ts = slice(tq * 512, (tq + 1) * 512)
        S.dma("pool", cc, cat_d, in_ap=cat_d.ap[:, ts].rearrange("(kt p) n -> p kt n", p=128))
        S.dma("sp", xs, x_d, in_ap=x_d.ap[:, ts].rearrange("(kt p) n -> p kt n", p=128))
        for sl in range(4):
            ws = wrot.get()
            S.dma("pool", ws, w_d, in_ap=w_d.ap[:, sl * 512:(sl + 1) * 512].rearrange("(kt p) c -> p kt c", p=128))
            for j in range(4):
                ot = sl * 4 + j
                pp_ = psrot.get()
                S.mmg(pp_[:], [(ws[:, kt, j * 128:(j + 1) * 128], cc[:, kt, :]) for kt in range(KT)], [ws, cc], [pp_])
                S.act(lambda e: e.copy(mix[:, ot, :], pp_[:]), [pp_], [mix])
        rms_rstd(S, [(mix, mix[:, kt, :]) for kt in range(KT)], 512, ones, ps_ss, sqrot, rstd, D)
        for kt in range(KT):
            t1 = tmprot.get()
            S.dve(lambda e: e.tensor_tensor(t1[:], mix[:, kt, :], rstd[:], ALU.mult), [mix, rstd], [t1])
            S.dve(lambda e: e.scalar_tensor_tensor(xs[:, kt, :], t1[:], pg[:, kt:kt + 1], xs[:, kt, :], ALU.mult, ALU.add),
                  [t1, pg, xs], [xs])
        S.dma("sp", None, xs, out_ap=x1_o.ap[:, ts].rearrange("(kt p) n -> p kt n", p=128), sem_of=xs, is_output=True)
        modulated_norm(S, xs, 512, gp2, sh2, ones, ps_ss, sqrot, rstd, tmprot, [mix[:, kt, :] for kt in range(KT)], [mix] * KT)
        S.dma("sp", None, mix, out_ap=h2_o.ap[:, ts].rearrange("(kt p) n -> p kt n", p=128), sem_of=mix, is_output=True)
    S.finish()
    return nc


NFT = D_FF // 128


def build_ffn(NT=2048):
    nc = new_nc()
    S = Sched(nc)
    NQ = NT // 512
    h2_d = S.din("h2T", [D, 2 + NT]); x1_d = S.din("x1T", [D, NT])
    wu_d = S.din("w_up", [D, 2 * D_FF]); wd_d = S.din("w_down", [D_FF, D])
    cw_d = S.din("cw", [128, 2 * NFT, 3]); cb_d = S.din("cb", [128, 2 * NFT])
    vec_d = S.din("vecs", [128, 2, KT])
    ones_d = S.din("ones", [128, 128])
    o_d = S.dout("x2T", [D, NT])
    ones = S.sb([128, 128]); S.dma("sp", ones, ones_d)
    vec = S.sb([128, 2, KT]); S.dma("sp", vec, vec_d)
    cw = S.sb([128, 2 * NFT, 3]); S.dma("sp", cw, cw_d)
    cb = S.sb([128, 2 * NFT]); S.dma("sp", cb, cb_d)
    pg = S.sb([128, KT])
    S.dve(lambda e: e.tensor_tensor(pg[:], vec[:, 0, :], vec[:, 1, :], ALU.mult), [vec], [pg])
    h2q = S.sb([128, KT, 514], BF16, name="h2q")
    actT = S.sb([128, NFT, 512], BF16, name="actT")
    wgu = [Rot([S.sb([128, KT, 128], BF16, name="wgu") for _ in range(2)]) for _ in range(2)]
    wdr = Rot([S.sb([128, NFT, 128], BF16, name="wd") for _ in range(2)])
    y = S.sb([128, KT, 512], name="y")
    urot = Rot([S.sb([128, 514], name="u") for _ in range(4)])
    crot = Rot([S.sb([128, 512], name="c") for _ in range(4)])
    x1r = Rot([S.sb([128, 512], name="x1") for _ in range(3)])
    sqrot = Rot([S.sb([128, 512], name="sq") for _ in range(3)])
    tmprot = Rot([S.sb([128, 512], name="tmp") for _ in range(3)])
    rstd = S.sb([128, 512])
    ps_ss = S.ps([128, 512])
    psrot = Rot([S.ps([128, 512], name="pso") for _ in range(4)])
    phrot = Rot([S.ps([128, 512], name="psh") for _ in range(2)])
    for tq in range(NQ):
        ts = slice(tq * 512, (tq + 1) * 512)
        S.dma("pool", h2q, h2_d, in_ap=h2_d.ap[:, tq * 512:tq * 512 + 514].rearrange("(kt p) n -> p kt n", p=128))
        for ft in range(NFT):
            cs_ = []
            for gi in range(2):
                f = gi * NFT + ft
                w = wgu[gi].get()
                S.dma("pool", w, wu_d, in_ap=wu_d.ap[:, f * 128:(f + 1) * 128].rearrange("(kt p) c -> p kt c", p=128))
                pm = psrot.get(); ph = phrot.get()
                S.mmg(pm[:], [(w[:, kt, :], h2q[:, kt, 2:514]) for kt in range(KT)], [w, h2q], [pm])
                S.mmg(ph[:, 0:2], [(w[:, kt, :], h2q[:, kt, 0:2]) for kt in range(KT)], [w, h2q], [ph])
                u = urot.get()
                S.act(lambda e: e.copy(u[:, 2:514], pm[:]), [pm], [u])
                S.dve(lambda e: e.tensor_copy(u[:, 0:2], ph[:, 0:2]), [ph], [u])
                c = crot.get()
                S.dve(lambda e: e.tensor_scalar(c[:], u[:, 2:514], cw[:, f, 2:3], cb[:, f:f + 1], ALU.mult, ALU.add), [u, cw, cb], [c])
                S.dve(lambda e: e.scalar_tensor_tensor(c[:], u[:, 1:513], cw[:, f, 1:2], c[:], ALU.mult, ALU.add), [u, cw, c], [c])
                S.dve(lambda e: e.scalar_tensor_tensor(c[:], u[:, 0:512], cw[:, f, 0:1], c[:], ALU.mult, ALU.add), [u, cw, c], [c])
                cs_.append(c)
            cg, cu = cs_
            S.act(lambda e: e.activation(cg[:], cg[:], AF.Gelu_apprx_tanh), [cg], [cg])
            S.dve(lambda e: e.tensor_tensor(actT[:, ft, :], cg[:], cu[:], ALU.mult), [cg, cu], [actT])
        for ot in range(KT):
            wd = wdr.get()
            S.dma("pool", wd, wd_d, in_ap=wd_d.ap[:, ot * 128:(ot + 1) * 128].rearrange("(ft p) c -> p ft c", p=128))
            pp_ = psrot.get()
            S.mmg(pp_[:], [(wd[:, ft, :], actT[:, ft, :]) for ft in range(NFT)], [wd, actT], [pp_])
            S.act(lambda e: e.copy(y[:, ot, :], pp_[:]), [pp_], [y])
        rms_rstd(S, [(y, y[:, kt, :]) for kt in range(KT)], 512, ones, ps_ss, sqrot, rstd, D)
        for kt in range(KT):
            t1 = tmprot.get(); x1 = x1r.get()
            S.dma("sp", x1, x1_d, in_ap=x1_d.ap[kt * 128:(kt + 1) * 128, ts])
            S.dve(lambda e: e.tensor_tensor(t1[:], y[:, kt, :], rstd[:], ALU.mult), [y, rstd], [t1])
            S.dve(lambda e: e.scalar_tensor_tensor(x1[:], t1[:], pg[:, kt:kt + 1], x1[:], ALU.mult, ALU.add), [t1, pg, x1], [x1])
            S.dma("sp", None, x1, out_ap=o_d.ap[kt * 128:(kt + 1) * 128, ts], sem_of=x1, is_output=True)
    S.finish()
    return nc


_NC_CACHE = {}


def _prog(name, builder, *args):
    return builder(*args)


def _col(v):
    return np.ascontiguousarray(np.asarray(v, np.float32).reshape(KT, 128).T)


def _cols(vs):
    return np.ascontiguousarray(np.stack([_col(v) for v in vs], 1))


def _run(nc, in_maps):
    res = run_bass_kernel_spmd(nc, in_maps, core_ids=list(range(NCORES)))
    return res.results


def _c(a):
    return np.ascontiguousarray(a)


def kernel(x, c, positions, ada_w, ada_b, mix_pre_norm, mix_post_norm, w_in, w_out,
           gdn_conv, gdn_a_log, gdn_dt_bias, gdn_norm, mla_q_norm, mla_w_uq, mla_kv_norm,
           mla_w_ukv, swa_sinks, ffn_pre_norm, ffn_post_norm, ffn_w_up, ffn_conv, ffn_conv_b,
           ffn_w_down):
    f32 = np.float32
    x = np.asarray(x, f32)
    NS = x.shape[1]
    TPC = NS // NCORES
    ones = np.ones((128, 128), f32)
    ident = np.eye(128, dtype=f32)
    xT = _c(x[0].T)
    positions = np.asarray(positions).astype(np.int32)
    tok = [slice(cc * TPC, (cc + 1) * TPC) for cc in range(NCORES)]

    ims = []
    for cc in range(NCORES):
        sl = slice(cc * 1536, (cc + 1) * 1536)
        ims.append({"c_col": _col(np.asarray(c, f32)[0]),
                    "ada_w0": _c(np.asarray(ada_w[0], f32)[:, sl]), "ada_w1": _c(np.asarray(ada_w[1], f32)[:, sl]),
                    "ada_b0": _c(np.asarray(ada_b[0], f32)[None, sl]), "ada_b1": _c(np.asarray(ada_b[1], f32)[None, sl])})
    r = _run(build_mod(), ims)
    mods = [np.concatenate([r[cc][f"mod{l}"][0] for cc in range(NCORES)]).reshape(6, D) for l in range(2)]

    inv = (10000.0 ** (-np.arange(32, dtype=f32) / 32)).astype(f32)
    inv2 = np.concatenate([inv, inv])[None, :].astype(f32)
    sgn = np.ones((64, 1), f32); sgn[:32] = -1
    swa_b, swa_bf = swa_bias_tables()
    gcst = gdn_consts()
    kidx = _c(np.arange(NS, dtype=f32).reshape(-1, 128).T)
    for l in range(2):
        shift1, scale1, gate1, shift2, scale2, gate2 = mods[l]
        w = np.asarray(w_in[l], f32)
        vec = _cols([mix_pre_norm[l], scale1, shift1])
        r = _run(build_inproj(TPC, D_IN), [{"xT": _c(xT[:, tok[cc]]), "w": w, "vecs": vec, "ones": ones} for cc in range(NCORES)])
        projT = np.concatenate([r[cc]["projT"] for cc in range(NCORES)], axis=1)
        del r
        conv = np.asarray(gdn_conv[l], f32)
        ims = []
        for h in range(8):
            im = dict(gcst)
            im["qT"] = _c(projT[h * 128:(h + 1) * 128]); im["kT"] = _c(projT[1024 + h * 128:1024 + (h + 1) * 128])
            im["vT"] = _c(projT[2048 + h * 128:2048 + (h + 1) * 128]); im["zT"] = _c(projT[3072 + h * 128:3072 + (h + 1) * 128])
            im["a_row"] = _c(projT[4096 + h][None, :]); im["b_row"] = _c(projT[4104 + h][None, :])
            im["cw"] = _c(np.concatenate([conv[:, h * 128:(h + 1) * 128].T, conv[:, 1024 + h * 128:1024 + (h + 1) * 128].T,
                                          conv[:, 2048 + h * 128:2048 + (h + 1) * 128].T], axis=1))
            par = np.empty((128, 3), f32)
            par[:, 0] = np.asarray(gdn_a_log[l], f32)[h]; par[:, 1] = np.asarray(gdn_dt_bias[l], f32)[h]
            par[:, 2] = np.asarray(gdn_norm[l], f32)
            im["par"] = par
            ims.append(im)
        r = _run(build_gdn(NS), ims)
        oaT = np.concatenate([r[h]["oT"] for h in range(8)], axis=0)
        del r, ims
        wuq = np.asarray(mla_w_uq[l], f32); wukv = np.asarray(mla_w_ukv[l], f32)
        qnw = np.asarray(mla_q_norm[l], f32)
        qn4 = np.zeros((128, 4), f32)
        for kt in range(4):
            n = min(128, 448 - kt * 128)
            qn4[:n, kt] = qnw[kt * 128:kt * 128 + n]
        wsw = _c(np.concatenate([np.concatenate([wuq[:, h * 192 + 160:h * 192 + 192], wuq[:, h * 192 + 128:h * 192 + 160]], 1)
                                 for h in range(4)], 1))
        ims = []
        for cc in range(NCORES):
            kr = projT[4688:4752, tok[cc]]
            ims.append({"cqT": _c(projT[4112:4560, tok[cc]]), "ckvT": _c(projT[4560:4688, tok[cc]]), "krT": _c(kr),
                        "krsT": _c(np.concatenate([kr[32:], kr[:32]], 0)), "pos": _c(positions[0:1, tok[cc]]),
                        "inv2": inv2, "qnw": qn4, "kvnw": _c(np.asarray(mla_kv_norm[l], f32).reshape(128, 1)),
                        "w_uq": wuq, "w_uq_sw": wsw, "w_ukv": wukv, "ones": ones, "sgn": sgn})
        r = _run(build_mlaprep(TPC), ims)
        qTf = np.concatenate([r[cc]["qT"] for cc in range(NCORES)], axis=2)
        knTf = np.concatenate([r[cc]["knT"] for cc in range(NCORES)], axis=2)
        kpeTf = np.concatenate([r[cc]["kpeT"] for cc in range(NCORES)], axis=1)
        Vf = np.concatenate([r[cc]["V"] for cc in range(NCORES)], axis=0)
        del r, ims
        nsl = NS // 512 // NCORES
        ims = []
        for cc in range(NCORES):
            tiles = [cc + NCORES * i for i in range(nsl)]
            qsel = np.concatenate([np.arange(t * 512, (t + 1) * 512) for t in tiles])
            ims.append({"qT": _c(qTf[:, :, qsel]), "qidx": _c(qsel.astype(f32)[None, :]), "knT": knTf, "kpeT": kpeTf, "V": Vf,
                        "kidx": kidx, "ones": ones, "ident": ident})
        nkt = tuple(4 * NCORES * (i + 1) for i in range(nsl)); mfrom = tuple(4 * NCORES * i for i in range(nsl))
        r = _run(build_mla(NS, nkt, mfrom), ims)
        obT = np.empty((512, NS), f32)
        for cc in range(NCORES):
            for i in range(nsl):
                t = cc + NCORES * i
                obT[:, t * 512:(t + 1) * 512] = r[cc]["obT"][:, i * 512:(i + 1) * 512]
        del r, ims, qTf, knTf, kpeTf, Vf
        kfull = np.concatenate([np.zeros((128, 128), f32), projT[5264:5392]], axis=1)
        vfull = np.concatenate([np.zeros((128, 128), f32), projT[5392:5520]], axis=1)
        sinkc = _c(np.broadcast_to(np.asarray(swa_sinks[l], f32)[None, :], (128, 8)))
        ims = []
        for cc in range(NCORES):
            ks = slice(cc * TPC, cc * TPC + TPC + 128)
            ims.append({"qT": _c(projT[4752:5264, tok[cc]]), "kT": _c(kfull[:, ks]), "V": _c(vfull[:, ks].T),
                        "bias": swa_b, "bias_first": swa_bf if cc == 0 else swa_b, "sinkc": sinkc,
                        "onec": np.ones((128, 1), f32), "ident": ident})
        r = _run(build_swa(TPC), ims)
        ocT = np.concatenate([r[cc]["ocT"] for cc in range(NCORES)], axis=1)
        del r, ims, projT, kfull, vfull
        catT = np.concatenate([oaT, obT, ocT], axis=0)
        del oaT, obT, ocT
        vec = _cols([mix_post_norm[l], gate1, ffn_pre_norm[l], scale2, shift2])
        wo = np.asarray(w_out[l], f32)
        r = _run(build_outproj(TPC), [{"catT": _c(catT[:, tok[cc]]), "xT": _c(xT[:, tok[cc]]), "w_out": wo, "vecs": vec, "ones": ones}
                                      for cc in range(NCORES)])
        x1T = np.concatenate([r[cc]["x1T"] for cc in range(NCORES)], axis=1)
        h2T = np.concatenate([np.zeros((D, 2), f32)] + [r[cc]["h2T"] for cc in range(NCORES)], axis=1)
        del r, catT
        wu = np.asarray(ffn_w_up[l], f32); wd = np.asarray(ffn_w_down[l], f32)
        cwt = _c(np.asarray(ffn_conv[l], f32).T.reshape(2 * NFT, 128, 3).transpose(1, 0, 2))
        cbt = _c(np.asarray(ffn_conv_b[l], f32).reshape(2 * NFT, 128).T)
        vec = _cols([ffn_post_norm[l], gate2])
        r = _run(build_ffn(TPC), [{"h2T": _c(h2T[:, cc * TPC:cc * TPC + TPC + 2]), "x1T": _c(x1T[:, tok[cc]]), "w_up": wu, "w_down": wd,
                                   "cw": cwt, "cb": cbt, "vecs": vec, "ones": ones} for cc in range(NCORES)])
        xT = np.concatenate([r[cc]["x2T"] for cc in range(NCORES)], axis=1)
        del r, x1T, h2T
    return _c(xT.T)[None].astype(f32)
```
